# Optimizing a Trainium2 kernel written in Bass

```python
import jax, jax.numpy as jnp
from jax import lax
import numpy as np

D_MODEL = 1024
BATCH = 8
SEQ = 8192
DEPTH = 2

GRID_W = 64
CTX_LEN = 256
M_HEADS = 4
M_HEAD_DIM = D_MODEL // 8
M_WIDTH = M_HEADS * M_HEAD_DIM
M_CONV = 5
MLSTM_CHUNK = 64
A_HEADS = 4
A_NOPE = D_MODEL // 8
A_ROPE = D_MODEL // 16
A_V = D_MODEL // 8
A_WIDTH = A_HEADS * A_V
Q_LORA = 3 * D_MODEL // 8
KV_LORA = D_MODEL // 4
A_SCALE = (A_NOPE + A_ROPE) ** -0.5
ROPE_BASE = 10000.0
Q_BLOCK = 128
MIX_WIDTH = M_WIDTH + A_WIDTH
OFF_Q = 0
OFF_K = M_WIDTH
OFF_V = 2 * M_WIDTH
OFF_O = 3 * M_WIDTH
OFF_G = 4 * M_WIDTH
OFF_CQ = OFF_G + 4 * M_HEADS
OFF_CKV = OFF_CQ + Q_LORA
OFF_KR = OFF_CKV + KV_LORA
IN_COLS = OFF_KR + A_ROPE
N_EXPERTS = 16
EXPERT_FF = D_MODEL
EC_CAPACITY = 2
ALPHA = (2 * DEPTH) ** 0.25
BETA = (8 * DEPTH) ** -0.25

kernel_name = 'hybrid_mlstm_mla_ecmoe_dit'


def layer_norm(x, eps=1e-6):
    xf = x.astype(jnp.float32)
    mu = jnp.mean(xf, axis=-1, keepdims=True)
    var = jnp.mean(jnp.square(xf - mu), axis=-1, keepdims=True)
    return ((xf - mu) * lax.rsqrt(var + eps)).astype(x.dtype)


def rms_norm(x, g, eps=1e-6):
    xf = x.astype(jnp.float32)
    return (xf * lax.rsqrt(jnp.mean(jnp.square(xf), axis=-1, keepdims=True) + eps)).astype(x.dtype) * g


def modulate(x, shift, scale):
    return layer_norm(x) * (1.0 + scale) + shift


def post_norm(x, y, g, b):
    return layer_norm(ALPHA * x + y, eps=1e-5) * g + b


def split_heads(a, nh):
    B, T, _ = a.shape
    return a.reshape(B, T, nh, -1).transpose(0, 2, 1, 3)


def flip_t(a):
    return jnp.flip(a, axis=2)


def short_conv(x, w, b):
    y = lax.conv_general_dilated(x, w[:, None, :].astype(x.dtype), window_strides=(1,), padding='SAME',
                                 dimension_numbers=('NWC', 'WIO', 'NWC'), feature_group_count=x.shape[-1])
    return jax.nn.silu(y + b)


def axial_rope(T):
    ROWS = T // GRID_W
    half = A_ROPE // 2
    inv = ROPE_BASE ** (-jnp.arange(0, half, 2, dtype=jnp.float32) / half)
    row = jnp.repeat(jnp.arange(ROWS, dtype=jnp.float32), GRID_W)
    col = jnp.tile(jnp.arange(GRID_W, dtype=jnp.float32), ROWS)
    ang = jnp.concatenate([row[:, None] * inv, col[:, None] * inv], axis=-1)
    return jnp.cos(ang), jnp.sin(ang)


def apply_rope(x, cos, sin):
    half = A_ROPE // 2
    cos = cos.astype(x.dtype)
    sin = sin.astype(x.dtype)
    x1, x2 = x[..., :half], x[..., half:]
    return jnp.concatenate([x1 * cos - x2 * sin, x2 * cos + x1 * sin], axis=-1)


def zero_state(B):
    f32 = jnp.float32
    return (jnp.zeros((B, M_HEADS, M_HEAD_DIM, M_HEAD_DIM), f32), jnp.zeros((B, M_HEADS, M_HEAD_DIM), f32),
            jnp.zeros((B, M_HEADS), f32))


def mlstm_chunkwise(q, k, v, i_pre, f_pre, state):
    B, H, T, DH = q.shape
    L = MLSTM_CHUNK
    nc = T // L
    f32 = jnp.float32

    def chunks(a):
        return jnp.moveaxis(a.reshape(a.shape[:2] + (nc, L) + a.shape[3:]), 2, 0)

    xs = (chunks(q.astype(f32) * DH ** -0.5), chunks(k.astype(f32)), chunks(v.astype(f32)),
          chunks(i_pre.astype(f32)), chunks(jax.nn.log_sigmoid(f_pre.astype(f32))))
    tri = jnp.tril(jnp.ones((L, L), dtype=bool))

    def step(carry, blk):
        C, n, m = carry
        qb, kb, vb, ig, lf = blk
        b = jnp.cumsum(lf, axis=-1)
        d_log = jnp.where(tri, b[..., :, None] - b[..., None, :] + ig[..., None, :], -jnp.inf)
        m_inter = b + m[..., None]
        m_t = jnp.maximum(jnp.max(d_log, axis=-1), m_inter)
        s = jnp.einsum('bhqd,bhkd->bhqk', qb, kb) * jnp.exp(d_log - m_t[..., None])
        w_inter = jnp.exp(m_inter - m_t)
        num = jnp.einsum('bhqk,bhkd->bhqd', s, vb) + w_inter[..., None] * jnp.einsum('bhqd,bhde->bhqe', qb, C)
        den = jnp.sum(s, axis=-1) + w_inter * jnp.einsum('bhqd,bhd->bhq', qb, n)
        h = num / jnp.maximum(jnp.abs(den), jnp.exp(-m_t))[..., None]
        g = b[..., -1:] - b + ig
        m_new = jnp.maximum(b[..., -1] + m, jnp.max(g, axis=-1))
        w_s = jnp.exp(g - m_new[..., None])
        decay = jnp.exp(b[..., -1] + m - m_new)
        C_new = decay[..., None, None] * C + jnp.einsum('bhl,bhld,bhle->bhde', w_s, kb, vb)
        n_new = decay[..., None] * n + jnp.einsum('bhl,bhld->bhd', w_s, kb)
        return (C_new, n_new, m_new), h

    state, hs = lax.scan(step, state, xs)
    h = jnp.moveaxis(hs, 0, 2).reshape(B, H, T, DH)
    return h.astype(q.dtype), state


def mlstm_final_state(k, v, i_pre, f_pre):
    kf, vf = k.astype(jnp.float32), v.astype(jnp.float32)
    b = jnp.cumsum(jax.nn.log_sigmoid(f_pre.astype(jnp.float32)), axis=-1)
    g = b[..., -1:] - b + i_pre.astype(jnp.float32)
    m = jnp.maximum(b[..., -1], jnp.max(g, axis=-1))
    w = jnp.exp(g - m[..., None])
    return (jnp.einsum('bht,bhtd,bhte->bhde', w, kf, vf), jnp.einsum('bht,bhtd->bhd', w, kf), m)


def mlstm_q_o(p, conv_w, conv_b):
    q = short_conv(p[..., OFF_Q:OFF_K], conv_w[:, :M_WIDTH], conv_b[:M_WIDTH])
    return split_heads(q, M_HEADS), p[..., OFF_O:OFF_G]


def mlstm_kv_gates(p, conv_w, conv_b, b_gates):
    B, T, _ = p.shape
    k = short_conv(p[..., OFF_K:OFF_V], conv_w[:, M_WIDTH:], conv_b[M_WIDTH:])
    v = p[..., OFF_V:OFF_O]
    g = (p[..., OFF_G:OFF_CQ] + b_gates).reshape(B, T, 4, M_HEADS).transpose(2, 0, 3, 1)
    return split_heads(k, M_HEADS), split_heads(v, M_HEADS), g[0], g[1], g[2], g[3]


def mlstm_merge(h_f, h_b, o, norm_w):
    h = layer_norm(h_f + h_b)
    B, H, T, DH = h.shape
    h = h.transpose(0, 2, 1, 3).reshape(B, T, H * DH)
    return h * norm_w * jax.nn.sigmoid(o)


def mla_q(p, q_norm_w, w_uq):
    B, T, _ = p.shape
    cq = rms_norm(p[..., OFF_CQ:OFF_CKV], q_norm_w)
    q = jnp.einsum('btr,rc->btc', cq, w_uq).reshape(B, T, A_HEADS, A_NOPE + A_ROPE).transpose(0, 2, 1, 3)
    return q[..., :A_NOPE], q[..., A_NOPE:]


def mla_kv(p, kv_norm_w, w_ukv):
    B, T, _ = p.shape
    ckv = rms_norm(p[..., OFF_CKV:OFF_KR], kv_norm_w)
    kv = jnp.einsum('btr,rc->btc', ckv, w_ukv).reshape(B, T, A_HEADS, A_NOPE + A_V).transpose(0, 2, 1, 3)
    return kv[..., :A_NOPE], p[..., OFF_KR:IN_COLS], kv[..., A_NOPE:]


def mla_attend(qn, qr, kn, kr, v):
    s = (jnp.einsum('bhqd,bhkd->bhqk', qn, kn, preferred_element_type=jnp.float32)
         + jnp.einsum('bhqr,bkr->bhqk', qr, kr, preferred_element_type=jnp.float32)) * A_SCALE
    p = jax.nn.softmax(s, axis=-1)
    return jnp.einsum('bhqk,bhkd->bhqd', p.astype(v.dtype), v)


def mla_blocked(qn, qr, kn, kr, v):
    B, H, T, _ = qn.shape
    nb = T // Q_BLOCK

    def blocks(a):
        return jnp.moveaxis(a.reshape(B, H, nb, Q_BLOCK, a.shape[-1]), 2, 0)

    out = lax.map(lambda qs: mla_attend(qs[0], qs[1], kn, kr, v), (blocks(qn), blocks(qr)))
    return jnp.moveaxis(out, 0, 2).reshape(B, H, T, -1)


def heads_to_tokens(a):
    B, H, T, D = a.shape
    return a.transpose(0, 2, 1, 3).reshape(B, T, H * D)


def token_mixer(h_x, h_c, cos, sin, w_in, b_gates, conv_w, conv_b, m_norm_w, q_norm_w, kv_norm_w,
                w_uq, w_ukv, w_out, ctx_out):
    p_x = jnp.einsum('btd,dc->btc', h_x, w_in)
    p_c = jnp.einsum('btd,dc->btc', h_c, w_in)
    k_c, v_c, if_c, ff_c, ib_c, fb_c = mlstm_kv_gates(p_c, conv_w, conv_b, b_gates)
    if ctx_out:
        q_c, o_c = mlstm_q_o(p_c, conv_w, conv_b)
        h_cf, st_f = mlstm_chunkwise(q_c, k_c, v_c, if_c, ff_c, zero_state(h_c.shape[0]))
        h_cb, st_b = mlstm_chunkwise(flip_t(q_c), flip_t(k_c), flip_t(v_c), flip_t(ib_c), flip_t(fb_c),
                                     zero_state(h_c.shape[0]))
        m_out_c = mlstm_merge(h_cf, flip_t(h_cb), o_c, m_norm_w)
    else:
        st_f = mlstm_final_state(k_c, v_c, if_c, ff_c)
        st_b = mlstm_final_state(flip_t(k_c), flip_t(v_c), flip_t(ib_c), flip_t(fb_c))
    q_x, o_x = mlstm_q_o(p_x, conv_w, conv_b)
    k_x, v_x, if_x, ff_x, ib_x, fb_x = mlstm_kv_gates(p_x, conv_w, conv_b, b_gates)
    h_xf, _ = mlstm_chunkwise(q_x, k_x, v_x, if_x, ff_x, st_f)
    h_xb, _ = mlstm_chunkwise(flip_t(q_x), flip_t(k_x), flip_t(v_x), flip_t(ib_x), flip_t(fb_x), st_b)
    m_out_x = mlstm_merge(h_xf, flip_t(h_xb), o_x, m_norm_w)
    kn_c, kr_c, va_c = mla_kv(p_c, kv_norm_w, w_ukv)
    qn_x, qr_x = mla_q(p_x, q_norm_w, w_uq)
    kn_x, kr_x, va_x = mla_kv(p_x, kv_norm_w, w_ukv)
    qr_x = apply_rope(qr_x, cos, sin)
    kr_x = apply_rope(kr_x, cos, sin)
    kn_all = jnp.concatenate([kn_c, kn_x], axis=2)
    kr_all = jnp.concatenate([kr_c, kr_x], axis=1)
    va_all = jnp.concatenate([va_c, va_x], axis=2)
    a_out_x = heads_to_tokens(mla_blocked(qn_x, qr_x, kn_all, kr_all, va_all))
    y_x = jnp.einsum('btc,cd->btd', jnp.concatenate([m_out_x, a_out_x], axis=-1), w_out)
    if ctx_out:
        qn_c, qr_c = mla_q(p_c, q_norm_w, w_uq)
        a_out_c = heads_to_tokens(mla_attend(qn_c, qr_c, kn_c, kr_c, va_c))
        y_c = jnp.einsum('btc,cd->btd', jnp.concatenate([m_out_c, a_out_c], axis=-1), w_out)
        return y_x, y_c
    return y_x, None


def ec_moe(h, w_router, w_gate, w_up, w_down):
    B, n, D = h.shape
    cap = EC_CAPACITY * n // N_EXPERTS
    aff = jax.nn.softmax(jnp.einsum('bnd,de->bne', h, w_router, preferred_element_type=jnp.float32), axis=-1)
    gate, idx = lax.top_k(aff.transpose(0, 2, 1), cap)
    flat = idx.reshape(B, N_EXPERTS * cap)
    xin = jnp.take_along_axis(h, flat[..., None], axis=1).reshape(B, N_EXPERTS, cap, D)
    act = jax.nn.silu(jnp.einsum('becd,edf->becf', xin, w_gate)) * jnp.einsum('becd,edf->becf', xin, w_up)
    out = jnp.einsum('becf,efd->becd', act, w_down) * gate[..., None].astype(h.dtype)
    return jax.vmap(lambda o, i: jnp.zeros((n, D), o.dtype).at[i].add(o))(out.reshape(B, N_EXPERTS * cap, D), flat)


def setup_inputs(seed: int = 0) -> dict:
    key = jax.random.key(seed)
    ks = jax.random.split(key, 24)
    f32 = jnp.float32
    L = DEPTH

    def nrm(k, shape, scale):
        return jax.random.normal(k, shape, f32) * scale

    gate_base = jnp.concatenate([jnp.zeros((M_HEADS,), f32), jnp.linspace(3.0, 6.0, M_HEADS, dtype=f32),
                                 jnp.zeros((M_HEADS,), f32), jnp.linspace(3.0, 6.0, M_HEADS, dtype=f32)])
    return {
        'x': nrm(ks[0], (BATCH, SEQ, D_MODEL), 1.0),
        'c': nrm(ks[1], (BATCH, D_MODEL), 1.0),
        'ctx': nrm(ks[2], (BATCH, CTX_LEN, D_MODEL), 1.0),
        'c_ctx': nrm(ks[3], (D_MODEL,), 1.0),
        'w_mod': nrm(ks[4], (L, D_MODEL, 6 * D_MODEL), 0.5 * D_MODEL ** -0.5),
        'b_mod': nrm(ks[5], (L, 6 * D_MODEL), 0.02),
        'w_in': nrm(ks[6], (L, D_MODEL, IN_COLS), D_MODEL ** -0.5),
        'b_gates': gate_base + nrm(ks[7], (L, 4 * M_HEADS), 0.1),
        'conv_w': nrm(ks[8], (L, M_CONV, 2 * M_WIDTH), M_CONV ** -0.5),
        'conv_b': nrm(ks[9], (L, 2 * M_WIDTH), 0.02),
        'm_norm_w': 1.0 + nrm(ks[10], (L, M_WIDTH), 0.1),
        'q_norm_w': 1.0 + nrm(ks[11], (L, Q_LORA), 0.1),
        'kv_norm_w': 1.0 + nrm(ks[12], (L, KV_LORA), 0.1),
        'w_uq': nrm(ks[13], (L, Q_LORA, A_HEADS * (A_NOPE + A_ROPE)), Q_LORA ** -0.5),
        'w_ukv': nrm(ks[14], (L, KV_LORA, A_HEADS * (A_NOPE + A_V)), KV_LORA ** -0.5),
        'w_out': nrm(ks[15], (L, MIX_WIDTH, D_MODEL), BETA * MIX_WIDTH ** -0.5),
        'ln1_g': 1.0 + nrm(ks[16], (L, D_MODEL), 0.1),
        'ln1_b': nrm(ks[17], (L, D_MODEL), 0.02),
        'w_router': nrm(ks[18], (L, D_MODEL, N_EXPERTS), D_MODEL ** -0.5),
        'w_gate': nrm(ks[19], (L, N_EXPERTS, D_MODEL, EXPERT_FF), D_MODEL ** -0.5),
        'w_up': nrm(ks[20], (L, N_EXPERTS, D_MODEL, EXPERT_FF), D_MODEL ** -0.5),
        'w_down': nrm(ks[21], (L, N_EXPERTS, EXPERT_FF, D_MODEL), BETA * EXPERT_FF ** -0.5),
        'ln2_g': 1.0 + nrm(ks[22], (L, D_MODEL), 0.1),
        'ln2_b': nrm(ks[23], (L, D_MODEL), 0.02),
    }


def reference(x, c, ctx, c_ctx, w_mod, b_mod, w_in, b_gates, conv_w, conv_b, m_norm_w, q_norm_w, kv_norm_w,
              w_uq, w_ukv, w_out, ln1_g, ln1_b, w_router, w_gate, w_up, w_down, ln2_g, ln2_b):
    cos, sin = axial_rope(x.shape[1])
    for l in range(DEPTH):
        ctx_out = l < DEPTH - 1
        mx = jnp.split((jnp.einsum('bd,dc->bc', jax.nn.silu(c), w_mod[l]) + b_mod[l])[:, None, :], 6, axis=-1)
        mc = jnp.split((jnp.einsum('d,dc->c', jax.nn.silu(c_ctx), w_mod[l]) + b_mod[l])[None, None, :], 6, axis=-1)
        h_x = modulate(x, mx[0], mx[1])
        h_c = modulate(ctx, mc[0], mc[1])
        y_x, y_c = token_mixer(h_x, h_c, cos, sin, w_in[l], b_gates[l], conv_w[l], conv_b[l], m_norm_w[l],
                               q_norm_w[l], kv_norm_w[l], w_uq[l], w_ukv[l], w_out[l], ctx_out)
        x = post_norm(x, mx[2] * y_x, ln1_g[l], ln1_b[l])
        x = post_norm(x, mx[5] * ec_moe(modulate(x, mx[3], mx[4]), w_router[l], w_gate[l], w_up[l], w_down[l]),
                      ln2_g[l], ln2_b[l])
        if ctx_out:
            ctx = post_norm(ctx, mc[2] * y_c, ln1_g[l], ln1_b[l])
            ctx = post_norm(ctx, mc[5] * ec_moe(modulate(ctx, mc[3], mc[4]), w_router[l], w_gate[l], w_up[l],
                                                w_down[l]), ln2_g[l], ln2_b[l])
    return x
```

```python
import math
import numpy as np
from contextlib import ExitStack
import concourse.bass as bass
import concourse.mybir as mybir
from concourse.bass_utils import run_bass_kernel_spmd

F32 = mybir.dt.float32
BF16 = mybir.dt.bfloat16
F16 = mybir.dt.float16
I32 = mybir.dt.int32
AF = mybir.ActivationFunctionType
ALU = mybir.AluOpType
AX = mybir.AxisListType

D = 1024
KC = 8
DEPTH = 2
CTX = 256
GRID_W = 64
MW = 512
OFF_Q, OFF_K, OFF_V, OFF_O, OFF_G = 0, 512, 1024, 1536, 2048
OFF_CQ = OFF_G + 16
OFF_CKV = OFF_CQ + 384
OFF_KR = OFF_CKV + 256
IN_COLS = OFF_KR + 64
NE = 16
ALPHA = (2 * DEPTH) ** 0.25
A_SCALE = 192 ** -0.5
DH = 128

SAME_ENGINE_SYNC = True
EPOCH = 30000
NDSEM = 16


class Buf:
    __slots__ = ("name", "last_w", "readers", "excl")

    def __init__(self, name="", excl=False):
        self.name = name
        self.last_w = None
        self.readers = {}
        self.excl = excl


class Ring:
    def __init__(self, tiles, excl=False):
        self.tiles = tiles
        self.bufs = [Buf(excl=excl) for _ in tiles]
        self.i = 0

    def next(self):
        j = self.i % len(self.tiles)
        self.i += 1
        return self.tiles[j], self.bufs[j]


class Sched:
    def __init__(self, nc, es):
        self.nc = nc
        self.es = es
        self.engs = {"pe": nc.tensor, "dve": nc.vector, "act": nc.scalar, "pool": nc.gpsimd, "sp": nc.sync}
        self.cnt = {e: 0 for e in self.engs}
        self.epoch = {e: 0 for e in self.engs}
        self.sems = {}
        for e in self.engs:
            self.sems[(e, 0)] = es.enter_context(nc.semaphore(f"s_{e}_0"))
        self.dq = ["sp", "act", "pool"]
        self.dcount = {q: 0 for q in self.dq}
        for q in self.dq:
            for i in range(NDSEM):
                self.sems[("d", q, i)] = es.enter_context(nc.semaphore(f"d_{q}_{i}"))
        self.seen = {e: {} for e in self.engs}
        self.ninst = 0
        self.nwait = 0
        self.deferred = []

    def _wait(self, e, tok):
        if tok is None:
            return
        if tok[0] == "e":
            _, F, ep, v = tok
            if F == e and (not SAME_ENGINE_SYNC or e == "pe"):
                return
            s = self.seen[e].get(F)
            if s is not None and s >= (ep, v):
                return
            self.engs[e].wait_ge(self.sems[(F, ep)], v)
            self.seen[e][F] = (ep, v)
            self.nwait += 1
        else:
            _, q, i, v = tok
            key = ("d", q, i)
            if self.seen[e].get(key, 0) >= v:
                return
            self.engs[e].wait_ge(self.sems[key], v)
            self.seen[e][key] = v
            self.nwait += 1

    def _deps(self, e, reads, writes):
        toks = []
        for b in reads:
            if b.last_w is not None:
                toks.append(b.last_w)
            if b.excl:
                toks.extend(t for t in b.readers.values() if not (t[0] == "e" and t[1] == e))
        for b in writes:
            if b.last_w is not None:
                toks.append(b.last_w)
            toks.extend(b.readers.values())
        for t in toks:
            self._wait(e, t)

    def _commit(self, tok, reads, writes):
        key = tok[:2] if tok[0] == "e" else tok[:3]
        for b in reads:
            b.readers[key] = tok
        for b in writes:
            b.last_w = tok
            b.readers = {}

    def op(self, e, fn, reads=(), writes=()):
        self._deps(e, reads, writes)
        if self.cnt[e] >= EPOCH:
            self.epoch[e] += 1
            self.cnt[e] = 0
            self.sems[(e, self.epoch[e])] = self.es.enter_context(self.nc.semaphore(f"s_{e}_{self.epoch[e]}"))
        ins = fn(self.engs[e])
        self.cnt[e] += 1
        ins.then_inc(self.sems[(e, self.epoch[e])], 1)
        tok = ("e", e, self.epoch[e], self.cnt[e])
        self._commit(tok, reads, writes)
        self.ninst += 1
        return tok

    def dma(self, q, out, in_, reads=(), writes=(), **kw):
        self._deps(q, reads, writes)
        j = self.dcount[q]
        i, rnd = j % NDSEM, j // NDSEM
        if rnd > 0:
            self._wait(q, ("d", q, i, 16 * rnd))
        ins = self.engs[q].dma_start(out=out, in_=in_, **kw)
        ins.then_inc(self.sems[("d", q, i)], 16)
        self.dcount[q] = j + 1
        tok = ("d", q, i, 16 * (rnd + 1))
        self._commit(tok, reads, writes)
        self.ninst += 1
        return tok

    def idma(self, out, out_offset, in_, in_offset, reads=(), writes=(), **kw):
        q = "pool"
        self._deps(q, reads, writes)
        j = self.dcount[q]
        i, rnd = j % NDSEM, j // NDSEM
        if rnd > 0:
            self._wait(q, ("d", q, i, 16 * rnd))
        ins = self.engs[q].indirect_dma_start(out=out, out_offset=out_offset, in_=in_, in_offset=in_offset, **kw)
        ins.then_inc(self.sems[("d", q, i)], 16)
        self.dcount[q] = j + 1
        tok = ("d", q, i, 16 * (rnd + 1))
        self._commit(tok, reads, writes)
        self.ninst += 1
        return tok

    def defer(self, fn):
        self.deferred.append(fn)

    def flush(self):
        d, self.deferred = self.deferred, []
        for fn in d:
            fn()

    def _all_tokens(self):
        toks = []
        for e in self.engs:
            if self.cnt[e] > 0 or self.epoch[e] > 0:
                toks.append(("e", e, self.epoch[e], self.cnt[e]))
        for q in self.dq:
            j = self.dcount[q]
            for i in range(NDSEM):
                n = (j - i + NDSEM - 1) // NDSEM if j > i else 0
                if n > 0:
                    toks.append(("d", q, i, 16 * n))
        return toks

    def barrier(self):
        self.flush()
        toks = self._all_tokens()
        for e in self.engs:
            for t in toks:
                if t[0] == "e" and t[1] == e:
                    continue
                self._wait(e, t)

    def finish(self):
        self.flush()
        for t in self._all_tokens():
            if t[0] == "d":
                self._wait("sp", t)


def build(T, depth=DEPTH, debug=False, stop_after=None):
    S_ = CTX + T
    NT = S_ // 128
    NX = T // 128
    assert T % 512 == 0
    nc = bass.Bass("TRN2", target_bir_lowering=False)
    skind = "ExternalOutput" if debug else "Internal"

    def din(name, shape, dt=F32):
        return nc.dram_tensor(name, list(shape), dt, kind="ExternalInput").ap()

    def dscr(name, shape, dt=F32):
        return nc.dram_tensor(name, list(shape), dt, kind=skind).ap()

    L = depth
    x_d = din("x", [T, D])
    ctx_d = din("ctx", [CTX, D])
    ccol_d = din("ccol", [128, 8, 2])
    w_mod_d = din("w_mod", [L, D, 6 * D])
    b_mod_d = din("b_mod", [L, 6 * D])
    w_in_d = din("w_in", [L, D, IN_COLS])
    b_gates_d = din("b_gates", [L, 16])
    conv_w_d = din("conv_w", [L, 5, 1024])
    conv_b_d = din("conv_b", [L, 1024])
    m_norm_w_d = din("m_norm_w", [L, 512])
    q_norm_w_d = din("q_norm_w", [L, 384])
    kv_norm_w_d = din("kv_norm_w", [L, 256])
    w_uq_d = din("w_uq", [L, 384, 768])
    w_ukv_d = din("w_ukv", [L, 256, 1024])
    w_out_d = din("w_out", [L, 1024, 1024])
    ln1_g_d = din("ln1_g", [L, D])
    ln1_b_d = din("ln1_b", [L, D])
    w_router_d = din("w_router", [L, D, NE])
    w_gate_d = din("w_gate", [L, NE, D, D])
    w_up_d = din("w_up", [L, NE, D, D])
    w_down_d = din("w_down", [L, NE, D, D])
    ln2_g_d = din("ln2_g", [L, D])
    ln2_b_d = din("ln2_b", [L, D])
    ident_d = din("c_ident", [128, 128])
    anti_d = din("c_anti", [128, 128])
    maskf_d = din("c_maskf", [128, 128])
    maskb_d = din("c_maskb", [128, 128])
    cos_d = din("c_cos", [64, T])
    sin_d = din("c_sin", [64, T])
    sel_d = din("c_sel", [36, 8, 128])
    out_d = nc.dram_tensor("out", [T, D], F32, kind="ExternalOutput").ap()

    modvec_d = dscr("modvec", [L, 2, 6 * D])
    pqk_d = dscr("pqk", [1024, S_])
    gTf_d = dscr("gTf", [8, S_])
    gTb_d = dscr("gTb", [8, S_])
    pv_d = dscr("pv", [S_, 512])
    so_d = dscr("so", [S_, 512])
    krT_d = dscr("krT", [64, S_], BF16)
    qT_d = dscr("qT", [4, 192, S_], BF16)
    knT_d = dscr("knT", [4, 128, S_], BF16)
    va_d = dscr("va", [S_, 512], BF16)
    qmT_d = dscr("qmT", [4, 128, S_], BF16)
    kmT_d = dscr("kmT", [4, 128, S_], BF16)
    kmtm_d = dscr("kmtm", [S_, 512], BF16)
    hf_d = dscr("hf", [S_, 512])
    hb_d = dscr("hb", [S_, 512])
    moT_d = dscr("moT", [512, S_], BF16)
    aoT_d = dscr("aoT", [512, S_], BF16)
    x1_d = dscr("x1", [S_, D])
    h2T_d = dscr("h2T", [1024, S_], BF16)
    affT_d = dscr("affT", [NE, S_])
    x2_d = dscr("x2", [S_, D])
    h2tm_d = dscr("h2tm", [S_, D], BF16)
    afftm_d = dscr("afftm", [S_, NE])
    cnt_d = dscr("cnt", [NE, T], F16)
    tc_d = dscr("tcnt", [1, NE * (T // 128)])
    moe_d = dscr("moe", [S_, D])
    slot_d = din("c_slot", [128, 8])
    wgb_d = dscr("wgb", [L, NE, D, D], BF16)
    wub_d = dscr("wub", [L, NE, D, D], BF16)
    wdb_d = dscr("wdb", [L, NE, D, D], BF16)

    es = ExitStack()
    with es:
        S = Sched(nc, es)

        uid = [0]

        def sb(st, name, shape, dt):
            uid[0] += 1
            return st.enter_context(nc.sbuf_tensor(f"{name}_{uid[0]}", list(shape), dt))

        def ps(st, name, shape, dt):
            uid[0] += 1
            return st.enter_context(nc.psum_tensor(f"{name}_{uid[0]}", list(shape), dt))

        B_ = {n: Buf(n) for n in ["modvec", "pqk", "gT", "pv", "so", "krT", "qT", "knT", "va", "qmT", "kmT", "kmtm",
                                  "hf", "hb", "moT", "aoT", "x1", "h2T", "affT", "x2", "wcast", "out", "h2tm", "afftm", "cnt", "moe"]}

        ident = sb(es, "ident", [128, 128], F32)
        anti = sb(es, "anti", [128, 128], F32)
        identb = sb(es, "identb", [128, 128], BF16)
        maskf = sb(es, "maskf", [128, 128], F32)
        maskb = sb(es, "maskb", [128, 128], F32)
        ones32 = sb(es, "ones32", [128, 128], F32)
        onesb = sb(es, "onesb", [128, 128], BF16)
        zcol = sb(es, "zcol", [128, 1], F32)
        sel36 = sb(es, "sel36", [36, 8, 128], F32)
        b_const = Buf("const")
        S.dma("sp", ident[:], ident_d, writes=[b_const])
        S.dma("sp", anti[:], anti_d, writes=[b_const])
        S.dma("sp", maskf[:], maskf_d, writes=[b_const])
        S.dma("sp", maskb[:], maskb_d, writes=[b_const])
        S.dma("sp", sel36[:], sel_d, writes=[b_const])
        slotf = sb(es, "slotf", [128, 8], F32)
        S.dma("sp", slotf[:], slot_d, writes=[b_const])
        S.op("dve", lambda e: e.memset(ones32[:], 1.0), writes=[b_const])
        S.op("dve", lambda e: e.memset(onesb[:], 1.0), writes=[b_const])
        S.op("dve", lambda e: e.memset(zcol[:], 0.0), writes=[b_const])
        S.op("dve", lambda e: e.tensor_copy(out=identb[:], in_=ident[:]), reads=[b_const], writes=[b_const])


        def tile_src(l, k):
            if l == 0:
                return ctx_d[k * 128:(k + 1) * 128, :] if k < 2 else x_d[(k - 2) * 128:(k - 1) * 128, :]
            return x2_d[k * 128:(k + 1) * 128, :]

        def ku_b(k):
            return (1 - k) if k < 2 else 2 + (NT - 1 - k)

        def k_of_ku_b(ku):
            return (1 - ku) if ku < 2 else NT - 1 - (ku - 2)

        def ln_stats(st8, xt, bx, eps, junk, bjunk):
            t, bst = st8
            S.op("dve", lambda e: e.tensor_reduce(out=t[:, 0:1], in_=xt, axis=AX.X, op=ALU.add), reads=[bx], writes=[bst])
            S.op("act", lambda e: e.activation(out=junk, in_=xt, func=AF.Square, accum_out=t[:, 1:2]), reads=[bx], writes=[bjunk, bst])
            S.op("dve", lambda e: e.tensor_scalar(out=t[:, 2:3], in0=t[:, 0:1], scalar1=-1.0 / D, scalar2=None, op0=ALU.mult), reads=[bst], writes=[bst])
            S.op("dve", lambda e: e.tensor_tensor(out=t[:, 3:4], in0=t[:, 2:3], in1=t[:, 2:3], op=ALU.mult), reads=[bst], writes=[bst])
            S.op("dve", lambda e: e.scalar_tensor_tensor(out=t[:, 3:4], in0=t[:, 1:2], scalar=1.0 / D, in1=t[:, 3:4], op0=ALU.mult, op1=ALU.subtract), reads=[bst], writes=[bst])
            S.op("dve", lambda e: e.tensor_scalar(out=t[:, 3:4], in0=t[:, 3:4], scalar1=eps, scalar2=None, op0=ALU.add), reads=[bst], writes=[bst])
            S.op("act", lambda e: e.activation(out=t[:, 4:5], in_=t[:, 3:4], func=AF.Sqrt), reads=[bst], writes=[bst])
            S.op("dve", lambda e: e.reciprocal(out=t[:, 4:5], in_=t[:, 4:5]), reads=[bst], writes=[bst])

        def ln_stats_multi(tiles, eps, st, bst, junk, bjunk):
            n = len(tiles)
            for i, (xt, bx) in enumerate(tiles):
                S.op("dve", lambda e: e.tensor_reduce(out=st[:, 0, i:i + 1], in_=xt, axis=AX.X, op=ALU.add), reads=[bx], writes=[bst])
                S.op("act", lambda e: e.activation(out=junk, in_=xt, func=AF.Square, accum_out=st[:, 1, i:i + 1]), reads=[bx], writes=[bjunk, bst])
            S.op("dve", lambda e: e.tensor_scalar(out=st[:, 2, :n], in0=st[:, 0, :n], scalar1=-1.0 / D, scalar2=None, op0=ALU.mult), reads=[bst], writes=[bst])
            S.op("dve", lambda e: e.tensor_tensor(out=st[:, 3, :n], in0=st[:, 2, :n], in1=st[:, 2, :n], op=ALU.mult), reads=[bst], writes=[bst])
            S.op("dve", lambda e: e.scalar_tensor_tensor(out=st[:, 3, :n], in0=st[:, 1, :n], scalar=1.0 / D, in1=st[:, 3, :n], op0=ALU.mult, op1=ALU.subtract), reads=[bst], writes=[bst])
            S.op("dve", lambda e: e.tensor_scalar(out=st[:, 3, :n], in0=st[:, 3, :n], scalar1=eps, scalar2=None, op0=ALU.add), reads=[bst], writes=[bst])
            S.op("act", lambda e: e.activation(out=st[:, 4, :n], in_=st[:, 3, :n], func=AF.Sqrt), reads=[bst], writes=[bst])
            S.op("dve", lambda e: e.reciprocal(out=st[:, 4, :n], in_=st[:, 4, :n]), reads=[bst], writes=[bst])

        def ln_stats_multi_g(tiles, eps, st, bst, junk, bjunk):
            n = len(tiles)
            for i, (xt, bx) in enumerate(tiles):
                S.op("dve", lambda e: e.tensor_reduce(out=st[:, 0, i:i + 1], in_=xt, axis=AX.X, op=ALU.add), reads=[bx], writes=[bst])
                S.op("act", lambda e: e.activation(out=junk, in_=xt, func=AF.Square, accum_out=st[:, 1, i:i + 1]), reads=[bx], writes=[bjunk, bst])
            yield
            S.op("dve", lambda e: e.tensor_scalar(out=st[:, 2, :n], in0=st[:, 0, :n], scalar1=-1.0 / D, scalar2=None, op0=ALU.mult), reads=[bst], writes=[bst])
            S.op("dve", lambda e: e.tensor_tensor(out=st[:, 3, :n], in0=st[:, 2, :n], in1=st[:, 2, :n], op=ALU.mult), reads=[bst], writes=[bst])
            S.op("dve", lambda e: e.scalar_tensor_tensor(out=st[:, 3, :n], in0=st[:, 1, :n], scalar=1.0 / D, in1=st[:, 3, :n], op0=ALU.mult, op1=ALU.subtract), reads=[bst], writes=[bst])
            S.op("dve", lambda e: e.tensor_scalar(out=st[:, 3, :n], in0=st[:, 3, :n], scalar1=eps, scalar2=None, op0=ALU.add), reads=[bst], writes=[bst])
            yield
            S.op("act", lambda e: e.activation(out=st[:, 4, :n], in_=st[:, 3, :n], func=AF.Sqrt), reads=[bst], writes=[bst])
            yield
            S.op("dve", lambda e: e.reciprocal(out=st[:, 4, :n], in_=st[:, 4, :n]), reads=[bst], writes=[bst])

        def load_col(st, name, src_1d, n):
            t = sb(st, name, [128, n], F32)
            b = Buf(name)
            S.dma("sp", t[:], src_1d.rearrange("(c p) -> p c", p=128), writes=[b], allow_slow_non_contiguous=True)
            return t, b

        def load_row_bc(st, name, src_row, n):
            t = sb(st, name, [128, n], F32)
            b = Buf(name)
            S.dma("sp", t[:], src_row.broadcast_to([128, n]), writes=[b])
            return t, b

        for l in range(L):
            last = (l == L - 1)
            ctx_out = not last

            with ExitStack() as ph:
                psA = Ring([ps(ph, f"psA{i}", [128, 512], F32) for i in range(2)], excl=True)
                cc = sb(ph, "cc", [128, 8, 2], F32)
                scs = sb(ph, "scs", [128, 8, 2], F32)
                bcc, bscs, bbm, bmr = Buf(), Buf(), Buf(), Buf()
                S.dma("sp", cc[:], ccol_d, writes=[bcc])
                S.op("act", lambda e: e.activation(out=scs[:], in_=cc[:], func=AF.Silu), reads=[bcc], writes=[bscs])
                modrow = sb(ph, "modrow", [2, 6 * D], F32)
                bmod = sb(ph, "bmod", [2, 6 * D], F32)
                S.dma("sp", bmod[:], b_mod_d[l:l + 1, :].broadcast_to([2, 6 * D]), writes=[bbm])
                wring = Ring([sb(ph, f"wm{i}", [128, 8, 512], F32) for i in range(2)])
                for g in range(12):
                    wt, wb_ = wring.next()
                    S.dma("sp", wt[:], w_mod_d[l, :, g * 512:(g + 1) * 512].rearrange("(kc p) n -> p kc n", p=128), writes=[wb_])
                    pt, pb = psA.next()
                    for kc in range(8):
                        S.op("pe", lambda e: e.matmul(pt[0:2, :], lhsT=scs[:, kc, :], rhs=wt[:, kc, :], start=(kc == 0), stop=(kc == 7)),
                             reads=[bscs, wb_], writes=[pb])
                    S.op("dve", lambda e: e.tensor_tensor(out=modrow[:, g * 512:(g + 1) * 512], in0=pt[0:2, :], in1=bmod[:, g * 512:(g + 1) * 512], op=ALU.add),
                         reads=[pb, bbm], writes=[bmr])
                S.dma("sp", modvec_d[l], modrow[:], reads=[bmr], writes=[B_["modvec"]])
                S.barrier()
            if stop_after == "A":
                break

            def modcol(st, name, r, i, plus1=False):
                t = sb(st, name, [128, 8], F32)
                b = Buf(name)
                S.dma("sp", t[:], modvec_d[l, r, i * D:(i + 1) * D].rearrange("(c p) -> p c", p=128), reads=[B_["modvec"]], writes=[b],
                      allow_slow_non_contiguous=True)
                if plus1:
                    S.op("dve", lambda e: e.tensor_scalar(out=t[:], in0=t[:], scalar1=1.0, scalar2=None, op0=ALU.add), reads=[b], writes=[b])
                return t, b

            def modrow_bc(st, name, r, i):
                t = sb(st, name, [128, D], F32)
                b = Buf(name)
                S.dma("sp", t[:], modvec_d[l, r:r + 1, i * D:(i + 1) * D].broadcast_to([128, D]), reads=[B_["modvec"]], writes=[b])
                return t, b

            blocks = [(0, 2)] + [(2 + 4 * i, 4) for i in range(NX // 4)]

            with ExitStack() as ph:
                psB = Ring([ps(ph, f"psB{i}", [128, 512], F32) for i in range(8)], excl=True)
                w_in_b = sb(ph, "w_in_b", [128, 8, IN_COLS], BF16)
                bwin = Buf("w_in_b")
                S.dma("pool", w_in_b[:], w_in_d[l].rearrange("(kc p) n -> p kc n", p=128), writes=[bwin])
                w_krJ = sb(ph, "w_krJ", [128, 8, 64], BF16)
                S.op("dve", lambda e: e.tensor_scalar(out=w_krJ[:, :, 0:32], in0=w_in_b[:, :, OFF_KR + 32:OFF_KR + 64], scalar1=-1.0, scalar2=None, op0=ALU.mult),
                     reads=[bwin], writes=[bwin])
                S.op("dve", lambda e: e.tensor_copy(out=w_krJ[:, :, 32:64], in_=w_in_b[:, :, OFF_KR:OFF_KR + 32]), reads=[bwin], writes=[bwin])
                w_uq_b = sb(ph, "w_uq_b", [128, 3, 768], BF16)
                w_uqJ = sb(ph, "w_uqJ", [128, 3, 4, 64], BF16)
                w_ukv_b = sb(ph, "w_ukv_b", [128, 2, 1024], BF16)
                w_ukv_v = sb(ph, "w_ukv_v", [128, 2, 512], BF16)
                bwuq = Buf("w_uq")
                bwukv = Buf("w_ukv")
                with ExitStack() as ph2:
                    w_uq32 = sb(ph2, "w_uq32", [128, 3, 768], F32)
                    S.dma("sp", w_uq32[:], w_uq_d[l].rearrange("(c p) n -> p c n", p=128), writes=[bwuq])
                    qnw, bqnw = load_col(ph2, "qnw", q_norm_w_d[l], 3)
                    for c in range(3):
                        S.op("dve", lambda e: e.tensor_scalar(out=w_uq_b[:, c, :], in0=w_uq32[:, c, :], scalar1=qnw[:, c:c + 1], scalar2=A_SCALE, op0=ALU.mult, op1=ALU.mult),
                             reads=[bwuq, bqnw], writes=[bwuq])
                    for h in range(4):
                        S.op("dve", lambda e: e.tensor_scalar(out=w_uqJ[:, :, h, 0:32], in0=w_uq_b[:, :, h * 192 + 160:h * 192 + 192], scalar1=-1.0, scalar2=None, op0=ALU.mult),
                             reads=[bwuq], writes=[bwuq])
                        S.op("dve", lambda e: e.tensor_copy(out=w_uqJ[:, :, h, 32:64], in_=w_uq_b[:, :, h * 192 + 128:h * 192 + 160]), reads=[bwuq], writes=[bwuq])
                    w_ukv32 = sb(ph2, "w_ukv32", [128, 2, 1024], F32)
                    S.dma("sp", w_ukv32[:], w_ukv_d[l].rearrange("(c p) n -> p c n", p=128), writes=[bwukv])
                    kvnw, bkvnw = load_col(ph2, "kvnw", kv_norm_w_d[l], 2)
                    for c in range(2):
                        S.op("dve", lambda e: e.tensor_scalar(out=w_ukv_b[:, c, :], in0=w_ukv32[:, c, :], scalar1=kvnw[:, c:c + 1], scalar2=None, op0=ALU.mult),
                             reads=[bwukv, bkvnw], writes=[bwukv])
                    for h in range(4):
                        S.op("dve", lambda e: e.tensor_copy(out=w_ukv_v[:, :, h * 128:(h + 1) * 128], in_=w_ukv_b[:, :, h * 256 + 128:h * 256 + 256]), reads=[bwukv], writes=[bwukv])
                    S.barrier()
                bg_bc, bbg = load_row_bc(ph, "bg_bc", b_gates_d[l:l + 1, :], 16)
                shx, bshx = modcol(ph, "shx", 0, 0)
                scx, bscx = modcol(ph, "scx", 0, 1, True)
                shc, bshc = modcol(ph, "shc", 1, 0)
                scc, bscc = modcol(ph, "scc", 1, 1, True)

                xring = Ring([sb(ph, f"xt{i}", [128, D], F32) for i in range(8)])
                stm_r = Ring([sb(ph, f"stm{i}", [128, 5, 4], F32) for i in range(2)])
                junk = sb(ph, "junkB", [128, D], BF16)
                bjunk = Buf()
                hT_r = Ring([sb(ph, f"hT{i}", [128, 8, 512], BF16) for i in range(2)])
                qk_r = Ring([sb(ph, f"qkst{i}", [128, 4, 512], F32) for i in range(2)])
                vst = sb(ph, "vst", [128, 4, 512], F32)
                bvst = Buf()
                ost = sb(ph, "ost", [128, 4, 512], F32)
                bost = Buf()
                gtm = sb(ph, "gtm", [128, 16], F32)
                bgtm = Buf()
                gstf = sb(ph, "gstf", [8, 512], F32)
                gstb = sb(ph, "gstb", [8, 512], F32)
                bgstf, bgstb = Buf(), Buf()
                cqb = sb(ph, "cqb", [128, 3, 512], BF16)
                cqsq = sb(ph, "cqsq", [128, 3, 512], F32)
                ckvb = sb(ph, "ckvb", [128, 2, 512], BF16)
                ckvsq = sb(ph, "ckvsq", [128, 2, 512], F32)
                bcq, bckv = Buf(), Buf()
                rq = sb(ph, "rq", [128, 512], F32)
                rkv = sb(ph, "rkv", [128, 512], F32)
                brq, brkv = Buf(), Buf()
                rkvc = sb(ph, "rkvc", [128, 4], F32)
                brkvc = Buf()
                cos_t = sb(ph, "cos_t", [64, 512], F32)
                sin_t = sb(ph, "sin_t", [64, 512], F32)
                bcs = Buf()
                krst = sb(ph, "krst", [64, 512], BF16)
                bkrst = Buf()
                tmp64 = sb(ph, "tmp64", [64, 512], F32)
                tmp64b = sb(ph, "tmp64b", [64, 512], F32)
                btmp = Buf()
                qst = sb(ph, "qst", [128, 4, 512], BF16)
                qrst = sb(ph, "qrst", [64, 4, 512], BF16)
                bqst, bqrst = Buf(), Buf()
                knst = sb(ph, "knst", [128, 4, 512], BF16)
                bknst = Buf()
                vast = sb(ph, "vast", [128, 4, 512], BF16)
                bvast = Buf()
                print("phase B sbuf remaining", nc.sbuf_bytes_remaining)

                evac_i = [0]

                def evac(out, in_, reads, writes):
                    evac_i[0] += 1
                    if evac_i[0] % 2:
                        S.op("act", lambda e: e.copy(out=out, in_=in_), reads=reads, writes=writes)
                    else:
                        S.op("dve", lambda e: e.tensor_copy(out=out, in_=in_), reads=reads, writes=writes)

                def load_block(bi):
                    k0, ntile = blocks[bi]
                    tiles = []
                    for ti in range(ntile):
                        xt, bx = xring.next()
                        S.dma("sp", xt[:], tile_src(l, k0 + ti), reads=[B_["x2"]], writes=[bx])
                        tiles.append((xt, bx))
                    return tiles

                nxt = load_block(0)
                for bi, (k0, ntile) in enumerate(blocks):
                    cur = nxt
                    is_ctx = (k0 == 0)
                    nb = ntile * 128
                    t0 = k0 * 128
                    sh_, sc_ = (shc, scc) if is_ctx else (shx, scx)
                    bsh_, bsc_ = (bshc, bscc) if is_ctx else (bshx, bscx)
                    xns = []
                    stm, bstm = stm_r.next()
                    ln_stats_multi([(cur[ti][0][:], cur[ti][1]) for ti in range(ntile)], 1e-6, stm, bstm, junk[:], bjunk)
                    for ti in range(ntile):
                        xt, bx = cur[ti]
                        S.op("dve", lambda e: e.tensor_scalar(out=xt[:], in0=xt[:], scalar1=stm[:, 2, ti:ti + 1], scalar2=stm[:, 4, ti:ti + 1], op0=ALU.add, op1=ALU.mult),
                             reads=[bx, bstm], writes=[bx])
                        xns.append((xt, bx))
                    if bi + 1 < len(blocks):
                        nxt = load_block(bi + 1)
                    S.flush()
                    hT, bhT = hT_r.next()
                    for kc in range(8):
                        pt, pb = psB.next()
                        for ti in range(ntile):
                            xn, bxn = xns[ti]
                            S.op("pe", lambda e: e.transpose(out=pt[:, ti * 128:(ti + 1) * 128], in_=xn[:, kc * 128:(kc + 1) * 128], identity=ident[:]),
                                 reads=[bxn, b_const], writes=[pb])
                        if kc % 2:
                            S.op("act", lambda e: e.activation(out=hT[:, kc, :nb], in_=pt[:, :nb], func=AF.Identity, bias=sh_[:, kc:kc + 1], scale=sc_[:, kc:kc + 1]),
                                 reads=[pb, bsh_, bsc_], writes=[bhT])
                        else:
                            S.op("dve", lambda e: e.tensor_scalar(out=hT[:, kc, :nb], in0=pt[:, :nb], scalar1=sc_[:, kc:kc + 1], scalar2=sh_[:, kc:kc + 1], op0=ALU.mult, op1=ALU.add),
                                 reads=[pb, bsh_, bsc_], writes=[bhT])
                    for oc in range(8):
                        if oc % 4 == 0:
                            qkst, bqkst = qk_r.next()
                        pt, pb = psB.next()
                        for kc in range(8):
                            S.op("pe", lambda e: e.matmul(pt[:, :nb], lhsT=w_in_b[:, kc, oc * 128:(oc + 1) * 128], rhs=hT[:, kc, :nb], start=(kc == 0), stop=(kc == 7)),
                                 reads=[bwin, bhT], writes=[pb])
                        evac(qkst[:, oc % 4, :nb], pt[:, :nb], [pb], [bqkst])
                        if oc % 4 == 3:
                            o0 = oc - 3
                            S.dma("sp", pqk_d.rearrange("(oc p) s -> p oc s", p=128)[:, o0:o0 + 4, t0:t0 + nb], qkst[:, :, :nb], reads=[bqkst], writes=[B_["pqk"]])
                    for ti in range(ntile):
                        k = k0 + ti
                        pt, pb = psB.next()
                        for kc in range(8):
                            S.op("pe", lambda e: e.matmul(pt[:, :], lhsT=hT[:, kc, ti * 128:(ti + 1) * 128], rhs=w_in_b[:, kc, OFF_V:OFF_V + 512], start=(kc == 0), stop=(kc == 7)),
                                 reads=[bwin, bhT], writes=[pb])
                        evac(vst[:, ti, :], pt[:, :], [pb], [bvst])
                        pt, pb = psB.next()
                        for kc in range(8):
                            S.op("pe", lambda e: e.matmul(pt[:, :], lhsT=hT[:, kc, ti * 128:(ti + 1) * 128], rhs=w_in_b[:, kc, OFF_O:OFF_O + 512], start=(kc == 0), stop=(kc == 7)),
                                 reads=[bwin, bhT], writes=[pb])
                        S.op("act", lambda e: e.activation(out=ost[:, ti, :], in_=pt[:, :], func=AF.Sigmoid), reads=[pb], writes=[bost])
                        pt, pb = psB.next()
                        for kc in range(8):
                            S.op("pe", lambda e: e.matmul(pt[:, 0:16], lhsT=hT[:, kc, ti * 128:(ti + 1) * 128], rhs=w_in_b[:, kc, OFF_G:OFF_G + 16], start=(kc == 0), stop=(kc == 7)),
                                 reads=[bwin, bhT], writes=[pb])
                        S.op("dve", lambda e: e.tensor_tensor(out=gtm[:], in0=pt[:, 0:16], in1=bg_bc[:], op=ALU.add), reads=[pb, bbg], writes=[bgtm])
                        pt2, pb2 = psB.next()
                        S.op("pe", lambda e: e.transpose(out=pt2[0:8, 0:128], in_=gtm[:, 0:8], identity=ident[:]), reads=[bgtm, b_const], writes=[pb2])
                        S.op("pe", lambda e: e.matmul(pt2[0:8, 128:256], lhsT=gtm[:, 8:16], rhs=anti[:], start=True, stop=True), reads=[bgtm, b_const], writes=[pb2])
                        S.op("act", lambda e: e.copy(out=gstf[:, ti * 128:(ti + 1) * 128], in_=pt2[0:8, 0:128]), reads=[pb2], writes=[bgstf])
                        tj = ntile - 1 - ti
                        S.op("act", lambda e: e.copy(out=gstb[:, tj * 128:(tj + 1) * 128], in_=pt2[0:8, 128:256]), reads=[pb2], writes=[bgstb])
                    S.dma("sp", pv_d[t0:t0 + nb, :].rearrange("(t p) c -> p t c", p=128), vst[:, :ntile, :], reads=[bvst], writes=[B_["pv"]])
                    S.dma("sp", so_d[t0:t0 + nb, :].rearrange("(t p) c -> p t c", p=128), ost[:, :ntile, :], reads=[bost], writes=[B_["so"]])
                    S.dma("sp", gTf_d[:, t0:t0 + nb], gstf[:, :nb], reads=[bgstf], writes=[B_["gT"]])
                    u0 = ku_b(k0 + ntile - 1) * 128
                    S.dma("sp", gTb_d[:, u0:u0 + nb], gstb[:, :nb], reads=[bgstb], writes=[B_["gT"]])
                    for c in range(3):
                        pt, pb = psB.next()
                        for kc in range(8):
                            S.op("pe", lambda e: e.matmul(pt[:, :nb], lhsT=w_in_b[:, kc, OFF_CQ + c * 128:OFF_CQ + (c + 1) * 128], rhs=hT[:, kc, :nb], start=(kc == 0), stop=(kc == 7)),
                                 reads=[bwin, bhT], writes=[pb])
                        S.op("dve", lambda e: e.tensor_copy(out=cqb[:, c, :nb], in_=pt[:, :nb]), reads=[pb], writes=[bcq])
                        S.op("act", lambda e: e.activation(out=cqsq[:, c, :nb], in_=pt[:, :nb], func=AF.Square), reads=[pb], writes=[bcq])
                    pt, pb = psB.next()
                    for c in range(3):
                        S.op("pe", lambda e: e.matmul(pt[:, :nb], lhsT=ones32[:], rhs=cqsq[:, c, :nb], start=(c == 0), stop=(c == 2)), reads=[bcq, b_const], writes=[pb])
                    S.op("dve", lambda e: e.tensor_scalar(out=rq[:, :nb], in0=pt[:, :nb], scalar1=1.0 / 384, scalar2=1e-6, op0=ALU.mult, op1=ALU.add), reads=[pb], writes=[brq])
                    S.op("act", lambda e: e.activation(out=rq[:, :nb], in_=rq[:, :nb], func=AF.Sqrt), reads=[brq], writes=[brq])
                    S.op("dve", lambda e: e.reciprocal(out=rq[:, :nb], in_=rq[:, :nb]), reads=[brq], writes=[brq])
                    for c in range(2):
                        pt, pb = psB.next()
                        for kc in range(8):
                            S.op("pe", lambda e: e.matmul(pt[:, :nb], lhsT=w_in_b[:, kc, OFF_CKV + c * 128:OFF_CKV + (c + 1) * 128], rhs=hT[:, kc, :nb], start=(kc == 0), stop=(kc == 7)),
                                 reads=[bwin, bhT], writes=[pb])
                        S.op("dve", lambda e: e.tensor_copy(out=ckvb[:, c, :nb], in_=pt[:, :nb]), reads=[pb], writes=[bckv])
                        S.op("act", lambda e: e.activation(out=ckvsq[:, c, :nb], in_=pt[:, :nb], func=AF.Square), reads=[pb], writes=[bckv])
                    pt, pb = psB.next()
                    for c in range(2):
                        S.op("pe", lambda e: e.matmul(pt[:, :nb], lhsT=ones32[:], rhs=ckvsq[:, c, :nb], start=(c == 0), stop=(c == 1)), reads=[bckv, b_const], writes=[pb])
                    S.op("dve", lambda e: e.tensor_scalar(out=rkv[:, :nb], in0=pt[:, :nb], scalar1=1.0 / 256, scalar2=1e-6, op0=ALU.mult, op1=ALU.add), reads=[pb], writes=[brkv])
                    S.op("act", lambda e: e.activation(out=rkv[:, :nb], in_=rkv[:, :nb], func=AF.Sqrt), reads=[brkv], writes=[brkv])
                    S.op("dve", lambda e: e.reciprocal(out=rkv[:, :nb], in_=rkv[:, :nb]), reads=[brkv], writes=[brkv])
                    pt, pb = psB.next()
                    for ti in range(ntile):
                        for c in range(2):
                            S.op("pe", lambda e: e.matmul(pt[:, ti:ti + 1], lhsT=ckvsq[:, c, ti * 128:(ti + 1) * 128], rhs=ones32[:, 0:1], start=(c == 0), stop=(c == 1)),
                                 reads=[bckv, b_const], writes=[pb])
                    S.op("dve", lambda e: e.tensor_scalar(out=rkvc[:, :ntile], in0=pt[:, :ntile], scalar1=1.0 / 256, scalar2=1e-6, op0=ALU.mult, op1=ALU.add), reads=[pb], writes=[brkvc])
                    S.op("act", lambda e: e.activation(out=rkvc[:, :ntile], in_=rkvc[:, :ntile], func=AF.Sqrt), reads=[brkvc], writes=[brkvc])
                    S.op("dve", lambda e: e.reciprocal(out=rkvc[:, :ntile], in_=rkvc[:, :ntile]), reads=[brkvc], writes=[brkvc])
                    if not is_ctx:
                        xo = t0 - CTX
                        S.dma("sp", cos_t[:, :nb], cos_d[:, xo:xo + nb], writes=[bcs])
                        S.dma("sp", sin_t[:, :nb], sin_d[:, xo:xo + nb], writes=[bcs])
                    pt, pb = psB.next()
                    for kc in range(8):
                        S.op("pe", lambda e: e.matmul(pt[0:64, :nb], lhsT=w_in_b[:, kc, OFF_KR:OFF_KR + 64], rhs=hT[:, kc, :nb], start=(kc == 0), stop=(kc == 7)),
                             reads=[bwin, bhT], writes=[pb])
                    if is_ctx:
                        S.op("act", lambda e: e.copy(out=krst[:, :nb], in_=pt[0:64, :nb]), reads=[pb], writes=[bkrst])
                    else:
                        pt2, pb2 = psB.next()
                        for kc in range(8):
                            S.op("pe", lambda e: e.matmul(pt2[0:64, :nb], lhsT=w_krJ[:, kc, :], rhs=hT[:, kc, :nb], start=(kc == 0), stop=(kc == 7)),
                                 reads=[bwin, bhT], writes=[pb2])
                        S.op("dve", lambda e: e.tensor_tensor(out=tmp64[:, :nb], in0=pt[0:64, :nb], in1=cos_t[:, :nb], op=ALU.mult), reads=[pb, bcs], writes=[btmp])
                        S.op("dve", lambda e: e.tensor_tensor(out=tmp64b[:, :nb], in0=pt2[0:64, :nb], in1=sin_t[:, :nb], op=ALU.mult), reads=[pb2, bcs], writes=[btmp])
                        S.op("dve", lambda e: e.tensor_tensor(out=krst[:, :nb], in0=tmp64[:, :nb], in1=tmp64b[:, :nb], op=ALU.add), reads=[btmp], writes=[bkrst])
                    S.dma("sp", krT_d[:, t0:t0 + nb], krst[:, :nb], reads=[bkrst], writes=[B_["krT"]])
                    if (not is_ctx) or ctx_out:
                        for h in range(4):
                            pt, pb = psB.next()
                            for c in range(3):
                                S.op("pe", lambda e: e.matmul(pt[:, :nb], lhsT=w_uq_b[:, c, h * 192:h * 192 + 128], rhs=cqb[:, c, :nb], start=(c == 0), stop=(c == 2)),
                                     reads=[bwuq, bcq], writes=[pb])
                            S.op("dve", lambda e: e.tensor_tensor(out=qst[:, h, :nb], in0=pt[:, :nb], in1=rq[:, :nb], op=ALU.mult), reads=[pb, brq], writes=[bqst])
                            pt, pb = psB.next()
                            for c in range(3):
                                S.op("pe", lambda e: e.matmul(pt[0:64, :nb], lhsT=w_uq_b[:, c, h * 192 + 128:h * 192 + 192], rhs=cqb[:, c, :nb], start=(c == 0), stop=(c == 2)),
                                     reads=[bwuq, bcq], writes=[pb])
                            if is_ctx:
                                S.op("dve", lambda e: e.tensor_tensor(out=qrst[:, h, :nb], in0=pt[0:64, :nb], in1=rq[0:64, :nb], op=ALU.mult), reads=[pb, brq], writes=[bqrst])
                            else:
                                pt2, pb2 = psB.next()
                                for c in range(3):
                                    S.op("pe", lambda e: e.matmul(pt2[0:64, :nb], lhsT=w_uqJ[:, c, h, :], rhs=cqb[:, c, :nb], start=(c == 0), stop=(c == 2)),
                                         reads=[bwuq, bcq], writes=[pb2])
                                S.op("dve", lambda e: e.tensor_tensor(out=tmp64[:, :nb], in0=pt[0:64, :nb], in1=cos_t[:, :nb], op=ALU.mult), reads=[pb, bcs], writes=[btmp])
                                S.op("dve", lambda e: e.tensor_tensor(out=tmp64b[:, :nb], in0=pt2[0:64, :nb], in1=sin_t[:, :nb], op=ALU.mult), reads=[pb2, bcs], writes=[btmp])
                                S.op("dve", lambda e: e.tensor_tensor(out=tmp64[:, :nb], in0=tmp64[:, :nb], in1=tmp64b[:, :nb], op=ALU.add), reads=[btmp], writes=[btmp])
                                S.op("dve", lambda e: e.tensor_tensor(out=qrst[:, h, :nb], in0=tmp64[:, :nb], in1=rq[0:64, :nb], op=ALU.mult), reads=[btmp, brq], writes=[bqrst])
                        S.dma("sp", qT_d[:, 0:128, t0:t0 + nb].rearrange("h p s -> p h s"), qst[:, :, :nb], reads=[bqst], writes=[B_["qT"]])
                        S.dma("sp", qT_d[:, 128:192, t0:t0 + nb].rearrange("h p s -> p h s"), qrst[:, :, :nb], reads=[bqrst], writes=[B_["qT"]])
                    for h in range(4):
                        pt, pb = psB.next()
                        for c in range(2):
                            S.op("pe", lambda e: e.matmul(pt[:, :nb], lhsT=w_ukv_b[:, c, h * 256:h * 256 + 128], rhs=ckvb[:, c, :nb], start=(c == 0), stop=(c == 1)),
                                 reads=[bwukv, bckv], writes=[pb])
                        S.op("dve", lambda e: e.tensor_tensor(out=knst[:, h, :nb], in0=pt[:, :nb], in1=rkv[:, :nb], op=ALU.mult), reads=[pb, brkv], writes=[bknst])
                    S.dma("sp", knT_d[:, :, t0:t0 + nb].rearrange("h p s -> p h s"), knst[:, :, :nb], reads=[bknst], writes=[B_["knT"]])
                    for ti in range(ntile):
                        pt, pb = psB.next()
                        for c in range(2):
                            S.op("pe", lambda e: e.matmul(pt[:, :], lhsT=ckvb[:, c, ti * 128:(ti + 1) * 128], rhs=w_ukv_v[:, c, :], start=(c == 0), stop=(c == 1)),
                                 reads=[bwukv, bckv], writes=[pb])
                        S.op("act", lambda e: e.activation(out=vast[:, ti, :], in_=pt[:, :], func=AF.Copy, scale=rkvc[:, ti:ti + 1]), reads=[pb, brkvc], writes=[bvast])
                    S.dma("sp", va_d[t0:t0 + nb, :].rearrange("(t p) c -> p t c", p=128), vast[:, :ntile, :], reads=[bvast], writes=[B_["va"]])
                S.barrier()
            if stop_after == "B":
                break
            def xstream():
                with ExitStack() as ph:
                    psC = Ring([ps(ph, f"psC{i}", [128, 1024], BF16) for i in range(2)], excl=True)
                    cw = sb(ph, "cw", [128, 5, 8], F32)
                    bcw = Buf()
                    for k in range(5):
                        S.dma("sp", cw[:, k, :], conv_w_d[l, k, :].rearrange("(oc p) -> p oc", p=128), writes=[bcw], allow_slow_non_contiguous=True)
                    cb_, bcb = load_col(ph, "cb", conv_b_d[l], 8)
                    CB = 1024
                    xin_r = Ring([sb(ph, f"xin{i}", [128, CB + 4], F32) for i in range(3)])
                    acc_r = Ring([sb(ph, f"cacc{i}", [128, CB], F32) for i in range(2)])
                    ctmp = sb(ph, "ctmp", [128, CB], F32)
                    bctmp = Buf()
                    qo_r = Ring([sb(ph, f"qo{i}", [128, CB], BF16) for i in range(3)])
                    kt_r = Ring([sb(ph, f"ktst{i}", [128, 8, 128], BF16) for i in range(2)])
                    pieces = []
                    for (sa, sb_) in ((0, CTX), (CTX, S_)):
                        a = sa
                        while a < sb_:
                            b = min(a + CB, sb_)
                            pieces.append((sa, sb_, a, b))
                            a = b
                    for oc in range(8):
                        eng = "dve"
                        for (sa, sb_, a, b) in pieces:
                            n = b - a
                            yield
                            xin, bxin = xin_r.next()
                            lo = a - 2 if a > sa else a
                            hi = b + 2 if b < sb_ else b
                            if a == sa:
                                S.op("dve", lambda e: e.memset(xin[:, 0:2], 0.0), writes=[bxin])
                            if b == sb_:
                                S.op("dve", lambda e: e.memset(xin[:, 2 + n:4 + n], 0.0), writes=[bxin])
                            S.dma("sp", xin[:, 2 - (a - lo):2 + n + (hi - b)], pqk_d[oc * 128:(oc + 1) * 128, lo:hi], reads=[B_["pqk"]], writes=[bxin])
                            yield
                            acc, bacc = acc_r.next()
                            S.op(eng, lambda e: e.tensor_scalar(out=acc[:, :n], in0=xin[:, 0:n], scalar1=cw[:, 0, oc:oc + 1], scalar2=None, op0=ALU.mult), reads=[bxin, bcw], writes=[bacc])
                            for k in range(1, 5):
                                if eng == "dve":
                                    S.op(eng, lambda e: e.scalar_tensor_tensor(out=acc[:, :n], in0=xin[:, k:k + n], scalar=cw[:, k, oc:oc + 1], in1=acc[:, :n], op0=ALU.mult, op1=ALU.add),
                                         reads=[bxin, bcw, bacc], writes=[bacc])
                                else:
                                    S.op(eng, lambda e: e.tensor_scalar(out=ctmp[:, :n], in0=xin[:, k:k + n], scalar1=cw[:, k, oc:oc + 1], scalar2=None, op0=ALU.mult), reads=[bxin, bcw], writes=[bctmp])
                                    S.op(eng, lambda e: e.tensor_tensor(out=acc[:, :n], in0=acc[:, :n], in1=ctmp[:, :n], op=ALU.add), reads=[bacc, bctmp], writes=[bacc])
                            yield
                            qo, bqo = qo_r.next()
                            S.op("act", lambda e: e.activation(out=qo[:, :n], in_=acc[:, :n], func=AF.Silu, bias=cb_[:, oc:oc + 1]), reads=[bacc, bcb], writes=[bqo])
                            if oc < 4:
                                S.dma("sp", qmT_d[oc, :, a:b], qo[:, :n], reads=[bqo], writes=[B_["qmT"]])
                            else:
                                h = oc - 4
                                S.dma("sp", kmT_d[h, :, a:b], qo[:, :n], reads=[bqo], writes=[B_["kmT"]])
                                nt_ = n // 128
                                yield
                                pt, pb = psC.next()
                                for j in range(nt_):
                                    S.op("pe", lambda e: e.transpose(out=pt[:, j * 128:(j + 1) * 128], in_=qo[:, j * 128:(j + 1) * 128], identity=identb[:]), reads=[bqo, b_const], writes=[pb])
                                yield
                                kt, bkt = kt_r.next()
                                S.op("dve", lambda e: e.tensor_copy(out=kt[:, :nt_, :], in_=pt[:, :n].rearrange("p (t c) -> p t c", c=128)), reads=[pb], writes=[bkt])
                                S.dma("sp", kmtm_d[a:b, h * 128:(h + 1) * 128].rearrange("(t p) c -> p t c", p=128), kt[:, :nt_, :], reads=[bkt], writes=[B_["kmtm"]])
                    S.barrier()

                with ExitStack() as phm:
                    ea_tm = sb(phm, "ea_tm", [128, NT, 8], F32)
                    fl_tm = sb(phm, "fl_tm", [128, NT, 8], F32)
                    decay_bc = sb(phm, "decay_bc", [128, 8, NT], F32)
                    bea, bfl, bdec = Buf(), Buf(), Buf()
                    with ExitStack() as ph:
                        psD = Ring([ps(ph, f"psD{i}", [128, 512], F32) for i in range(3)], excl=True)
                        t1 = sb(ph, "t1", [36, S_], F32)
                        t2 = sb(ph, "t2", [36, S_], F32)
                        t3 = sb(ph, "t3", [36, S_], F32)
                        bt1, bt2, bt3 = Buf(), Buf(), Buf()
                        S.op("dve", lambda e: e.memset(t1[:], 30.0), writes=[bt1])
                        S.op("pool", lambda e: e.memset(t3[:], 0.0), writes=[bt3])
                        S.dma("sp", t3[0:4, :], gTf_d[0:4, :], reads=[B_["gT"]], writes=[bt3])
                        S.dma("sp", t3[32:36, :], gTb_d[0:4, :], reads=[B_["gT"]], writes=[bt3])
                        S.dma("sp", t1[0:4, :], gTf_d[4:8, :], reads=[B_["gT"]], writes=[bt1])
                        S.dma("sp", t1[32:36, :], gTb_d[4:8, :], reads=[B_["gT"]], writes=[bt1])
                        S.op("act", lambda e: e.activation(out=t1[:], in_=t1[:], func=AF.Exp, scale=-1.0), reads=[bt1], writes=[bt1])
                        S.op("dve", lambda e: e.tensor_scalar(out=t1[:], in0=t1[:], scalar1=1.0, scalar2=None, op0=ALU.add), reads=[bt1], writes=[bt1])
                        S.op("act", lambda e: e.activation(out=t1[:], in_=t1[:], func=AF.Ln), reads=[bt1], writes=[bt1])
                        S.op("dve", lambda e: e.tensor_tensor_scan(out=t2[:], data0=t1[:], data1=zcol[0:36, 0:1].broadcast_to([36, S_]), initial=0.0, op0=ALU.add, op1=ALU.add),
                             reads=[bt1, b_const], writes=[bt2])
                        yield
                        S.op("dve", lambda e: e.tensor_tensor(out=t3[:], in0=t3[:], in1=t2[:], op=ALU.add), reads=[bt3, bt2], writes=[bt3])
                        S.op("dve", lambda e: e.tensor_tensor_scan(out=t1[:], data0=t3[:], data1=t3[:], initial=0.0, op0=ALU.max, op1=ALU.max), reads=[bt3, bt1], writes=[bt1])
                        yield
                        mcur = sb(ph, "mcur", [36, NT], F32)
                        mprev = sb(ph, "mprev", [36, NT], F32)
                        mprevc = sb(ph, "mprevc", [36, NT], F32)
                        dec = sb(ph, "dec", [36, NT], F32)
                        bm = Buf()
                        t1v = t1[:].rearrange("p (t c) -> p t c", c=128)
                        t2v = t2[:].rearrange("p (t c) -> p t c", c=128)
                        t3v = t3[:].rearrange("p (t c) -> p t c", c=128)
                        S.op("dve", lambda e: e.tensor_copy(out=mcur[:].rearrange("p (t o) -> p t o", o=1), in_=t1v[:, :, 127:128]), reads=[bt1], writes=[bm])
                        S.op("dve", lambda e: e.memset(mprev[:, 0:1], 0.0), writes=[bm])
                        S.op("dve", lambda e: e.tensor_copy(out=mprev[:, 1:NT], in_=mcur[:, 0:NT - 1]), reads=[bm], writes=[bm])
                        S.op("dve", lambda e: e.tensor_tensor(out=dec[:], in0=mprev[:], in1=mcur[:], op=ALU.subtract), reads=[bm], writes=[bm])
                        S.op("act", lambda e: e.activation(out=dec[:], in_=dec[:], func=AF.Exp), reads=[bm], writes=[bm])
                        S.op("dve", lambda e: e.tensor_scalar(out=mprevc[:], in0=mprev[:], scalar1=-0.5 * math.log(DH), scalar2=None, op0=ALU.add), reads=[bm], writes=[bm])
                        mpb = mprev[:].rearrange("p (t o) -> p t o", o=1).broadcast_to([36, NT, 128])
                        mpcb = mprevc[:].rearrange("p (t o) -> p t o", o=1).broadcast_to([36, NT, 128])
                        S.op("dve", lambda e: e.tensor_tensor(out=t3v, in0=t3v, in1=mpb, op=ALU.subtract), reads=[bt3, bm], writes=[bt3])
                        S.op("act", lambda e: e.activation(out=t3[:], in_=t3[:], func=AF.Exp), reads=[bt3], writes=[bt3])
                        yield
                        S.op("dve", lambda e: e.tensor_tensor(out=t2v, in0=t2v, in1=mpcb, op=ALU.subtract), reads=[bt2, bm], writes=[bt2])
                        S.op("act", lambda e: e.activation(out=t2[:], in_=t2[:], func=AF.Exp), reads=[bt2], writes=[bt2])
                        tm_r = Ring([sb(ph, f"tmA{i}", [128, 128], F32) for i in range(2)])
                        for ku in range(NT):
                            yield
                            pt, pb = psD.next()
                            S.op("pe", lambda e: e.transpose(out=pt[:, 0:36], in_=t3[0:36, ku * 128:(ku + 1) * 128], identity=ident[0:36, 0:36]), reads=[bt3, b_const], writes=[pb])
                            S.op("pe", lambda e: e.transpose(out=pt[:, 36:72], in_=t2[0:36, ku * 128:(ku + 1) * 128], identity=ident[0:36, 0:36]), reads=[bt2, b_const], writes=[pb])
                            yield
                            tmA, btm = tm_r.next()
                            S.op("act", lambda e: e.copy(out=tmA[:, 0:72], in_=pt[:, 0:72]), reads=[pb], writes=[btm])
                            yield
                            S.op("pool", lambda e: e.tensor_copy(out=ea_tm[:, ku, 0:4], in_=tmA[:, 0:4]), reads=[btm], writes=[bea])
                            S.op("pool", lambda e: e.tensor_copy(out=fl_tm[:, ku, 0:4], in_=tmA[:, 36:40]), reads=[btm], writes=[bfl])
                            pt2, pb2 = psD.next()
                            S.op("pe", lambda e: e.matmul(pt2[:, 0:4], lhsT=anti[:], rhs=tmA[:, 32:36], start=True, stop=True), reads=[btm, b_const], writes=[pb2])
                            S.op("pe", lambda e: e.matmul(pt2[:, 4:8], lhsT=anti[:], rhs=tmA[:, 68:72], start=True, stop=True), reads=[btm, b_const], writes=[pb2])
                            kb = k_of_ku_b(ku)
                            yield
                            S.op("dve", lambda e: e.tensor_copy(out=ea_tm[:, kb, 4:8], in_=pt2[:, 0:4]), reads=[pb2], writes=[bea])
                            S.op("dve", lambda e: e.tensor_copy(out=fl_tm[:, kb, 4:8], in_=pt2[:, 4:8]), reads=[pb2], writes=[bfl])
                        for j in range(8):
                            pt, pb = psD.next()
                            S.op("pe", lambda e: e.matmul(pt[:, 0:NT], lhsT=sel36[0:36, j, :], rhs=dec[0:36, 0:NT], start=True, stop=True), reads=[bm, b_const], writes=[pb])
                            S.op("dve", lambda e: e.tensor_copy(out=decay_bc[:, j, :], in_=pt[:, 0:NT]), reads=[pb], writes=[bdec])
                        S.barrier()
                    with ExitStack() as ph:
                        psE = Ring([ps(ph, f"psE{i}", [128, 512], F32) for i in range(4)], excl=True)
                        C32 = [[sb(ph, f"C32_{d}{h}", [128, 129], F32) for h in range(4)] for d in range(2)]
                        Cb = [[sb(ph, f"Cb_{d}{h}", [128, 129], BF16) for h in range(4)] for d in range(2)]
                        bC32 = [[Buf() for h in range(4)] for d in range(2)]
                        bCb = [[Buf() for h in range(4)] for d in range(2)]
                        for d in range(2):
                            for h in range(4):
                                S.op("pool", lambda e: e.memset(C32[d][h][:], 0.0), writes=[bC32[d][h]])
                                S.op("pool", lambda e: e.memset(Cb[d][h][:], 0.0), writes=[bCb[d][h]])
                        qt_r = Ring([sb(ph, f"eQT{i}", [128, 4, 128], BF16) for i in range(4)])
                        kt_r = Ring([sb(ph, f"eKT{i}", [128, 4, 128], BF16) for i in range(4)])
                        ktm_r = Ring([sb(ph, f"eKtm{i}", [128, 512], BF16) for i in range(4)])
                        v_r = Ring([sb(ph, f"eV{i}", [128, 512], F32) for i in range(4)])
                        vp_r = Ring([sb(ph, f"eVp{i}", [128, 129], BF16) for i in range(8)])
                        sm_r = Ring([sb(ph, f"eSm{i}", [128, 128], BF16) for i in range(8)])
                        dm_r = Ring([sb(ph, f"edm{i}", [128, 2], F32) for i in range(8)])
                        hst_r = Ring([sb(ph, f"ehst{i}", [128, 512], F32) for i in range(4)])
                        masks = [maskf, maskb]

                        def e_load(u, d):
                            k = u if d == 0 else k_of_ku_b(u)
                            QT, bQT = qt_r.next()
                            KT, bKT = kt_r.next()
                            Ktm, bKtm = ktm_r.next()
                            V, bV = v_r.next()
                            S.dma("sp", QT[:], qmT_d[:, :, k * 128:(k + 1) * 128].rearrange("h p s -> p h s"), reads=[B_["qmT"]], writes=[bQT])
                            S.dma("sp", KT[:], kmT_d[:, :, k * 128:(k + 1) * 128].rearrange("h p s -> p h s"), reads=[B_["kmT"]], writes=[bKT])
                            S.dma("sp", Ktm[:], kmtm_d[k * 128:(k + 1) * 128, :], reads=[B_["kmtm"]], writes=[bKtm])
                            S.dma("sp", V[:], pv_d[k * 128:(k + 1) * 128, :], reads=[B_["pv"]], writes=[bV])
                            return (k, QT, bQT, KT, bKT, Ktm, bKtm, V, bV)

                        steps = [(u, d) for u in range(NT) for d in range(2)]
                        (bS_, bbS), (bOa, bbOa), (bOb, bbOb), (bCa, bbCa) = [(psE.tiles[i], psE.bufs[i]) for i in range(4)]

                        def pO_of(h):
                            return (bOa, bbOa, h * 129) if h < 3 else (bOb, bbOb, 0)

                        def pC_of(h):
                            return (bCa, bbCa, h * 129) if h < 3 else (bOb, bbOb, 129)

                        nxt = e_load(*steps[0])
                        for si, (u, d) in enumerate(steps):
                            (k, QT, bQT, KT, bKT, Ktm, bKtm, V, bV) = nxt
                            if si + 1 < len(steps):
                                nxt = e_load(*steps[si + 1])
                            S.flush()
                            hst, bhst = hst_r.next()
                            Vps, Sms, dms = [], [], []
                            for h in range(4):
                                j = d * 4 + h
                                Vp, bVp = vp_r.next()
                                S.op("act", lambda e: e.activation(out=Vp[:, 0:128], in_=V[:, h * 128:(h + 1) * 128], func=AF.Copy, scale=ea_tm[:, k, j:j + 1]), reads=[bV, bea], writes=[bVp])
                                S.op("pool", lambda e: e.tensor_copy(out=Vp[:, 128:129], in_=ea_tm[:, k, j:j + 1]), reads=[bea], writes=[bVp])
                                S.op("pe", lambda e: e.matmul(bS_[:, h * 128:(h + 1) * 128], lhsT=KT[:, h, :], rhs=QT[:, h, :], start=True, stop=True), reads=[bKT, bQT], writes=[bbS])
                                Vps.append((Vp, bVp))
                            yield
                            for h in range(4):
                                Sm, bSm = sm_r.next()
                                S.op("dve", lambda e: e.tensor_tensor(out=Sm[:], in0=bS_[:, h * 128:(h + 1) * 128], in1=masks[d][:], op=ALU.mult), reads=[bbS, b_const], writes=[bSm])
                                Sms.append((Sm, bSm))
                            yield
                            for h in range(4):
                                pO, bpO, o = pO_of(h)
                                S.op("pe", lambda e: e.matmul(pO[:, o:o + 129], lhsT=Sms[h][0][:], rhs=Vps[h][0][:, 0:129], start=True, stop=False), reads=[Sms[h][1], Vps[h][1]], writes=[bpO])
                                S.op("pe", lambda e: e.matmul(pO[:, o:o + 129], lhsT=QT[:, h, :], rhs=Cb[d][h][:, 0:129], start=False, stop=True), reads=[bQT, bCb[d][h]], writes=[bpO])
                            yield
                            for h in range(4):
                                pO, bpO, o = pO_of(h)
                                dm, bdm = dm_r.next()
                                S.op("act", lambda e: e.copy(out=dm[:, 0:1], in_=pO[:, o + 128:o + 129]), reads=[bpO], writes=[bdm])
                                dms.append((dm, bdm))
                            yield
                            for h in range(4):
                                j = d * 4 + h
                                dm, bdm = dms[h]
                                S.op("dve", lambda e: e.scalar_tensor_tensor(out=dm[:, 1:2], in0=dm[:, 0:1], scalar=-1.0, in1=dm[:, 0:1], op0=ALU.mult, op1=ALU.max), reads=[bdm], writes=[bdm])
                                S.op("dve", lambda e: e.tensor_tensor(out=dm[:, 1:2], in0=dm[:, 1:2], in1=fl_tm[:, k, j:j + 1], op=ALU.max), reads=[bdm, bfl], writes=[bdm])
                                S.op("dve", lambda e: e.reciprocal(out=dm[:, 1:2], in_=dm[:, 1:2]), reads=[bdm], writes=[bdm])
                            yield
                            for h in range(4):
                                pO, bpO, o = pO_of(h)
                                S.op("act", lambda e: e.activation(out=hst[:, h * 128:(h + 1) * 128], in_=pO[:, o:o + 128], func=AF.Copy, scale=dms[h][0][:, 1:2]), reads=[bpO, dms[h][1]], writes=[bhst])
                            for h in range(4):
                                pC, bpC, o = pC_of(h)
                                S.op("pe", lambda e: e.matmul(pC[:, o:o + 129], lhsT=Ktm[:, h * 128:(h + 1) * 128], rhs=Vps[h][0][:, 0:129], start=True, stop=True), reads=[bKtm, Vps[h][1]], writes=[bpC])
                            yield
                            for h in range(4):
                                pC, bpC, o = pC_of(h)
                                S.op("dve", lambda e: e.tensor_tensor(out=C32[d][h][:], in0=pC[:, o:o + 129], in1=C32[d][h][:], op=ALU.add), reads=[bpC, bC32[d][h]], writes=[bC32[d][h]])
                            yield
                            for h in range(4):
                                j = d * 4 + h
                                S.op("act", lambda e: e.activation(out=Cb[d][h][:], in_=C32[d][h][:], func=AF.Copy, scale=decay_bc[:, j, u:u + 1]), reads=[bC32[d][h], bdec], writes=[bCb[d][h]])
                            yield
                            for h in range(4):
                                j = d * 4 + h
                                S.op("dve", lambda e: e.tensor_scalar(out=C32[d][h][:], in0=C32[d][h][:], scalar1=decay_bc[:, j, u:u + 1], scalar2=None, op0=ALU.mult),
                                     reads=[bC32[d][h], bdec], writes=[bC32[d][h]])
                            dst = hf_d if d == 0 else hb_d
                            bn = "hf" if d == 0 else "hb"
                            S.defer(lambda dst=dst, k=k, hst=hst, bhst=bhst, bn=bn: S.dma("sp", dst[k * 128:(k + 1) * 128, :], hst[:], reads=[bhst], writes=[B_[bn]]))
                            yield
                        S.barrier()
                with ExitStack() as ph:
                    psF = Ring([ps(ph, f"psF{i}", [128, 1024], BF16) for i in range(2)], excl=True)
                    nw_bc, bnw = load_row_bc(ph, "nw_bc", m_norm_w_d[l:l + 1, :], 512)
                    hf_r = Ring([sb(ph, f"fhf{i}", [128, 512], F32) for i in range(3)])
                    hb_r = Ring([sb(ph, f"fhb{i}", [128, 512], F32) for i in range(3)])
                    so_r = Ring([sb(ph, f"fso{i}", [128, 512], F32) for i in range(3)])
                    sq_r = Ring([sb(ph, f"fsq{i}", [128, 512], F32) for i in range(2)])
                    st_r = Ring([sb(ph, f"fst{i}", [128, 16], F32) for i in range(3)])
                    mo_r = Ring([sb(ph, f"fmo{i}", [128, 512], BF16) for i in range(2)])
                    mt_r = Ring([sb(ph, f"fmt{i}", [128, 4, 128], BF16) for i in range(2)])
                    ftiles = list(range(0 if ctx_out else 2, NT))

                    def f_load(k):
                        a, ba = hf_r.next()
                        b, bb = hb_r.next()
                        c, bc = so_r.next()
                        S.dma("sp", a[:], hf_d[k * 128:(k + 1) * 128, :], reads=[B_["hf"]], writes=[ba])
                        S.dma("sp", b[:], hb_d[k * 128:(k + 1) * 128, :], reads=[B_["hb"]], writes=[bb])
                        S.dma("sp", c[:], so_d[k * 128:(k + 1) * 128, :], reads=[B_["so"]], writes=[bc])
                        return (a, ba, b, bb, c, bc)

                    nxt = f_load(ftiles[0])
                    for fi, k in enumerate(ftiles):
                        (a, ba, b, bb, c, bc) = nxt
                        if fi + 1 < len(ftiles):
                            nxt = f_load(ftiles[fi + 1])
                        S.flush()
                        yield
                        st, bst = st_r.next()
                        sq, bsq = sq_r.next()
                        a3 = a[:].rearrange("p (h c) -> p h c", c=128)
                        sq3 = sq[:].rearrange("p (h c) -> p h c", c=128)
                        S.op("dve", lambda e: e.tensor_tensor(out=a[:], in0=a[:], in1=b[:], op=ALU.add), reads=[ba, bb], writes=[ba])
                        S.op("dve", lambda e: e.tensor_reduce(out=st[:, 0:4], in_=a3, axis=AX.X, op=ALU.add), reads=[ba], writes=[bst])
                        S.op("dve", lambda e: e.tensor_scalar(out=st[:, 0:4], in0=st[:, 0:4], scalar1=-1.0 / 128, scalar2=None, op0=ALU.mult), reads=[bst], writes=[bst])
                        S.op("dve", lambda e: e.tensor_tensor(out=a3, in0=a3, in1=st[:, 0:4].rearrange("p (h o) -> p h o", o=1).broadcast_to([128, 4, 128]), op=ALU.add), reads=[ba, bst], writes=[ba])
                        yield
                        S.op("act", lambda e: e.activation(out=sq[:], in_=a[:], func=AF.Square), reads=[ba], writes=[bsq])
                        yield
                        S.op("dve", lambda e: e.tensor_reduce(out=st[:, 4:8], in_=sq3, axis=AX.X, op=ALU.add), reads=[bsq], writes=[bst])
                        S.op("dve", lambda e: e.tensor_scalar(out=st[:, 4:8], in0=st[:, 4:8], scalar1=1.0 / 128, scalar2=1e-6, op0=ALU.mult, op1=ALU.add), reads=[bst], writes=[bst])
                        yield
                        S.op("act", lambda e: e.activation(out=st[:, 4:8], in_=st[:, 4:8], func=AF.Sqrt), reads=[bst], writes=[bst])
                        yield
                        S.op("dve", lambda e: e.reciprocal(out=st[:, 4:8], in_=st[:, 4:8]), reads=[bst], writes=[bst])
                        S.op("dve", lambda e: e.tensor_tensor(out=a3, in0=a3, in1=st[:, 4:8].rearrange("p (h o) -> p h o", o=1).broadcast_to([128, 4, 128]), op=ALU.mult), reads=[ba, bst], writes=[ba])
                        S.op("pool", lambda e: e.tensor_tensor(out=c[:], in0=c[:], in1=nw_bc[:], op=ALU.mult), reads=[bc, bnw], writes=[bc])
                        mo, bmo = mo_r.next()
                        S.op("dve", lambda e: e.tensor_tensor(out=mo[:], in0=a[:], in1=c[:], op=ALU.mult), reads=[ba, bc], writes=[bmo])
                        yield
                        pt, pb = psF.next()
                        for cch in range(4):
                            S.op("pe", lambda e: e.transpose(out=pt[:, cch * 128:(cch + 1) * 128], in_=mo[:, cch * 128:(cch + 1) * 128], identity=identb[:]), reads=[bmo, b_const], writes=[pb])
                        yield
                        mt, bmt = mt_r.next()
                        S.op("act", lambda e: e.copy(out=mt[:], in_=pt[:, 0:512].rearrange("p (c t) -> p c t", t=128)), reads=[pb], writes=[bmt])
                        S.defer(lambda k=k, mt=mt, bmt=bmt: S.dma("sp", moT_d.rearrange("(c p) s -> p c s", p=128)[:, :, k * 128:(k + 1) * 128], mt[:], reads=[bmt], writes=[B_["moT"]]))
                    S.barrier()
            with ExitStack() as ph:
                psS = Ring([ps(ph, f"psS{i}", [128, 512], F32) for i in range(2)], excl=True)
                psO = Ring([ps(ph, f"psO{i}", [128, 512], F32) for i in range(1)], excl=True)
                psL = Ring([ps(ph, f"psL{i}", [128, 512], F32) for i in range(1)], excl=True)
                for e_ in range(NE):
                    for (src, dst) in ((w_gate_d, wgb_d), (w_up_d, wub_d), (w_down_d, wdb_d)):
                        S.dma("pool", dst[l, e_], src[l, e_], writes=[B_["wcast"]])
                krT = sb(ph, "g_krT", [64, S_], BF16)
                bkr = Buf()
                S.dma("sp", krT[:], krT_d, reads=[B_["krT"]], writes=[bkr])
                kn_r = Ring([sb(ph, f"g_kn{i}", [128, S_], BF16) for i in range(1)])
                va_r = Ring([sb(ph, f"g_va{i}", [128, NT, 128], BF16) for i in range(1)])
                qn_r = Ring([sb(ph, f"g_qn{i}", [128, 512], BF16) for i in range(3)])
                qr_r = Ring([sb(ph, f"g_qr{i}", [64, 512], BF16) for i in range(3)])
                pT_r = Ring([sb(ph, f"g_pT{i}", [128, 512], BF16) for i in range(4)])
                rec_r = Ring([sb(ph, f"g_rec{i}", [128, 512], F32) for i in range(2)])
                ao_r = Ring([sb(ph, f"g_ao{i}", [128, 512], BF16) for i in range(2)])
                qblocks = [(CTX + 512 * i, 512, 0, NT) for i in range(T // 512)]
                if ctx_out:
                    qblocks = [(0, CTX, 0, 2)] + qblocks
                work = [(h, qb) for h in range(4) for qb in qblocks]

                def g_loadh(h):
                    kn, bkn = kn_r.next()
                    va, bva = va_r.next()
                    S.dma("sp", kn[:], knT_d[h], reads=[B_["knT"]], writes=[bkn])
                    S.dma("sp", va[:], va_d[:, h * 128:(h + 1) * 128].rearrange("(t p) c -> p t c", p=128), reads=[B_["va"]], writes=[bva])
                    return (kn, bkn, va, bva)

                def g_loadq(h, qb):
                    t0, nb, k0, k1 = qb
                    qn, bqn = qn_r.next()
                    qr, bqr = qr_r.next()
                    S.dma("sp", qn[:, :nb], qT_d[h, 0:128, t0:t0 + nb], reads=[B_["qT"]], writes=[bqn])
                    S.dma("sp", qr[:, :nb], qT_d[h, 128:192, t0:t0 + nb], reads=[B_["qT"]], writes=[bqr])
                    return (qn, bqn, qr, bqr)

                hl = {}
                nq = g_loadq(*work[0])
                gen = xstream()
                gstep = [0]
                for wi, (h, qb) in enumerate(work):
                    t0, nb, k0, k1 = qb
                    (qn, bqn, qr, bqr) = nq
                    if wi + 1 < len(work):
                        nq = g_loadq(*work[wi + 1])
                    if h not in hl:
                        hl[h] = g_loadh(h)
                    S.flush()
                    (kn, bkn, va, bva) = hl[h]
                    pO, bpO = psO.next()
                    pL, bpL = psL.next()
                    def g_qk(kt):
                        pS, bpS = psS.next()
                        S.op("pe", lambda e: e.matmul(pS[:, :nb], lhsT=kn[:, kt * 128:(kt + 1) * 128], rhs=qn[:, :nb], start=True, stop=False), reads=[bkn, bqn], writes=[bpS])
                        S.op("pe", lambda e: e.matmul(pS[:, :nb], lhsT=krT[:, kt * 128:(kt + 1) * 128], rhs=qr[:, :nb], start=False, stop=True), reads=[bkr, bqr], writes=[bpS])
                        return (pS, bpS)

                    cur = g_qk(k0)
                    for kt in range(k0, k1):
                        pS, bpS = cur
                        if kt + 1 < k1:
                            cur = g_qk(kt + 1)
                        gstep[0] += 1
                        if gstep[0] % 2 == 0:
                            next(gen, None)
                        pT, bpT = pT_r.next()
                        S.op("act", lambda e: e.activation(out=pT[:, :nb], in_=pS[:, :nb], func=AF.Exp), reads=[bpS], writes=[bpT])
                        S.op("pe", lambda e: e.matmul(pO[:, :nb], lhsT=va[:, kt, :], rhs=pT[:, :nb], start=(kt == k0), stop=(kt == k1 - 1)), reads=[bva, bpT], writes=[bpO])
                        S.op("pe", lambda e: e.matmul(pL[:, :nb], lhsT=onesb[:], rhs=pT[:, :nb], start=(kt == k0), stop=(kt == k1 - 1)), reads=[b_const, bpT], writes=[bpL])
                    rec, brec = rec_r.next()
                    ao, bao = ao_r.next()
                    S.op("dve", lambda e: e.reciprocal(out=rec[:, :nb], in_=pL[:, :nb]), reads=[bpL], writes=[brec])
                    S.op("dve", lambda e: e.tensor_tensor(out=ao[:, :nb], in0=pO[:, :nb], in1=rec[:, :nb], op=ALU.mult), reads=[bpO, brec], writes=[bao])
                    S.defer(lambda h=h, t0=t0, nb=nb, ao=ao, bao=bao: S.dma("sp", aoT_d[h * 128:(h + 1) * 128, t0:t0 + nb], ao[:, :nb], reads=[bao], writes=[B_["aoT"]]))
                for _ in gen:
                    pass
                S.barrier()
            if stop_after == "G":
                break

            hblocks = [b for b in blocks if ctx_out or b[0] != 0]
            with ExitStack() as phI:
                wgt_tm = sb(phI, "wgt_tm", [128, NT, NE], F32)
                bwgt = Buf()
                with ExitStack() as ph:
                    psH = Ring([ps(ph, f"psH{i}", [128, 512], F32) for i in range(8)], excl=True)
                    w_out_b = sb(ph, "w_out_b", [128, 8, 1024], BF16)
                    bwo = Buf()
                    S.dma("pool", w_out_b[:], w_out_d[l].rearrange("(c p) n -> p c n", p=128), writes=[bwo])
                    wr32 = sb(ph, "wr32", [128, 8, NE], F32)
                    bwr = Buf()
                    S.dma("sp", wr32[:], w_router_d[l].rearrange("(c p) n -> p c n", p=128), writes=[bwr])
                    g2x, bg2x = modrow_bc(ph, "g2x", 0, 2)
                    ln1g, bl1g = load_row_bc(ph, "ln1g", ln1_g_d[l:l + 1, :], D)
                    ln1b, bl1b = load_row_bc(ph, "ln1b", ln1_b_d[l:l + 1, :], D)
                    sh3x, bsh3x = modcol(ph, "sh3x", 0, 3)
                    sc4x, bsc4x = modcol(ph, "sc4x", 0, 4, True)
                    sh3r, bsh3r = modrow_bc(ph, "sh3r", 0, 3)
                    sc4r, bsc4r = modrow_bc(ph, "sc4r", 0, 4)
                    S.op("dve", lambda e: e.tensor_scalar(out=sc4r[:], in0=sc4r[:], scalar1=1.0, scalar2=None, op0=ALU.add), reads=[bsc4r], writes=[bsc4r])
                    h2tm_r = Ring([sb(ph, f"h_h2tm{i}", [128, D], BF16) for i in range(4)])
                    if ctx_out:
                        g2c, bg2c = modrow_bc(ph, "g2c", 1, 2)
                        sh3c, bsh3c = modcol(ph, "sh3c", 1, 3)
                        sc4c, bsc4c = modcol(ph, "sc4c", 1, 4, True)
                    mo_r = Ring([sb(ph, f"h_mo{i}", [128, 4, 512], BF16) for i in range(2)])
                    ao_r = Ring([sb(ph, f"h_ao{i}", [128, 4, 512], BF16) for i in range(2)])
                    x_r = Ring([sb(ph, f"h_x{i}", [128, D], F32) for i in range(8)])
                    tmp_r = Ring([sb(ph, f"h_tmp{i}", [128, D], F32) for i in range(3)])
                    xn2_r = Ring([sb(ph, f"h_xn2{i}", [128, D], F32) for i in range(8)])
                    stm_r = Ring([sb(ph, f"h_stm{i}", [128, 5, 4], F32) for i in range(6)])
                    junk = sb(ph, "h_junk", [128, D], BF16)
                    bjunk = Buf()
                    h32_r = Ring([sb(ph, f"h2T32{i}", [128, 8, 512], F32) for i in range(2)])
                    hb_r = Ring([sb(ph, f"h2Tb{i}", [128, 8, 512], BF16) for i in range(2)])
                    sm_r = Ring([sb(ph, f"h_sm{i}", [128, 24], F32) for i in range(8)])
                    affTs_r = Ring([sb(ph, f"affTs{i}", [NE, 512], F32) for i in range(2)])
                    print("phase H sbuf remaining", nc.sbuf_bytes_remaining)

                    def h_load(bi):
                        k0, ntile = hblocks[bi]
                        nb = ntile * 128
                        t0 = k0 * 128
                        mo, bmo = mo_r.next()
                        ao, bao = ao_r.next()
                        S.dma("sp", mo[:, :, :nb], moT_d.rearrange("(c p) s -> p c s", p=128)[:, :, t0:t0 + nb], reads=[B_["moT"]], writes=[bmo])
                        S.dma("sp", ao[:, :, :nb], aoT_d.rearrange("(c p) s -> p c s", p=128)[:, :, t0:t0 + nb], reads=[B_["aoT"]], writes=[bao])
                        xs = []
                        for ti in range(ntile):
                            xt, bx = x_r.next()
                            S.dma("sp", xt[:], tile_src(l, k0 + ti), reads=[B_["x2"]], writes=[bx])
                            xs.append((xt, bx))
                        return (mo, bmo, ao, bao, xs)

                    def h_block(bi, ld):
                        k0, ntile = hblocks[bi]
                        (mo, bmo, ao, bao, xs) = ld
                        h2T32, bh32 = h32_r.next()
                        h2Tb, bhb = hb_r.next()
                        affTs, baffTs = affTs_r.next()
                        is_ctx = (k0 == 0)
                        nb = ntile * 128
                        t0 = k0 * 128
                        g2, bg2 = (g2c, bg2c) if is_ctx else (g2x, bg2x)
                        sh3, bsh3 = (sh3c, bsh3c) if is_ctx else (sh3x, bsh3x)
                        sc4, bsc4 = (sc4c, bsc4c) if is_ctx else (sc4x, bsc4x)
                        xn2s = []
                        for ti in range(ntile):
                            xt, bx = xs[ti]
                            tmp, btmp = tmp_r.next()
                            for half in range(2):
                                pt, pb = psH.next()
                                for c in range(8):
                                    src, bsrc = (mo, bmo) if c < 4 else (ao, bao)
                                    S.op("pe", lambda e: e.matmul(pt[:, :], lhsT=src[:, c % 4, ti * 128:(ti + 1) * 128], rhs=w_out_b[:, c, half * 512:(half + 1) * 512], start=(c == 0), stop=(c == 7)),
                                         reads=[bsrc, bwo], writes=[pb])
                                S.op("dve", lambda e: e.tensor_tensor(out=tmp[:, half * 512:(half + 1) * 512], in0=pt[:, :], in1=g2[:, half * 512:(half + 1) * 512], op=ALU.mult),
                                     reads=[pb, bg2], writes=[btmp])
                            S.op("dve", lambda e: e.scalar_tensor_tensor(out=xt[:], in0=xt[:], scalar=ALPHA, in1=tmp[:], op0=ALU.mult, op1=ALU.add), reads=[bx, btmp], writes=[bx])
                        yield
                        stm, bstm = stm_r.next()
                        yield from ln_stats_multi_g([(xs[ti][0][:], xs[ti][1]) for ti in range(ntile)], 1e-5, stm, bstm, junk[:], bjunk)
                        yield
                        for ti in range(ntile):
                            k = k0 + ti
                            xt, bx = xs[ti]
                            S.op("act", lambda e: e.activation(out=xt[:], in_=xt[:], func=AF.Identity, bias=stm[:, 2, ti:ti + 1], scale=1.0), reads=[bx, bstm], writes=[bx])
                            S.op("dve", lambda e: e.scalar_tensor_tensor(out=xt[:], in0=xt[:], scalar=stm[:, 4, ti:ti + 1], in1=ln1g[:], op0=ALU.mult, op1=ALU.mult), reads=[bx, bstm, bl1g], writes=[bx])
                            S.op("dve", lambda e: e.tensor_tensor(out=xt[:], in0=xt[:], in1=ln1b[:], op=ALU.add), reads=[bx, bl1b], writes=[bx])
                            S.defer(lambda k=k, xt=xt, bx=bx: S.dma("sp", x1_d[k * 128:(k + 1) * 128, :], xt[:], reads=[bx], writes=[B_["x1"]]))
                        yield
                        stm2, bstm2 = stm_r.next()
                        yield from ln_stats_multi_g([(xs[ti][0][:], xs[ti][1]) for ti in range(ntile)], 1e-6, stm2, bstm2, junk[:], bjunk)
                        yield
                        for ti in range(ntile):
                            k = k0 + ti
                            xt, bx = xs[ti]
                            xn2, bxn2 = xn2_r.next()
                            S.op("dve", lambda e: e.tensor_scalar(out=xn2[:], in0=xt[:], scalar1=stm2[:, 2, ti:ti + 1], scalar2=stm2[:, 4, ti:ti + 1], op0=ALU.add, op1=ALU.mult), reads=[bx, bstm2], writes=[bxn2])
                            xn2s.append((xn2, bxn2))
                            if not is_ctx:
                                tmp2, btmp2 = tmp_r.next()
                                h2tm, bh2tm = h2tm_r.next()
                                S.op("dve", lambda e: e.tensor_tensor(out=tmp2[:], in0=xn2[:], in1=sc4r[:], op=ALU.mult), reads=[bxn2, bsc4r], writes=[btmp2])
                                S.op("dve", lambda e: e.tensor_tensor(out=h2tm[:], in0=tmp2[:], in1=sh3r[:], op=ALU.add), reads=[btmp2, bsh3r], writes=[bh2tm])
                                S.dma("sp", h2tm_d[k * 128:(k + 1) * 128, :], h2tm[:], reads=[bh2tm], writes=[B_["h2tm"]])
                        yield
                        for kc in range(8):
                            pt, pb = psH.next()
                            for ti in range(ntile):
                                xn2, bxn2 = xn2s[ti]
                                S.op("pe", lambda e: e.transpose(out=pt[:, ti * 128:(ti + 1) * 128], in_=xn2[:, kc * 128:(kc + 1) * 128], identity=ident[:]), reads=[bxn2, b_const], writes=[pb])
                            S.op("act", lambda e: e.activation(out=h2T32[:, kc, :nb], in_=pt[:, :nb], func=AF.Identity, bias=sh3[:, kc:kc + 1], scale=sc4[:, kc:kc + 1]),
                                 reads=[pb, bsh3, bsc4], writes=[bh32])
                        yield
                        S.op("pool", lambda e: e.tensor_copy(out=h2Tb[:, :, :nb], in_=h2T32[:, :, :nb]), reads=[bh32], writes=[bhb])
                        S.dma("sp", h2T_d.rearrange("(c p) s -> p c s", p=128)[:, :, t0:t0 + nb], h2Tb[:, :, :nb], reads=[bhb], writes=[B_["h2T"]])
                        for ti in range(ntile):
                            pt, pb = psH.next()
                            for kc in range(8):
                                S.op("pe", lambda e: e.matmul(pt[:, 0:NE], lhsT=h2T32[:, kc, ti * 128:(ti + 1) * 128], rhs=wr32[:, kc, :], start=(kc == 0), stop=(kc == 7)), reads=[bh32, bwr], writes=[pb])
                            sm, bsm = sm_r.next()
                            S.op("dve", lambda e: e.tensor_reduce(out=sm[:, 16:17], in_=pt[:, 0:NE], axis=AX.X, op=ALU.max), reads=[pb], writes=[bsm])
                            S.op("dve", lambda e: e.tensor_scalar(out=sm[:, 16:17], in0=sm[:, 16:17], scalar1=-1.0, scalar2=None, op0=ALU.mult), reads=[bsm], writes=[bsm])
                            S.op("act", lambda e: e.activation(out=sm[:, 0:NE], in_=pt[:, 0:NE], func=AF.Exp, bias=sm[:, 16:17], accum_out=sm[:, 17:18]), reads=[pb, bsm], writes=[bsm])
                            S.op("dve", lambda e: e.reciprocal(out=sm[:, 17:18], in_=sm[:, 17:18]), reads=[bsm], writes=[bsm])
                            S.op("dve", lambda e: e.tensor_scalar(out=sm[:, 0:NE], in0=sm[:, 0:NE], scalar1=sm[:, 17:18], scalar2=None, op0=ALU.mult), reads=[bsm], writes=[bsm])
                            S.dma("sp", afftm_d[(k0 + ti) * 128:(k0 + ti + 1) * 128, :], sm[:, 0:NE], reads=[bsm], writes=[B_["afftm"]])
                            pt2, pb2 = psH.next()
                            S.op("pe", lambda e: e.transpose(out=pt2[0:NE, 0:128], in_=sm[:, 0:NE], identity=ident[:]), reads=[bsm, b_const], writes=[pb2])
                            S.op("act", lambda e: e.copy(out=affTs[:, ti * 128:(ti + 1) * 128], in_=pt2[0:NE, 0:128]), reads=[pb2], writes=[baffTs])
                        S.dma("sp", affT_d[:, t0:t0 + nb], affTs[:, :nb], reads=[baffTs], writes=[B_["affT"]])

                    hl_ = {0: h_load(0)}
                    active = []
                    nstart = [0]

                    def h_start():
                        bi = nstart[0]
                        if bi >= len(hblocks):
                            return
                        S.flush()
                        if bi not in hl_:
                            hl_[bi] = h_load(bi)
                        active.append(h_block(bi, hl_.pop(bi)))
                        nstart[0] += 1

                    h_start()
                    rounds = 0
                    while active:
                        rounds += 1
                        if len(active) < 2 and rounds % 4 == 0:
                            h_start()
                        for g in list(active):
                            try:
                                next(g)
                            except StopIteration:
                                active.remove(g)
                                h_start()
                    S.barrier()
                if stop_after == "H":
                    break
                with ExitStack() as ph:
                    psI = Ring([ps(ph, f"psI{i}", [128, 512], F32) for i in range(2)], excl=True)
                    affT = sb(ph, "i_affT", [NE, S_], F32)
                    wT = sb(ph, "i_wT", [NE, S_], F32)
                    junkI = sb(ph, "i_junk", [NE, T], F16)
                    baff, bwT, bjk = Buf(), Buf(), Buf()
                    S.dma("sp", affT[:], affT_d, reads=[B_["affT"]], writes=[baff])
                    if not ctx_out:
                        S.op("dve", lambda e: e.memset(wT[:, 0:CTX], 0.0), writes=[bwT])
                    sets = [(CTX, S_, T // 8)]
                    if ctx_out:
                        sets = [(0, CTX, CTX // 8)] + sets
                    for (a, b, kcap) in sets:
                        n = b - a
                        cs = sb(ph, f"i_cs{a}", [NE, 8], F32)
                        bcs = Buf()
                        S.op("dve", lambda e: e.memset(cs[:, 0:1], 0.0), writes=[bcs])
                        S.op("dve", lambda e: e.memset(cs[:, 1:2], 1.0), writes=[bcs])
                        for it in range(32):
                            S.op("dve", lambda e: e.tensor_scalar(out=cs[:, 2:3], in0=cs[:, 0:1], scalar1=0.5, scalar2=None, op0=ALU.mult), reads=[bcs], writes=[bcs])
                            S.op("dve", lambda e: e.scalar_tensor_tensor(out=cs[:, 2:3], in0=cs[:, 1:2], scalar=0.5, in1=cs[:, 2:3], op0=ALU.mult, op1=ALU.add), reads=[bcs], writes=[bcs])
                            S.op("dve", lambda e: e.memset(cs[:, 3:4], 0.0), writes=[bcs])
                            S.op("dve", lambda e: e.tensor_scalar(out=junkI[:, :n], in0=affT[:, a:b], scalar1=cs[:, 2:3], scalar2=0.0, op0=ALU.is_ge, op1=ALU.add, accum_out=cs[:, 3:4]),
                                 reads=[baff, bcs], writes=[bjk, bcs])
                            S.op("dve", lambda e: e.tensor_scalar(out=cs[:, 4:5], in0=cs[:, 3:4], scalar1=float(kcap) - 0.5, scalar2=None, op0=ALU.is_ge), reads=[bcs], writes=[bcs])
                            S.op("dve", lambda e: e.tensor_tensor(out=cs[:, 5:6], in0=cs[:, 2:3], in1=cs[:, 0:1], op=ALU.subtract), reads=[bcs], writes=[bcs])
                            S.op("dve", lambda e: e.scalar_tensor_tensor(out=cs[:, 0:1], in0=cs[:, 5:6], scalar=cs[:, 4:5], in1=cs[:, 0:1], op0=ALU.mult, op1=ALU.add), reads=[bcs], writes=[bcs])
                            S.op("dve", lambda e: e.tensor_tensor(out=cs[:, 5:6], in0=cs[:, 1:2], in1=cs[:, 2:3], op=ALU.subtract), reads=[bcs], writes=[bcs])
                            S.op("dve", lambda e: e.scalar_tensor_tensor(out=cs[:, 1:2], in0=cs[:, 5:6], scalar=cs[:, 4:5], in1=cs[:, 2:3], op0=ALU.mult, op1=ALU.add), reads=[bcs], writes=[bcs])
                        if a == 0:
                            S.op("dve", lambda e: e.scalar_tensor_tensor(out=wT[:, a:b], in0=affT[:, a:b], scalar=cs[:, 0:1], in1=affT[:, a:b], op0=ALU.is_ge, op1=ALU.mult), reads=[baff, bcs], writes=[bwT])
                        else:
                            S.op("dve", lambda e: e.tensor_scalar(out=wT[:, a:b], in0=affT[:, a:b], scalar1=cs[:, 0:1], scalar2=None, op0=ALU.is_ge), reads=[baff, bcs], writes=[bwT])
                            S.op("dve", lambda e: e.tensor_tensor_scan(out=affT[:, a:b], data0=wT[:, a:b], data1=zcol[0:NE, 0:1].broadcast_to([NE, n]), initial=0.0, op0=ALU.add, op1=ALU.add),
                                 reads=[bwT, b_const, baff], writes=[baff])
                            S.op("dve", lambda e: e.tensor_copy(out=junkI[:, :n], in_=affT[:, a:b]), reads=[baff, bjk], writes=[bjk])
                            S.dma("sp", cnt_d, junkI[:, :n], reads=[bjk], writes=[B_["cnt"]])
                            tcs = sb(ph, "i_tcs", [NE, NX], F32)
                            btcs = Buf()
                            S.op("dve", lambda e: e.tensor_copy(out=tcs[:].rearrange("p (t o) -> p t o", o=1), in_=affT[:, a:b].rearrange("p (t c) -> p t c", c=128)[:, :, 127:128]),
                                 reads=[baff], writes=[btcs])
                            S.dma("sp", tc_d.rearrange("o (e k) -> (o e) k", e=NE), tcs[:], reads=[btcs], writes=[B_["cnt"]])
                    for k in range(2 if ctx_out else 0):
                        pt, pb = psI.next()
                        S.op("pe", lambda e: e.transpose(out=pt[:, 0:NE], in_=wT[0:NE, k * 128:(k + 1) * 128], identity=ident[0:NE, 0:NE]), reads=[bwT, b_const], writes=[pb])
                        S.op("act", lambda e: e.copy(out=wgt_tm[:, k, :], in_=pt[:, 0:NE]), reads=[pb], writes=[bwgt])
                    S.barrier()
                if stop_after == "I":
                    break
                with ExitStack() as ph:
                    psG = Ring([ps(ph, f"psG{i}", [128, 512], F32) for i in range(3)], excl=True)
                    psY = Ring([ps(ph, f"psY{i}", [128, 512], F32) for i in range(3)], excl=True)
                    psT = Ring([ps(ph, f"psT{i}", [128, 1024], BF16) for i in range(2)], excl=True)
                    w_r = Ring([sb(ph, f"m_w{i}", [128, 8, 1024], BF16) for i in range(5)])
                    act_r = Ring([sb(ph, f"m_act{i}", [128, 8, 512], BF16) for i in range(2)])
                    sg_r = Ring([sb(ph, f"m_sg{i}", [128, 512], F32) for i in range(3)])
                    zt = sb(ph, "m_zt", [128, D], F32)
                    bzt = Buf()
                    S.op("dve", lambda e: e.memset(zt[:], 0.0), writes=[bzt])
                    S.dma("sp", moe_d[CTX:S_, :].rearrange("(t p) d -> p t d", p=128), zt[:].rearrange("p (o d) -> p o d", o=1).broadcast_to([128, NX, D]), reads=[bzt], writes=[B_["moe"]])
                    wsrc = (wgb_d, wub_d, wdb_d)
                    nblk = 2 if ctx_out else 1
                    seq = [(bi, e_, m) for bi in range(nblk) for e_ in range(NE) for m in range(3)]
                    loaded = {}
                    nload = [0]

                    def w_load(i):
                        while nload[0] <= i and nload[0] < len(seq):
                            j = nload[0]
                            bi_, e2, m = seq[j]
                            wt, wb_ = w_r.next()
                            S.dma("sp", wt[:], wsrc[m][l, e2].rearrange("(c p) n -> p c n", p=128), reads=[B_["wcast"]], writes=[wb_])
                            loaded[j] = (wt, wb_)
                            nload[0] += 1

                    def ffn_gate_up(wg, bwg, wu, bwu, h2, bh2, hs, n):
                        act, bact = act_r.next()
                        for fc in range(8):
                            pg, bpg = psG.next()
                            for kc in range(8):
                                S.op("pe", lambda e: e.matmul(pg[:, :n], lhsT=wg[:, kc, fc * 128:(fc + 1) * 128], rhs=h2[:, kc, hs:hs + n], start=(kc == 0), stop=(kc == 7)), reads=[bwg, bh2], writes=[bpg])
                            pu, bpu = psG.next()
                            for kc in range(8):
                                S.op("pe", lambda e: e.matmul(pu[:, :n], lhsT=wu[:, kc, fc * 128:(fc + 1) * 128], rhs=h2[:, kc, hs:hs + n], start=(kc == 0), stop=(kc == 7)), reads=[bwu, bh2], writes=[bpu])
                            sg, bsg = sg_r.next()
                            S.op("act", lambda e: e.activation(out=sg[:, :n], in_=pg[:, :n], func=AF.Silu), reads=[bpg], writes=[bsg])
                            S.op("dve", lambda e: e.tensor_tensor(out=act[:, fc, :n], in0=pu[:, :n], in1=sg[:, :n], op=ALU.mult), reads=[bpu, bsg], writes=[bact])
                        return act, bact

                    w_load(2)
                    si = 0
                    if ctx_out:
                        with ExitStack() as phc:
                            h2c = sb(phc, "m_h2c", [128, 8, CTX], BF16)
                            accc = sb(phc, "m_accc", [128, 2, D], F32)
                            bh2c, baccc = Buf(), Buf()
                            S.dma("sp", h2c[:], h2T_d.rearrange("(c p) s -> p c s", p=128)[:, :, 0:CTX], reads=[B_["h2T"]], writes=[bh2c])
                            for e_ in range(NE):
                                w_load(si + 2)
                                wg, bwg = loaded.pop(si)
                                wu, bwu = loaded.pop(si + 1)
                                wd, bwd = loaded.pop(si + 2)
                                w_load(si + 3)
                                act, bact = ffn_gate_up(wg, bwg, wu, bwu, h2c, bh2c, 0, CTX)
                                w_load(si + 5)
                                for tl in range(2):
                                    for oh in range(2):
                                        py, bpy = psY.next()
                                        for fc in range(8):
                                            S.op("pe", lambda e: e.matmul(py[:, :], lhsT=act[:, fc, tl * 128:(tl + 1) * 128], rhs=wd[:, fc, oh * 512:(oh + 1) * 512], start=(fc == 0), stop=(fc == 7)), reads=[bact, bwd], writes=[bpy])
                                        if e_ == 0:
                                            S.op("dve", lambda e: e.tensor_scalar(out=accc[:, tl, oh * 512:(oh + 1) * 512], in0=py[:, :], scalar1=wgt_tm[:, tl, e_:e_ + 1], scalar2=None, op0=ALU.mult),
                                                 reads=[bpy, bwgt], writes=[baccc])
                                        else:
                                            S.op("dve", lambda e: e.scalar_tensor_tensor(out=accc[:, tl, oh * 512:(oh + 1) * 512], in0=py[:, :], scalar=wgt_tm[:, tl, e_:e_ + 1], in1=accc[:, tl, oh * 512:(oh + 1) * 512], op0=ALU.mult, op1=ALU.add),
                                                 reads=[bpy, bwgt, baccc], writes=[baccc])
                                si += 3
                            S.dma("sp", moe_d[0:CTX, :].rearrange("(t p) d -> p t d", p=128), accc[:], reads=[baccc], writes=[B_["moe"]])
                            S.barrier()
                    tcb = sb(ph, "m_tcb", [128, NE, NX], F32)
                    btcb = Buf()
                    S.dma("sp", tcb[:].rearrange("p e k -> p (e k)"), tc_d.broadcast_to([128, NE * NX]), reads=[B_["cnt"]], writes=[btcb])
                    junkM = sb(ph, "m_junk", [128, 128], F32)
                    bjm = Buf()
                    kf = sb(ph, "m_kf", [128, NE, 8], F32)
                    rowf = sb(ph, "m_rowf", [128, NE, 8], F32)
                    rowi = sb(ph, "m_rowi", [128, NE, 8], I32)
                    posf = sb(ph, "m_posf", [128, NE, 8], F32)
                    ct_r = Ring([sb(ph, f"m_ct{i}", [128, 128], F16) for i in range(8)])
                    cnt2d = cnt_d.rearrange("e (k c) -> (e k) c", c=128)
                    idxf = sb(ph, "m_idxf", [128, NE, 8], F32)
                    idxi = sb(ph, "m_idxi", [128, NE, 8], I32)
                    bidx = [Buf() for _ in range(NE)]
                    gts = sb(ph, "m_gts", [128, NE, 8, NE], F32)
                    bgts = [Buf() for _ in range(NE)]
                    xg_r = Ring([sb(ph, f"m_xg{i}", [128, 4, D], BF16) for i in range(3)])
                    XgT_r = Ring([sb(ph, f"m_XgT{i}", [128, 8, 512], BF16) for i in range(2)])
                    ys_r = Ring([sb(ph, f"m_ys{i}", [128, D], F32) for i in range(4)])
                    print("phase M sbuf remaining", nc.sbuf_bytes_remaining)
                    NSL = T // 8 // 128
                    HPE = max(1, NSL // 4)
                    SPH = NSL // HPE
                    units = [(e_, hh) for e_ in range(NE) for hh in range(HPE)]
                    def stageA(u):
                        e_, hh = units[u]
                        if hh == 0:
                            be = bidx[e_]
                            for j in range(NSL):
                                S.op("dve", lambda e: e.memset(kf[:, e_, j:j + 1], 0.0), writes=[be])
                                S.op("dve", lambda e: e.tensor_scalar(out=junkM[:, 0:NX], in0=tcb[:, e_, :], scalar1=slotf[:, j:j + 1], scalar2=0.0, op0=ALU.is_le, op1=ALU.add, accum_out=kf[:, e_, j:j + 1]),
                                     reads=[btcb, b_const, bjm], writes=[bjm, be])
                            S.op("dve", lambda e: e.tensor_scalar(out=rowf[:, e_, 0:NSL], in0=kf[:, e_, 0:NSL], scalar1=float(e_ * NX), scalar2=None, op0=ALU.add), reads=[be], writes=[be])
                            S.op("dve", lambda e: e.tensor_copy(out=rowi[:, e_, 0:NSL], in_=rowf[:, e_, 0:NSL]), reads=[be], writes=[be])
                            for j in range(NSL):
                                ct, bct = ct_r.next()
                                S.idma(out=ct[:], out_offset=None, in_=cnt2d, in_offset=bass.IndirectOffsetOnAxis(ap=rowi[:, e_, j:j + 1], axis=0), reads=[be, B_["cnt"]], writes=[bct])
                                S.op("dve", lambda e: e.memset(posf[:, e_, j:j + 1], 0.0), writes=[be])
                                S.op("dve", lambda e: e.tensor_scalar(out=junkM[:, 0:128], in0=ct[:], scalar1=slotf[:, j:j + 1], scalar2=0.0, op0=ALU.is_le, op1=ALU.add, accum_out=posf[:, e_, j:j + 1]),
                                     reads=[bct, b_const, bjm], writes=[bjm, be])
                            S.op("dve", lambda e: e.scalar_tensor_tensor(out=idxf[:, e_, 0:NSL], in0=kf[:, e_, 0:NSL], scalar=128.0, in1=posf[:, e_, 0:NSL], op0=ALU.mult, op1=ALU.add), reads=[be], writes=[be])
                            S.op("dve", lambda e: e.tensor_scalar(out=idxf[:, e_, 0:NSL], in0=idxf[:, e_, 0:NSL], scalar1=float(CTX), scalar2=None, op0=ALU.add), reads=[be], writes=[be])
                            S.op("dve", lambda e: e.tensor_copy(out=idxi[:, e_, 0:NSL], in_=idxf[:, e_, 0:NSL]), reads=[be], writes=[be])
                        xg, bxg = xg_r.next()
                        for jj in range(SPH):
                            j = hh * SPH + jj
                            S.idma(out=xg[:, jj, :], out_offset=None, in_=h2tm_d[:, :], in_offset=bass.IndirectOffsetOnAxis(ap=idxi[:, e_, j:j + 1], axis=0),
                                   reads=[bidx[e_], B_["h2tm"]], writes=[bxg])
                            S.idma(out=gts[:, e_, j, :], out_offset=None, in_=afftm_d[:, :], in_offset=bass.IndirectOffsetOnAxis(ap=idxi[:, e_, j:j + 1], axis=0),
                                   reads=[bidx[e_], B_["afftm"]], writes=[bgts[e_]])
                        return (xg, bxg)

                    def stageB(u, xgt):
                        e_, hh = units[u]
                        xg, bxg = xgt
                        XgT, bXgT = XgT_r.next()
                        for kc in range(8):
                            pt, pb = psT.next()
                            for jj in range(SPH):
                                S.op("pe", lambda e: e.transpose(out=pt[:, jj * 128:(jj + 1) * 128], in_=xg[:, jj, kc * 128:(kc + 1) * 128], identity=identb[:]), reads=[bxg, b_const], writes=[pb])
                            if kc % 2:
                                S.op("act", lambda e: e.copy(out=XgT[:, kc, 0:SPH * 128], in_=pt[:, 0:SPH * 128]), reads=[pb], writes=[bXgT])
                            else:
                                S.op("dve", lambda e: e.tensor_copy(out=XgT[:, kc, 0:SPH * 128], in_=pt[:, 0:SPH * 128]), reads=[pb], writes=[bXgT])
                        return (XgT, bXgT)

                    wcur = {}
                    sc_prev = [B_["moe"].last_w]
                    sc_cur = []

                    def stageC(u, XgTt):
                        nonlocal si
                        e_, hh = units[u]
                        XgT, bXgT = XgTt
                        if hh == 0:
                            w_load(si + 2)
                            wcur["g"] = loaded.pop(si)
                            wcur["u"] = loaded.pop(si + 1)
                            wcur["d"] = loaded.pop(si + 2)
                            w_load(si + 4)
                        wg, bwg = wcur["g"]
                        wu, bwu = wcur["u"]
                        wd, bwd = wcur["d"]
                        n = SPH * 128
                        act, bact = ffn_gate_up(wg, bwg, wu, bwu, XgT, bXgT, 0, n)
                        if hh == HPE - 1:
                            w_load(si + 5)
                        for jj in range(SPH):
                            j = hh * SPH + jj
                            ys, bys = ys_r.next()
                            for oh in range(2):
                                py, bpy = psY.next()
                                for fc in range(8):
                                    S.op("pe", lambda e: e.matmul(py[:, :], lhsT=act[:, fc, jj * 128:(jj + 1) * 128], rhs=wd[:, fc, oh * 512:(oh + 1) * 512], start=(fc == 0), stop=(fc == 7)), reads=[bact, bwd], writes=[bpy])
                                if oh == 0:
                                    S.op("act", lambda e: e.activation(out=ys[:, 0:512], in_=py[:, :], func=AF.Copy, scale=gts[:, e_, j, e_:e_ + 1]), reads=[bpy, bgts[e_]], writes=[bys])
                                else:
                                    S.op("dve", lambda e: e.tensor_scalar(out=ys[:, 512:1024], in0=py[:, :], scalar1=gts[:, e_, j, e_:e_ + 1], scalar2=None, op0=ALU.mult), reads=[bpy, bgts[e_]], writes=[bys])
                            for t in sc_prev:
                                S._wait("pool", t)
                            tk = S.idma(out=moe_d[:, :], out_offset=bass.IndirectOffsetOnAxis(ap=idxi[:, e_, j:j + 1], axis=0), in_=ys[:], in_offset=None,
                                        reads=[bys, bidx[e_]], writes=[], compute_op=ALU.add)
                            sc_cur.append(tk)
                        if hh == HPE - 1:
                            sc_prev[:] = sc_cur
                            sc_cur[:] = []
                        if hh == HPE - 1:
                            si += 3

                    xgq = {}
                    XgTq = {}
                    nU = len(units)
                    xgq[0] = stageA(0)
                    if nU > 1:
                        xgq[1] = stageA(1)
                    XgTq[0] = stageB(0, xgq.pop(0))
                    for u in range(nU):
                        if u + 2 < nU:
                            xgq[u + 2] = stageA(u + 2)
                        if u + 1 < nU:
                            XgTq[u + 1] = stageB(u + 1, xgq.pop(u + 1))
                        stageC(u, XgTq.pop(u))
                    S.barrier()
                with ExitStack() as ph:
                    g5x, bg5x = modrow_bc(ph, "g5x", 0, 5)
                    if ctx_out:
                        g5c, bg5c = modrow_bc(ph, "g5c", 1, 5)
                    ln2g, bl2g = load_row_bc(ph, "ln2g", ln2_g_d[l:l + 1, :], D)
                    ln2b, bl2b = load_row_bc(ph, "ln2b", ln2_b_d[l:l + 1, :], D)
                    x1_r = Ring([sb(ph, f"p_x1{i}", [128, D], F32) for i in range(4)])
                    mo_r = Ring([sb(ph, f"p_mo{i}", [128, D], F32) for i in range(4)])
                    st_r = Ring([sb(ph, f"p_st{i}", [128, 8], F32) for i in range(3)])
                    junk = sb(ph, "p_junk", [128, D], BF16)
                    bjunk = Buf()
                    ptiles = list(range(0 if ctx_out else 2, NT))

                    def p_load(k):
                        xt, bx = x1_r.next()
                        mt, bm = mo_r.next()
                        S.dma("sp", xt[:], x1_d[k * 128:(k + 1) * 128, :], reads=[B_["x1"]], writes=[bx])
                        S.dma("sp", mt[:], moe_d[k * 128:(k + 1) * 128, :], reads=[B_["moe"]], writes=[bm])
                        return (xt, bx, mt, bm)

                    nxt = p_load(ptiles[0])
                    for pi, k in enumerate(ptiles):
                        (xt, bx, mt, bm) = nxt
                        S.flush()
                        if pi + 1 < len(ptiles):
                            nxt = p_load(ptiles[pi + 1])
                        g5, bg5 = (g5c, bg5c) if k < 2 else (g5x, bg5x)
                        S.op("dve", lambda e: e.tensor_tensor(out=mt[:], in0=mt[:], in1=g5[:], op=ALU.mult), reads=[bm, bg5], writes=[bm])
                        S.op("dve", lambda e: e.scalar_tensor_tensor(out=xt[:], in0=xt[:], scalar=ALPHA, in1=mt[:], op0=ALU.mult, op1=ALU.add), reads=[bx, bm], writes=[bx])
                        st8 = st_r.next()
                        ln_stats(st8, xt[:], bx, 1e-5, junk[:], bjunk)
                        S.op("act", lambda e: e.activation(out=xt[:], in_=xt[:], func=AF.Identity, bias=st8[0][:, 2:3], scale=1.0), reads=[bx, st8[1]], writes=[bx])
                        S.op("dve", lambda e: e.scalar_tensor_tensor(out=xt[:], in0=xt[:], scalar=st8[0][:, 4:5], in1=ln2g[:], op0=ALU.mult, op1=ALU.mult), reads=[bx, st8[1], bl2g], writes=[bx])
                        S.op("dve", lambda e: e.tensor_tensor(out=xt[:], in0=xt[:], in1=ln2b[:], op=ALU.add), reads=[bx, bl2b], writes=[bx])
                        if last:
                            if k >= 2:
                                S.defer(lambda k=k, xt=xt, bx=bx: S.dma("sp", out_d[(k - 2) * 128:(k - 1) * 128, :], xt[:], reads=[bx], writes=[B_["out"]]))
                            if debug:
                                S.defer(lambda k=k, xt=xt, bx=bx: S.dma("sp", x2_d[k * 128:(k + 1) * 128, :], xt[:], reads=[bx], writes=[B_["x2"]]))
                        else:
                            S.defer(lambda k=k, xt=xt, bx=bx: S.dma("sp", x2_d[k * 128:(k + 1) * 128, :], xt[:], reads=[bx], writes=[B_["x2"]]))
                    S.barrier()
        S.finish()
        print("ninst", S.ninst, "nwait", S.nwait)
    return nc


def host_consts(T):
    ident = np.eye(128, dtype=np.float32)
    anti = np.ascontiguousarray(ident[::-1])
    jj = np.arange(128)[:, None]
    ii = np.arange(128)[None, :]
    maskf = (jj <= ii).astype(np.float32)
    maskb = (jj >= ii).astype(np.float32)
    half = 32
    inv = (10000.0 ** (-np.arange(0, half, 2, dtype=np.float32) / np.float32(half))).astype(np.float32)
    rows = T // GRID_W
    row = np.repeat(np.arange(rows, dtype=np.float32), GRID_W)
    col = np.tile(np.arange(GRID_W, dtype=np.float32), rows)
    ang = np.concatenate([row[:, None] * inv, col[:, None] * inv], axis=-1).astype(np.float32)
    cos = np.cos(ang).astype(np.float32).T
    sin = np.sin(ang).astype(np.float32).T
    cos2 = np.ascontiguousarray(np.concatenate([cos, cos], axis=0))
    sin2 = np.ascontiguousarray(np.concatenate([sin, sin], axis=0))
    sel = np.zeros((36, 8, 128), np.float32)
    for j in range(8):
        r = j if j < 4 else 32 + (j - 4)
        sel[r, j, :] = 1.0
    slot = (np.arange(8, dtype=np.float32)[None, :] * 128 + np.arange(128, dtype=np.float32)[:, None]).astype(np.float32)
    return {"c_slot": np.ascontiguousarray(slot), "c_ident": ident, "c_anti": anti, "c_maskf": maskf, "c_maskb": maskb, "c_cos": cos2, "c_sin": sin2, "c_sel": sel}


WNAMES = ["w_mod", "b_mod", "w_in", "b_gates", "conv_w", "conv_b", "m_norm_w", "q_norm_w", "kv_norm_w", "w_uq", "w_ukv", "w_out",
          "ln1_g", "ln1_b", "w_router", "w_gate", "w_up", "w_down", "ln2_g", "ln2_b"]


def make_in_map(b, inputs, T, consts):
    m = {}
    m["x"] = np.ascontiguousarray(inputs["x"][b, :T], dtype=np.float32)
    m["ctx"] = np.ascontiguousarray(inputs["ctx"][b], dtype=np.float32)
    cc = np.stack([np.asarray(inputs["c"][b], np.float32), np.asarray(inputs["c_ctx"], np.float32)], axis=-1)
    m["ccol"] = np.ascontiguousarray(cc.reshape(8, 128, 2).transpose(1, 0, 2))
    for n in WNAMES:
        m[n] = inputs[n]
    m.update(consts)
    return m


_NC_CACHE = {}


def kernel(**inputs):
    T = inputs["x"].shape[1]
    nb = inputs["x"].shape[0]
    inputs = {k: np.ascontiguousarray(np.asarray(v), dtype=np.float32) for k, v in inputs.items()}
    consts = host_consts(T)
    if T not in _NC_CACHE:
        _NC_CACHE[T] = build(T)
    nc = _NC_CACHE[T]
    in_maps = [make_in_map(b, inputs, T, consts) for b in range(nb)]
    res = run_bass_kernel_spmd(nc, in_maps, core_ids=list(range(nb)))
    return np.stack([np.asarray(r["out"], dtype=np.float32) for r in res.results], axis=0)
```

```python
import math
import numpy as np
from contextlib import ExitStack
import concourse.bass as bass
import concourse.mybir as mybir
from concourse.bass_utils import run_bass_kernel_spmd

F32 = mybir.dt.float32
BF16 = mybir.dt.bfloat16
F16 = mybir.dt.float16
I32 = mybir.dt.int32
AF = mybir.ActivationFunctionType
ALU = mybir.AluOpType
AX = mybir.AxisListType

D = 1024
KC = 8
DEPTH = 2
CTX = 256
GRID_W = 64
MW = 512
OFF_Q, OFF_K, OFF_V, OFF_O, OFF_G = 0, 512, 1024, 1536, 2048
OFF_CQ = OFF_G + 16
OFF_CKV = OFF_CQ + 384
OFF_KR = OFF_CKV + 256
IN_COLS = OFF_KR + 64
NE = 16
ALPHA = (2 * DEPTH) ** 0.25
A_SCALE = 192 ** -0.5
DH = 128

SAME_ENGINE_SYNC = True
EPOCH = 30000
NDSEM = 16


class Buf:
    __slots__ = ("name", "last_w", "readers", "excl")

    def __init__(self, name="", excl=False):
        self.name = name
        self.last_w = None
        self.readers = {}
        self.excl = excl


class Ring:
    def __init__(self, tiles, excl=False):
        self.tiles = tiles
        self.bufs = [Buf(excl=excl) for _ in tiles]
        self.i = 0

    def next(self):
        j = self.i % len(self.tiles)
        self.i += 1
        return self.tiles[j], self.bufs[j]


class Sched:
    def __init__(self, nc, es):
        self.nc = nc
        self.es = es
        self.engs = {"pe": nc.tensor, "dve": nc.vector, "act": nc.scalar, "pool": nc.gpsimd, "sp": nc.sync}
        self.cnt = {e: 0 for e in self.engs}
        self.epoch = {e: 0 for e in self.engs}
        self.sems = {}
        for e in self.engs:
            self.sems[(e, 0)] = es.enter_context(nc.semaphore(f"s_{e}_0"))
        self.dq = ["sp", "act", "pool"]
        self.dcount = {q: 0 for q in self.dq}
        for q in self.dq:
            for i in range(NDSEM):
                self.sems[("d", q, i)] = es.enter_context(nc.semaphore(f"d_{q}_{i}"))
        self.seen = {e: {} for e in self.engs}
        self.ninst = 0
        self.nwait = 0
        self.deferred = []

    def _wait(self, e, tok):
        if tok is None:
            return
        if tok[0] == "e":
            _, F, ep, v = tok
            if F == e and (not SAME_ENGINE_SYNC or e == "pe"):
                return
            s = self.seen[e].get(F)
            if s is not None and s >= (ep, v):
                return
            self.engs[e].wait_ge(self.sems[(F, ep)], v)
            self.seen[e][F] = (ep, v)
            self.nwait += 1
        else:
            _, q, i, v = tok
            key = ("d", q, i)
            if self.seen[e].get(key, 0) >= v:
                return
            self.engs[e].wait_ge(self.sems[key], v)
            self.seen[e][key] = v
            self.nwait += 1

    def _deps(self, e, reads, writes):
        toks = []
        for b in reads:
            if b.last_w is not None:
                toks.append(b.last_w)
            if b.excl:
                toks.extend(t for t in b.readers.values() if not (t[0] == "e" and t[1] == e))
        for b in writes:
            if b.last_w is not None:
                toks.append(b.last_w)
            toks.extend(b.readers.values())
        for t in toks:
            self._wait(e, t)

    def _commit(self, tok, reads, writes):
        key = tok[:2] if tok[0] == "e" else tok[:3]
        for b in reads:
            b.readers[key] = tok
        for b in writes:
            b.last_w = tok
            b.readers = {}

    def op(self, e, fn, reads=(), writes=()):
        self._deps(e, reads, writes)
        if self.cnt[e] >= EPOCH:
            self.epoch[e] += 1
            self.cnt[e] = 0
            self.sems[(e, self.epoch[e])] = self.es.enter_context(self.nc.semaphore(f"s_{e}_{self.epoch[e]}"))
        ins = fn(self.engs[e])
        self.cnt[e] += 1
        ins.then_inc(self.sems[(e, self.epoch[e])], 1)
        tok = ("e", e, self.epoch[e], self.cnt[e])
        self._commit(tok, reads, writes)
        self.ninst += 1
        return tok

    def dma(self, q, out, in_, reads=(), writes=(), **kw):
        self._deps(q, reads, writes)
        j = self.dcount[q]
        i, rnd = j % NDSEM, j // NDSEM
        if rnd > 0:
            self._wait(q, ("d", q, i, 16 * rnd))
        ins = self.engs[q].dma_start(out=out, in_=in_, **kw)
        ins.then_inc(self.sems[("d", q, i)], 16)
        self.dcount[q] = j + 1
        tok = ("d", q, i, 16 * (rnd + 1))
        self._commit(tok, reads, writes)
        self.ninst += 1
        return tok

    def idma(self, out, out_offset, in_, in_offset, reads=(), writes=(), **kw):
        q = "pool"
        self._deps(q, reads, writes)
        j = self.dcount[q]
        i, rnd = j % NDSEM, j // NDSEM
        if rnd > 0:
            self._wait(q, ("d", q, i, 16 * rnd))
        ins = self.engs[q].indirect_dma_start(out=out, out_offset=out_offset, in_=in_, in_offset=in_offset, **kw)
        ins.then_inc(self.sems[("d", q, i)], 16)
        self.dcount[q] = j + 1
        tok = ("d", q, i, 16 * (rnd + 1))
        self._commit(tok, reads, writes)
        self.ninst += 1
        return tok

    def defer(self, fn):
        self.deferred.append(fn)

    def flush(self):
        d, self.deferred = self.deferred, []
        for fn in d:
            fn()

    def _all_tokens(self):
        toks = []
        for e in self.engs:
            if self.cnt[e] > 0 or self.epoch[e] > 0:
                toks.append(("e", e, self.epoch[e], self.cnt[e]))
        for q in self.dq:
            j = self.dcount[q]
            for i in range(NDSEM):
                n = (j - i + NDSEM - 1) // NDSEM if j > i else 0
                if n > 0:
                    toks.append(("d", q, i, 16 * n))
        return toks

    def barrier(self):
        self.flush()
        toks = self._all_tokens()
        for e in self.engs:
            for t in toks:
                if t[0] == "e" and t[1] == e:
                    continue
                self._wait(e, t)

    def finish(self):
        self.flush()
        for t in self._all_tokens():
            if t[0] == "d":
                self._wait("sp", t)


def build(T, depth=DEPTH, debug=False, stop_after=None):
    S_ = CTX + T
    NT = S_ // 128
    NX = T // 128
    assert T % 512 == 0
    nc = bass.Bass("TRN2", target_bir_lowering=False)
    skind = "ExternalOutput" if debug else "Internal"

    def din(name, shape, dt=F32):
        return nc.dram_tensor(name, list(shape), dt, kind="ExternalInput").ap()

    def dscr(name, shape, dt=F32):
        return nc.dram_tensor(name, list(shape), dt, kind=skind).ap()

    L = depth
    x_d = din("x", [T, D])
    ctx_d = din("ctx", [CTX, D])
    ccol_d = din("ccol", [128, 8, 2])
    w_mod_d = din("w_mod", [L, D, 6 * D])
    b_mod_d = din("b_mod", [L, 6 * D])
    w_in_d = din("w_in", [L, D, IN_COLS])
    b_gates_d = din("b_gates", [L, 16])
    conv_w_d = din("conv_w", [L, 5, 1024])
    conv_b_d = din("conv_b", [L, 1024])
    m_norm_w_d = din("m_norm_w", [L, 512])
    q_norm_w_d = din("q_norm_w", [L, 384])
    kv_norm_w_d = din("kv_norm_w", [L, 256])
    w_uq_d = din("w_uq", [L, 384, 768])
    w_ukv_d = din("w_ukv", [L, 256, 1024])
    w_out_d = din("w_out", [L, 1024, 1024])
    ln1_g_d = din("ln1_g", [L, D])
    ln1_b_d = din("ln1_b", [L, D])
    w_router_d = din("w_router", [L, D, NE])
    w_gate_d = din("w_gate", [L, NE, D, D])
    w_up_d = din("w_up", [L, NE, D, D])
    w_down_d = din("w_down", [L, NE, D, D])
    ln2_g_d = din("ln2_g", [L, D])
    ln2_b_d = din("ln2_b", [L, D])
    ident_d = din("c_ident", [128, 128])
    anti_d = din("c_anti", [128, 128])
    maskf_d = din("c_maskf", [128, 128])
    maskb_d = din("c_maskb", [128, 128])
    cos_d = din("c_cos", [64, T])
    sin_d = din("c_sin", [64, T])
    sel_d = din("c_sel", [36, 8, 128])
    out_d = nc.dram_tensor("out", [T, D], F32, kind="ExternalOutput").ap()

    modvec_d = dscr("modvec", [L, 2, 6 * D])
    pqk_d = dscr("pqk", [1024, S_])
    gTf_d = dscr("gTf", [8, S_])
    gTb_d = dscr("gTb", [8, S_])
    pv_d = dscr("pv", [S_, 512])
    so_d = dscr("so", [S_, 512])
    krT_d = dscr("krT", [64, S_], BF16)
    qT_d = dscr("qT", [4, 192, S_], BF16)
    knT_d = dscr("knT", [4, 128, S_], BF16)
    va_d = dscr("va", [S_, 512], BF16)
    qmT_d = dscr("qmT", [4, 128, S_], BF16)
    kmT_d = dscr("kmT", [4, 128, S_], BF16)
    kmtm_d = dscr("kmtm", [S_, 512], BF16)
    hf_d = dscr("hf", [S_, 512])
    hb_d = dscr("hb", [S_, 512])
    moT_d = dscr("moT", [512, S_], BF16)
    aoT_d = dscr("aoT", [512, S_], BF16)
    x1_d = dscr("x1", [S_, D])
    h2T_d = dscr("h2T", [1024, S_], BF16)
    affT_d = dscr("affT", [NE, S_])
    x2_d = dscr("x2", [S_, D])
    h2tm_d = dscr("h2tm", [S_, D], BF16)
    afftm_d = dscr("afftm", [S_, NE])
    cnt_d = dscr("cnt", [NE, T], F16)
    tc_d = dscr("tcnt", [1, NE * (T // 128)])
    moe_d = dscr("moe", [S_, D])
    slot_d = din("c_slot", [128, 8])
    wgb_d = dscr("wgb", [L, NE, D, D], BF16)
    wub_d = dscr("wub", [L, NE, D, D], BF16)
    wdb_d = dscr("wdb", [L, NE, D, D], BF16)

    es = ExitStack()
    with es:
        S = Sched(nc, es)

        uid = [0]

        def sb(st, name, shape, dt):
            uid[0] += 1
            return st.enter_context(nc.sbuf_tensor(f"{name}_{uid[0]}", list(shape), dt))

        def ps(st, name, shape, dt):
            uid[0] += 1
            return st.enter_context(nc.psum_tensor(f"{name}_{uid[0]}", list(shape), dt))

        B_ = {n: Buf(n) for n in ["modvec", "pqk", "gT", "pv", "so", "krT", "qT", "knT", "va", "qmT", "kmT", "kmtm",
                                  "hf", "hb", "moT", "aoT", "x1", "h2T", "affT", "x2", "wcast", "out", "h2tm", "afftm", "cnt", "moe"]}

        ident = sb(es, "ident", [128, 128], F32)
        anti = sb(es, "anti", [128, 128], F32)
        identb = sb(es, "identb", [128, 128], BF16)
        maskf = sb(es, "maskf", [128, 128], F32)
        maskb = sb(es, "maskb", [128, 128], F32)
        ones32 = sb(es, "ones32", [128, 128], F32)
        onesb = sb(es, "onesb", [128, 128], BF16)
        zcol = sb(es, "zcol", [128, 1], F32)
        sel36 = sb(es, "sel36", [36, 8, 128], F32)
        b_const = Buf("const")
        S.dma("sp", ident[:], ident_d, writes=[b_const])
        S.dma("sp", anti[:], anti_d, writes=[b_const])
        S.dma("sp", maskf[:], maskf_d, writes=[b_const])
        S.dma("sp", maskb[:], maskb_d, writes=[b_const])
        S.dma("sp", sel36[:], sel_d, writes=[b_const])
        slotf = sb(es, "slotf", [128, 8], F32)
        S.dma("sp", slotf[:], slot_d, writes=[b_const])
        S.op("dve", lambda e: e.memset(ones32[:], 1.0), writes=[b_const])
        S.op("dve", lambda e: e.memset(onesb[:], 1.0), writes=[b_const])
        S.op("dve", lambda e: e.memset(zcol[:], 0.0), writes=[b_const])
        S.op("dve", lambda e: e.tensor_copy(out=identb[:], in_=ident[:]), reads=[b_const], writes=[b_const])


        def tile_src(l, k):
            if l == 0:
                return ctx_d[k * 128:(k + 1) * 128, :] if k < 2 else x_d[(k - 2) * 128:(k - 1) * 128, :]
            return x2_d[k * 128:(k + 1) * 128, :]

        def ku_b(k):
            return (1 - k) if k < 2 else 2 + (NT - 1 - k)

        def k_of_ku_b(ku):
            return (1 - ku) if ku < 2 else NT - 1 - (ku - 2)

        def ln_stats(st8, xt, bx, eps, junk, bjunk):
            t, bst = st8
            S.op("dve", lambda e: e.tensor_reduce(out=t[:, 0:1], in_=xt, axis=AX.X, op=ALU.add), reads=[bx], writes=[bst])
            S.op("act", lambda e: e.activation(out=junk, in_=xt, func=AF.Square, accum_out=t[:, 1:2]), reads=[bx], writes=[bjunk, bst])
            S.op("dve", lambda e: e.tensor_scalar(out=t[:, 2:3], in0=t[:, 0:1], scalar1=-1.0 / D, scalar2=None, op0=ALU.mult), reads=[bst], writes=[bst])
            S.op("dve", lambda e: e.tensor_tensor(out=t[:, 3:4], in0=t[:, 2:3], in1=t[:, 2:3], op=ALU.mult), reads=[bst], writes=[bst])
            S.op("dve", lambda e: e.scalar_tensor_tensor(out=t[:, 3:4], in0=t[:, 1:2], scalar=1.0 / D, in1=t[:, 3:4], op0=ALU.mult, op1=ALU.subtract), reads=[bst], writes=[bst])
            S.op("dve", lambda e: e.tensor_scalar(out=t[:, 3:4], in0=t[:, 3:4], scalar1=eps, scalar2=None, op0=ALU.add), reads=[bst], writes=[bst])
            S.op("act", lambda e: e.activation(out=t[:, 4:5], in_=t[:, 3:4], func=AF.Sqrt), reads=[bst], writes=[bst])
            S.op("dve", lambda e: e.reciprocal(out=t[:, 4:5], in_=t[:, 4:5]), reads=[bst], writes=[bst])

        def ln_stats_multi(tiles, eps, st, bst, junk, bjunk):
            n = len(tiles)
            for i, (xt, bx) in enumerate(tiles):
                S.op("dve", lambda e: e.tensor_reduce(out=st[:, 0, i:i + 1], in_=xt, axis=AX.X, op=ALU.add), reads=[bx], writes=[bst])
                S.op("act", lambda e: e.activation(out=junk, in_=xt, func=AF.Square, accum_out=st[:, 1, i:i + 1]), reads=[bx], writes=[bjunk, bst])
            S.op("dve", lambda e: e.tensor_scalar(out=st[:, 2, :n], in0=st[:, 0, :n], scalar1=-1.0 / D, scalar2=None, op0=ALU.mult), reads=[bst], writes=[bst])
            S.op("dve", lambda e: e.tensor_tensor(out=st[:, 3, :n], in0=st[:, 2, :n], in1=st[:, 2, :n], op=ALU.mult), reads=[bst], writes=[bst])
            S.op("dve", lambda e: e.scalar_tensor_tensor(out=st[:, 3, :n], in0=st[:, 1, :n], scalar=1.0 / D, in1=st[:, 3, :n], op0=ALU.mult, op1=ALU.subtract), reads=[bst], writes=[bst])
            S.op("dve", lambda e: e.tensor_scalar(out=st[:, 3, :n], in0=st[:, 3, :n], scalar1=eps, scalar2=None, op0=ALU.add), reads=[bst], writes=[bst])
            S.op("act", lambda e: e.activation(out=st[:, 4, :n], in_=st[:, 3, :n], func=AF.Sqrt), reads=[bst], writes=[bst])
            S.op("dve", lambda e: e.reciprocal(out=st[:, 4, :n], in_=st[:, 4, :n]), reads=[bst], writes=[bst])

        def load_col(st, name, src_1d, n):
            t = sb(st, name, [128, n], F32)
            b = Buf(name)
            S.dma("sp", t[:], src_1d.rearrange("(c p) -> p c", p=128), writes=[b], allow_slow_non_contiguous=True)
            return t, b

        def load_row_bc(st, name, src_row, n):
            t = sb(st, name, [128, n], F32)
            b = Buf(name)
            S.dma("sp", t[:], src_row.broadcast_to([128, n]), writes=[b])
            return t, b

        for l in range(L):
            last = (l == L - 1)
            ctx_out = not last

            with ExitStack() as ph:
                psA = Ring([ps(ph, f"psA{i}", [128, 512], F32) for i in range(2)], excl=True)
                cc = sb(ph, "cc", [128, 8, 2], F32)
                scs = sb(ph, "scs", [128, 8, 2], F32)
                bcc, bscs, bbm, bmr = Buf(), Buf(), Buf(), Buf()
                S.dma("sp", cc[:], ccol_d, writes=[bcc])
                S.op("act", lambda e: e.activation(out=scs[:], in_=cc[:], func=AF.Silu), reads=[bcc], writes=[bscs])
                modrow = sb(ph, "modrow", [2, 6 * D], F32)
                bmod = sb(ph, "bmod", [2, 6 * D], F32)
                S.dma("sp", bmod[:], b_mod_d[l:l + 1, :].broadcast_to([2, 6 * D]), writes=[bbm])
                wring = Ring([sb(ph, f"wm{i}", [128, 8, 512], F32) for i in range(2)])
                for g in range(12):
                    wt, wb_ = wring.next()
                    S.dma("sp", wt[:], w_mod_d[l, :, g * 512:(g + 1) * 512].rearrange("(kc p) n -> p kc n", p=128), writes=[wb_])
                    pt, pb = psA.next()
                    for kc in range(8):
                        S.op("pe", lambda e: e.matmul(pt[0:2, :], lhsT=scs[:, kc, :], rhs=wt[:, kc, :], start=(kc == 0), stop=(kc == 7)),
                             reads=[bscs, wb_], writes=[pb])
                    S.op("dve", lambda e: e.tensor_tensor(out=modrow[:, g * 512:(g + 1) * 512], in0=pt[0:2, :], in1=bmod[:, g * 512:(g + 1) * 512], op=ALU.add),
                         reads=[pb, bbm], writes=[bmr])
                S.dma("sp", modvec_d[l], modrow[:], reads=[bmr], writes=[B_["modvec"]])
                S.barrier()
            if stop_after == "A":
                break

            def modcol(st, name, r, i, plus1=False):
                t = sb(st, name, [128, 8], F32)
                b = Buf(name)
                S.dma("sp", t[:], modvec_d[l, r, i * D:(i + 1) * D].rearrange("(c p) -> p c", p=128), reads=[B_["modvec"]], writes=[b],
                      allow_slow_non_contiguous=True)
                if plus1:
                    S.op("dve", lambda e: e.tensor_scalar(out=t[:], in0=t[:], scalar1=1.0, scalar2=None, op0=ALU.add), reads=[b], writes=[b])
                return t, b

            def modrow_bc(st, name, r, i):
                t = sb(st, name, [128, D], F32)
                b = Buf(name)
                S.dma("sp", t[:], modvec_d[l, r:r + 1, i * D:(i + 1) * D].broadcast_to([128, D]), reads=[B_["modvec"]], writes=[b])
                return t, b

            blocks = [(0, 2)] + [(2 + 4 * i, 4) for i in range(NX // 4)]

            with ExitStack() as ph:
                psB = Ring([ps(ph, f"psB{i}", [128, 512], F32) for i in range(8)], excl=True)
                w_in_b = sb(ph, "w_in_b", [128, 8, IN_COLS], BF16)
                bwin = Buf("w_in_b")
                S.dma("pool", w_in_b[:], w_in_d[l].rearrange("(kc p) n -> p kc n", p=128), writes=[bwin])
                w_krJ = sb(ph, "w_krJ", [128, 8, 64], BF16)
                S.op("dve", lambda e: e.tensor_scalar(out=w_krJ[:, :, 0:32], in0=w_in_b[:, :, OFF_KR + 32:OFF_KR + 64], scalar1=-1.0, scalar2=None, op0=ALU.mult),
                     reads=[bwin], writes=[bwin])
                S.op("dve", lambda e: e.tensor_copy(out=w_krJ[:, :, 32:64], in_=w_in_b[:, :, OFF_KR:OFF_KR + 32]), reads=[bwin], writes=[bwin])
                w_uq_b = sb(ph, "w_uq_b", [128, 3, 768], BF16)
                w_uqJ = sb(ph, "w_uqJ", [128, 3, 4, 64], BF16)
                w_ukv_b = sb(ph, "w_ukv_b", [128, 2, 1024], BF16)
                w_ukv_v = sb(ph, "w_ukv_v", [128, 2, 512], BF16)
                bwuq = Buf("w_uq")
                bwukv = Buf("w_ukv")
                with ExitStack() as ph2:
                    w_uq32 = sb(ph2, "w_uq32", [128, 3, 768], F32)
                    S.dma("sp", w_uq32[:], w_uq_d[l].rearrange("(c p) n -> p c n", p=128), writes=[bwuq])
                    qnw, bqnw = load_col(ph2, "qnw", q_norm_w_d[l], 3)
                    for c in range(3):
                        S.op("dve", lambda e: e.tensor_scalar(out=w_uq_b[:, c, :], in0=w_uq32[:, c, :], scalar1=qnw[:, c:c + 1], scalar2=A_SCALE, op0=ALU.mult, op1=ALU.mult),
                             reads=[bwuq, bqnw], writes=[bwuq])
                    for h in range(4):
                        S.op("dve", lambda e: e.tensor_scalar(out=w_uqJ[:, :, h, 0:32], in0=w_uq_b[:, :, h * 192 + 160:h * 192 + 192], scalar1=-1.0, scalar2=None, op0=ALU.mult),
                             reads=[bwuq], writes=[bwuq])
                        S.op("dve", lambda e: e.tensor_copy(out=w_uqJ[:, :, h, 32:64], in_=w_uq_b[:, :, h * 192 + 128:h * 192 + 160]), reads=[bwuq], writes=[bwuq])
                    w_ukv32 = sb(ph2, "w_ukv32", [128, 2, 1024], F32)
                    S.dma("sp", w_ukv32[:], w_ukv_d[l].rearrange("(c p) n -> p c n", p=128), writes=[bwukv])
                    kvnw, bkvnw = load_col(ph2, "kvnw", kv_norm_w_d[l], 2)
                    for c in range(2):
                        S.op("dve", lambda e: e.tensor_scalar(out=w_ukv_b[:, c, :], in0=w_ukv32[:, c, :], scalar1=kvnw[:, c:c + 1], scalar2=None, op0=ALU.mult),
                             reads=[bwukv, bkvnw], writes=[bwukv])
                    for h in range(4):
                        S.op("dve", lambda e: e.tensor_copy(out=w_ukv_v[:, :, h * 128:(h + 1) * 128], in_=w_ukv_b[:, :, h * 256 + 128:h * 256 + 256]), reads=[bwukv], writes=[bwukv])
                    S.barrier()
                bg_bc, bbg = load_row_bc(ph, "bg_bc", b_gates_d[l:l + 1, :], 16)
                shx, bshx = modcol(ph, "shx", 0, 0)
                scx, bscx = modcol(ph, "scx", 0, 1, True)
                shc, bshc = modcol(ph, "shc", 1, 0)
                scc, bscc = modcol(ph, "scc", 1, 1, True)

                xring = Ring([sb(ph, f"xt{i}", [128, D], F32) for i in range(8)])
                stm_r = Ring([sb(ph, f"stm{i}", [128, 5, 4], F32) for i in range(2)])
                junk = sb(ph, "junkB", [128, D], BF16)
                bjunk = Buf()
                hT = sb(ph, "hT", [128, 8, 512], BF16)
                bhT = Buf("hT")
                qkst = sb(ph, "qkst", [128, 4, 512], F32)
                bqkst = Buf()
                vst = sb(ph, "vst", [128, 4, 512], F32)
                bvst = Buf()
                ost = sb(ph, "ost", [128, 4, 512], F32)
                bost = Buf()
                gtm = sb(ph, "gtm", [128, 16], F32)
                bgtm = Buf()
                gstf = sb(ph, "gstf", [8, 512], F32)
                gstb = sb(ph, "gstb", [8, 512], F32)
                bgstf, bgstb = Buf(), Buf()
                cqb = sb(ph, "cqb", [128, 3, 512], BF16)
                cqsq = sb(ph, "cqsq", [128, 3, 512], F32)
                ckvb = sb(ph, "ckvb", [128, 2, 512], BF16)
                ckvsq = sb(ph, "ckvsq", [128, 2, 512], F32)
                bcq, bckv = Buf(), Buf()
                rq = sb(ph, "rq", [128, 512], F32)
                rkv = sb(ph, "rkv", [128, 512], F32)
                brq, brkv = Buf(), Buf()
                rkvc = sb(ph, "rkvc", [128, 4], F32)
                brkvc = Buf()
                cos_t = sb(ph, "cos_t", [64, 512], F32)
                sin_t = sb(ph, "sin_t", [64, 512], F32)
                bcs = Buf()
                krst = sb(ph, "krst", [64, 512], BF16)
                bkrst = Buf()
                tmp64 = sb(ph, "tmp64", [64, 512], F32)
                tmp64b = sb(ph, "tmp64b", [64, 512], F32)
                btmp = Buf()
                qst = sb(ph, "qst", [128, 4, 512], BF16)
                qrst = sb(ph, "qrst", [64, 4, 512], BF16)
                bqst, bqrst = Buf(), Buf()
                knst = sb(ph, "knst", [128, 4, 512], BF16)
                bknst = Buf()
                vast = sb(ph, "vast", [128, 4, 512], BF16)
                bvast = Buf()
                print("phase B sbuf remaining", nc.sbuf_bytes_remaining)

                evac_i = [0]

                def evac(out, in_, reads, writes):
                    evac_i[0] += 1
                    if evac_i[0] % 2:
                        S.op("act", lambda e: e.copy(out=out, in_=in_), reads=reads, writes=writes)
                    else:
                        S.op("dve", lambda e: e.tensor_copy(out=out, in_=in_), reads=reads, writes=writes)

                def load_block(bi):
                    k0, ntile = blocks[bi]
                    tiles = []
                    for ti in range(ntile):
                        xt, bx = xring.next()
                        S.dma("sp", xt[:], tile_src(l, k0 + ti), reads=[B_["x2"]], writes=[bx])
                        tiles.append((xt, bx))
                    return tiles

                nxt = load_block(0)
                for bi, (k0, ntile) in enumerate(blocks):
                    cur = nxt
                    is_ctx = (k0 == 0)
                    nb = ntile * 128
                    t0 = k0 * 128
                    sh_, sc_ = (shc, scc) if is_ctx else (shx, scx)
                    bsh_, bsc_ = (bshc, bscc) if is_ctx else (bshx, bscx)
                    xns = []
                    stm, bstm = stm_r.next()
                    ln_stats_multi([(cur[ti][0][:], cur[ti][1]) for ti in range(ntile)], 1e-6, stm, bstm, junk[:], bjunk)
                    for ti in range(ntile):
                        xt, bx = cur[ti]
                        S.op("dve", lambda e: e.tensor_scalar(out=xt[:], in0=xt[:], scalar1=stm[:, 2, ti:ti + 1], scalar2=stm[:, 4, ti:ti + 1], op0=ALU.add, op1=ALU.mult),
                             reads=[bx, bstm], writes=[bx])
                        xns.append((xt, bx))
                    if bi + 1 < len(blocks):
                        nxt = load_block(bi + 1)
                    S.flush()
                    for kc in range(8):
                        pt, pb = psB.next()
                        for ti in range(ntile):
                            xn, bxn = xns[ti]
                            S.op("pe", lambda e: e.transpose(out=pt[:, ti * 128:(ti + 1) * 128], in_=xn[:, kc * 128:(kc + 1) * 128], identity=ident[:]),
                                 reads=[bxn, b_const], writes=[pb])
                        if kc % 2:
                            S.op("act", lambda e: e.activation(out=hT[:, kc, :nb], in_=pt[:, :nb], func=AF.Identity, bias=sh_[:, kc:kc + 1], scale=sc_[:, kc:kc + 1]),
                                 reads=[pb, bsh_, bsc_], writes=[bhT])
                        else:
                            S.op("dve", lambda e: e.tensor_scalar(out=hT[:, kc, :nb], in0=pt[:, :nb], scalar1=sc_[:, kc:kc + 1], scalar2=sh_[:, kc:kc + 1], op0=ALU.mult, op1=ALU.add),
                                 reads=[pb, bsh_, bsc_], writes=[bhT])
                    for oc in range(8):
                        pt, pb = psB.next()
                        for kc in range(8):
                            S.op("pe", lambda e: e.matmul(pt[:, :nb], lhsT=w_in_b[:, kc, oc * 128:(oc + 1) * 128], rhs=hT[:, kc, :nb], start=(kc == 0), stop=(kc == 7)),
                                 reads=[bwin, bhT], writes=[pb])
                        evac(qkst[:, oc % 4, :nb], pt[:, :nb], [pb], [bqkst])
                        if oc % 4 == 3:
                            o0 = oc - 3
                            S.dma("sp", pqk_d.rearrange("(oc p) s -> p oc s", p=128)[:, o0:o0 + 4, t0:t0 + nb], qkst[:, :, :nb], reads=[bqkst], writes=[B_["pqk"]])
                    for ti in range(ntile):
                        k = k0 + ti
                        pt, pb = psB.next()
                        for kc in range(8):
                            S.op("pe", lambda e: e.matmul(pt[:, :], lhsT=hT[:, kc, ti * 128:(ti + 1) * 128], rhs=w_in_b[:, kc, OFF_V:OFF_V + 512], start=(kc == 0), stop=(kc == 7)),
                                 reads=[bwin, bhT], writes=[pb])
                        evac(vst[:, ti, :], pt[:, :], [pb], [bvst])
                        pt, pb = psB.next()
                        for kc in range(8):
                            S.op("pe", lambda e: e.matmul(pt[:, :], lhsT=hT[:, kc, ti * 128:(ti + 1) * 128], rhs=w_in_b[:, kc, OFF_O:OFF_O + 512], start=(kc == 0), stop=(kc == 7)),
                                 reads=[bwin, bhT], writes=[pb])
                        S.op("act", lambda e: e.activation(out=ost[:, ti, :], in_=pt[:, :], func=AF.Sigmoid), reads=[pb], writes=[bost])
                        pt, pb = psB.next()
                        for kc in range(8):
                            S.op("pe", lambda e: e.matmul(pt[:, 0:16], lhsT=hT[:, kc, ti * 128:(ti + 1) * 128], rhs=w_in_b[:, kc, OFF_G:OFF_G + 16], start=(kc == 0), stop=(kc == 7)),
                                 reads=[bwin, bhT], writes=[pb])
                        S.op("dve", lambda e: e.tensor_tensor(out=gtm[:], in0=pt[:, 0:16], in1=bg_bc[:], op=ALU.add), reads=[pb, bbg], writes=[bgtm])
                        pt2, pb2 = psB.next()
                        S.op("pe", lambda e: e.transpose(out=pt2[0:8, 0:128], in_=gtm[:, 0:8], identity=ident[:]), reads=[bgtm, b_const], writes=[pb2])
                        S.op("pe", lambda e: e.matmul(pt2[0:8, 128:256], lhsT=gtm[:, 8:16], rhs=anti[:], start=True, stop=True), reads=[bgtm, b_const], writes=[pb2])
                        S.op("act", lambda e: e.copy(out=gstf[:, ti * 128:(ti + 1) * 128], in_=pt2[0:8, 0:128]), reads=[pb2], writes=[bgstf])
                        tj = ntile - 1 - ti
                        S.op("act", lambda e: e.copy(out=gstb[:, tj * 128:(tj + 1) * 128], in_=pt2[0:8, 128:256]), reads=[pb2], writes=[bgstb])
                    S.dma("sp", pv_d[t0:t0 + nb, :].rearrange("(t p) c -> p t c", p=128), vst[:, :ntile, :], reads=[bvst], writes=[B_["pv"]])
                    S.dma("sp", so_d[t0:t0 + nb, :].rearrange("(t p) c -> p t c", p=128), ost[:, :ntile, :], reads=[bost], writes=[B_["so"]])
                    S.dma("sp", gTf_d[:, t0:t0 + nb], gstf[:, :nb], reads=[bgstf], writes=[B_["gT"]])
                    u0 = ku_b(k0 + ntile - 1) * 128
                    S.dma("sp", gTb_d[:, u0:u0 + nb], gstb[:, :nb], reads=[bgstb], writes=[B_["gT"]])
                    for c in range(3):
                        pt, pb = psB.next()
                        for kc in range(8):
                            S.op("pe", lambda e: e.matmul(pt[:, :nb], lhsT=w_in_b[:, kc, OFF_CQ + c * 128:OFF_CQ + (c + 1) * 128], rhs=hT[:, kc, :nb], start=(kc == 0), stop=(kc == 7)),
                                 reads=[bwin, bhT], writes=[pb])
                        S.op("dve", lambda e: e.tensor_copy(out=cqb[:, c, :nb], in_=pt[:, :nb]), reads=[pb], writes=[bcq])
                        S.op("act", lambda e: e.activation(out=cqsq[:, c, :nb], in_=pt[:, :nb], func=AF.Square), reads=[pb], writes=[bcq])
                    pt, pb = psB.next()
                    for c in range(3):
                        S.op("pe", lambda e: e.matmul(pt[:, :nb], lhsT=ones32[:], rhs=cqsq[:, c, :nb], start=(c == 0), stop=(c == 2)), reads=[bcq, b_const], writes=[pb])
                    S.op("dve", lambda e: e.tensor_scalar(out=rq[:, :nb], in0=pt[:, :nb], scalar1=1.0 / 384, scalar2=1e-6, op0=ALU.mult, op1=ALU.add), reads=[pb], writes=[brq])
                    S.op("act", lambda e: e.activation(out=rq[:, :nb], in_=rq[:, :nb], func=AF.Sqrt), reads=[brq], writes=[brq])
                    S.op("dve", lambda e: e.reciprocal(out=rq[:, :nb], in_=rq[:, :nb]), reads=[brq], writes=[brq])
                    for c in range(2):
                        pt, pb = psB.next()
                        for kc in range(8):
                            S.op("pe", lambda e: e.matmul(pt[:, :nb], lhsT=w_in_b[:, kc, OFF_CKV + c * 128:OFF_CKV + (c + 1) * 128], rhs=hT[:, kc, :nb], start=(kc == 0), stop=(kc == 7)),
                                 reads=[bwin, bhT], writes=[pb])
                        S.op("dve", lambda e: e.tensor_copy(out=ckvb[:, c, :nb], in_=pt[:, :nb]), reads=[pb], writes=[bckv])
                        S.op("act", lambda e: e.activation(out=ckvsq[:, c, :nb], in_=pt[:, :nb], func=AF.Square), reads=[pb], writes=[bckv])
                    pt, pb = psB.next()
                    for c in range(2):
                        S.op("pe", lambda e: e.matmul(pt[:, :nb], lhsT=ones32[:], rhs=ckvsq[:, c, :nb], start=(c == 0), stop=(c == 1)), reads=[bckv, b_const], writes=[pb])
                    S.op("dve", lambda e: e.tensor_scalar(out=rkv[:, :nb], in0=pt[:, :nb], scalar1=1.0 / 256, scalar2=1e-6, op0=ALU.mult, op1=ALU.add), reads=[pb], writes=[brkv])
                    S.op("act", lambda e: e.activation(out=rkv[:, :nb], in_=rkv[:, :nb], func=AF.Sqrt), reads=[brkv], writes=[brkv])
                    S.op("dve", lambda e: e.reciprocal(out=rkv[:, :nb], in_=rkv[:, :nb]), reads=[brkv], writes=[brkv])
                    pt, pb = psB.next()
                    for ti in range(ntile):
                        for c in range(2):
                            S.op("pe", lambda e: e.matmul(pt[:, ti:ti + 1], lhsT=ckvsq[:, c, ti * 128:(ti + 1) * 128], rhs=ones32[:, 0:1], start=(c == 0), stop=(c == 1)),
                                 reads=[bckv, b_const], writes=[pb])
                    S.op("dve", lambda e: e.tensor_scalar(out=rkvc[:, :ntile], in0=pt[:, :ntile], scalar1=1.0 / 256, scalar2=1e-6, op0=ALU.mult, op1=ALU.add), reads=[pb], writes=[brkvc])
                    S.op("act", lambda e: e.activation(out=rkvc[:, :ntile], in_=rkvc[:, :ntile], func=AF.Sqrt), reads=[brkvc], writes=[brkvc])
                    S.op("dve", lambda e: e.reciprocal(out=rkvc[:, :ntile], in_=rkvc[:, :ntile]), reads=[brkvc], writes=[brkvc])
                    if not is_ctx:
                        xo = t0 - CTX
                        S.dma("sp", cos_t[:, :nb], cos_d[:, xo:xo + nb], writes=[bcs])
                        S.dma("sp", sin_t[:, :nb], sin_d[:, xo:xo + nb], writes=[bcs])
                    pt, pb = psB.next()
                    for kc in range(8):
                        S.op("pe", lambda e: e.matmul(pt[0:64, :nb], lhsT=w_in_b[:, kc, OFF_KR:OFF_KR + 64], rhs=hT[:, kc, :nb], start=(kc == 0), stop=(kc == 7)),
                             reads=[bwin, bhT], writes=[pb])
                    if is_ctx:
                        S.op("act", lambda e: e.copy(out=krst[:, :nb], in_=pt[0:64, :nb]), reads=[pb], writes=[bkrst])
                    else:
                        pt2, pb2 = psB.next()
                        for kc in range(8):
                            S.op("pe", lambda e: e.matmul(pt2[0:64, :nb], lhsT=w_krJ[:, kc, :], rhs=hT[:, kc, :nb], start=(kc == 0), stop=(kc == 7)),
                                 reads=[bwin, bhT], writes=[pb2])
                        S.op("dve", lambda e: e.tensor_tensor(out=tmp64[:, :nb], in0=pt[0:64, :nb], in1=cos_t[:, :nb], op=ALU.mult), reads=[pb, bcs], writes=[btmp])
                        S.op("dve", lambda e: e.tensor_tensor(out=tmp64b[:, :nb], in0=pt2[0:64, :nb], in1=sin_t[:, :nb], op=ALU.mult), reads=[pb2, bcs], writes=[btmp])
                        S.op("dve", lambda e: e.tensor_tensor(out=krst[:, :nb], in0=tmp64[:, :nb], in1=tmp64b[:, :nb], op=ALU.add), reads=[btmp], writes=[bkrst])
                    S.dma("sp", krT_d[:, t0:t0 + nb], krst[:, :nb], reads=[bkrst], writes=[B_["krT"]])
                    if (not is_ctx) or ctx_out:
                        for h in range(4):
                            pt, pb = psB.next()
                            for c in range(3):
                                S.op("pe", lambda e: e.matmul(pt[:, :nb], lhsT=w_uq_b[:, c, h * 192:h * 192 + 128], rhs=cqb[:, c, :nb], start=(c == 0), stop=(c == 2)),
                                     reads=[bwuq, bcq], writes=[pb])
                            S.op("dve", lambda e: e.tensor_tensor(out=qst[:, h, :nb], in0=pt[:, :nb], in1=rq[:, :nb], op=ALU.mult), reads=[pb, brq], writes=[bqst])
                            pt, pb = psB.next()
                            for c in range(3):
                                S.op("pe", lambda e: e.matmul(pt[0:64, :nb], lhsT=w_uq_b[:, c, h * 192 + 128:h * 192 + 192], rhs=cqb[:, c, :nb], start=(c == 0), stop=(c == 2)),
                                     reads=[bwuq, bcq], writes=[pb])
                            if is_ctx:
                                S.op("dve", lambda e: e.tensor_tensor(out=qrst[:, h, :nb], in0=pt[0:64, :nb], in1=rq[0:64, :nb], op=ALU.mult), reads=[pb, brq], writes=[bqrst])
                            else:
                                pt2, pb2 = psB.next()
                                for c in range(3):
                                    S.op("pe", lambda e: e.matmul(pt2[0:64, :nb], lhsT=w_uqJ[:, c, h, :], rhs=cqb[:, c, :nb], start=(c == 0), stop=(c == 2)),
                                         reads=[bwuq, bcq], writes=[pb2])
                                S.op("dve", lambda e: e.tensor_tensor(out=tmp64[:, :nb], in0=pt[0:64, :nb], in1=cos_t[:, :nb], op=ALU.mult), reads=[pb, bcs], writes=[btmp])
                                S.op("dve", lambda e: e.tensor_tensor(out=tmp64b[:, :nb], in0=pt2[0:64, :nb], in1=sin_t[:, :nb], op=ALU.mult), reads=[pb2, bcs], writes=[btmp])
                                S.op("dve", lambda e: e.tensor_tensor(out=tmp64[:, :nb], in0=tmp64[:, :nb], in1=tmp64b[:, :nb], op=ALU.add), reads=[btmp], writes=[btmp])
                                S.op("dve", lambda e: e.tensor_tensor(out=qrst[:, h, :nb], in0=tmp64[:, :nb], in1=rq[0:64, :nb], op=ALU.mult), reads=[btmp, brq], writes=[bqrst])
                        S.dma("sp", qT_d[:, 0:128, t0:t0 + nb].rearrange("h p s -> p h s"), qst[:, :, :nb], reads=[bqst], writes=[B_["qT"]])
                        S.dma("sp", qT_d[:, 128:192, t0:t0 + nb].rearrange("h p s -> p h s"), qrst[:, :, :nb], reads=[bqrst], writes=[B_["qT"]])
                    for h in range(4):
                        pt, pb = psB.next()
                        for c in range(2):
                            S.op("pe", lambda e: e.matmul(pt[:, :nb], lhsT=w_ukv_b[:, c, h * 256:h * 256 + 128], rhs=ckvb[:, c, :nb], start=(c == 0), stop=(c == 1)),
                                 reads=[bwukv, bckv], writes=[pb])
                        S.op("dve", lambda e: e.tensor_tensor(out=knst[:, h, :nb], in0=pt[:, :nb], in1=rkv[:, :nb], op=ALU.mult), reads=[pb, brkv], writes=[bknst])
                    S.dma("sp", knT_d[:, :, t0:t0 + nb].rearrange("h p s -> p h s"), knst[:, :, :nb], reads=[bknst], writes=[B_["knT"]])
                    for ti in range(ntile):
                        pt, pb = psB.next()
                        for c in range(2):
                            S.op("pe", lambda e: e.matmul(pt[:, :], lhsT=ckvb[:, c, ti * 128:(ti + 1) * 128], rhs=w_ukv_v[:, c, :], start=(c == 0), stop=(c == 1)),
                                 reads=[bwukv, bckv], writes=[pb])
                        S.op("act", lambda e: e.activation(out=vast[:, ti, :], in_=pt[:, :], func=AF.Copy, scale=rkvc[:, ti:ti + 1]), reads=[pb, brkvc], writes=[bvast])
                    S.dma("sp", va_d[t0:t0 + nb, :].rearrange("(t p) c -> p t c", p=128), vast[:, :ntile, :], reads=[bvast], writes=[B_["va"]])
                S.barrier()
            if stop_after == "B":
                break
            def xstream():
                with ExitStack() as ph:
                    psC = Ring([ps(ph, f"psC{i}", [128, 1024], BF16) for i in range(2)], excl=True)
                    cw = sb(ph, "cw", [128, 5, 8], F32)
                    bcw = Buf()
                    for k in range(5):
                        S.dma("sp", cw[:, k, :], conv_w_d[l, k, :].rearrange("(oc p) -> p oc", p=128), writes=[bcw], allow_slow_non_contiguous=True)
                    cb_, bcb = load_col(ph, "cb", conv_b_d[l], 8)
                    CB = 1024
                    xin_r = Ring([sb(ph, f"xin{i}", [128, CB + 4], F32) for i in range(3)])
                    acc_r = Ring([sb(ph, f"cacc{i}", [128, CB], F32) for i in range(2)])
                    ctmp = sb(ph, "ctmp", [128, CB], F32)
                    bctmp = Buf()
                    qo_r = Ring([sb(ph, f"qo{i}", [128, CB], BF16) for i in range(3)])
                    kt_r = Ring([sb(ph, f"ktst{i}", [128, 8, 128], BF16) for i in range(2)])
                    pieces = []
                    for (sa, sb_) in ((0, CTX), (CTX, S_)):
                        a = sa
                        while a < sb_:
                            b = min(a + CB, sb_)
                            pieces.append((sa, sb_, a, b))
                            a = b
                    for oc in range(8):
                        eng = "dve"
                        for (sa, sb_, a, b) in pieces:
                            n = b - a
                            yield
                            xin, bxin = xin_r.next()
                            lo = a - 2 if a > sa else a
                            hi = b + 2 if b < sb_ else b
                            if a == sa:
                                S.op("dve", lambda e: e.memset(xin[:, 0:2], 0.0), writes=[bxin])
                            if b == sb_:
                                S.op("dve", lambda e: e.memset(xin[:, 2 + n:4 + n], 0.0), writes=[bxin])
                            S.dma("sp", xin[:, 2 - (a - lo):2 + n + (hi - b)], pqk_d[oc * 128:(oc + 1) * 128, lo:hi], reads=[B_["pqk"]], writes=[bxin])
                            yield
                            acc, bacc = acc_r.next()
                            S.op(eng, lambda e: e.tensor_scalar(out=acc[:, :n], in0=xin[:, 0:n], scalar1=cw[:, 0, oc:oc + 1], scalar2=None, op0=ALU.mult), reads=[bxin, bcw], writes=[bacc])
                            for k in range(1, 5):
                                if eng == "dve":
                                    S.op(eng, lambda e: e.scalar_tensor_tensor(out=acc[:, :n], in0=xin[:, k:k + n], scalar=cw[:, k, oc:oc + 1], in1=acc[:, :n], op0=ALU.mult, op1=ALU.add),
                                         reads=[bxin, bcw, bacc], writes=[bacc])
                                else:
                                    S.op(eng, lambda e: e.tensor_scalar(out=ctmp[:, :n], in0=xin[:, k:k + n], scalar1=cw[:, k, oc:oc + 1], scalar2=None, op0=ALU.mult), reads=[bxin, bcw], writes=[bctmp])
                                    S.op(eng, lambda e: e.tensor_tensor(out=acc[:, :n], in0=acc[:, :n], in1=ctmp[:, :n], op=ALU.add), reads=[bacc, bctmp], writes=[bacc])
                            yield
                            qo, bqo = qo_r.next()
                            S.op("act", lambda e: e.activation(out=qo[:, :n], in_=acc[:, :n], func=AF.Silu, bias=cb_[:, oc:oc + 1]), reads=[bacc, bcb], writes=[bqo])
                            if oc < 4:
                                S.dma("sp", qmT_d[oc, :, a:b], qo[:, :n], reads=[bqo], writes=[B_["qmT"]])
                            else:
                                h = oc - 4
                                S.dma("sp", kmT_d[h, :, a:b], qo[:, :n], reads=[bqo], writes=[B_["kmT"]])
                                nt_ = n // 128
                                yield
                                pt, pb = psC.next()
                                for j in range(nt_):
                                    S.op("pe", lambda e: e.transpose(out=pt[:, j * 128:(j + 1) * 128], in_=qo[:, j * 128:(j + 1) * 128], identity=identb[:]), reads=[bqo, b_const], writes=[pb])
                                yield
                                kt, bkt = kt_r.next()
                                S.op("dve", lambda e: e.tensor_copy(out=kt[:, :nt_, :], in_=pt[:, :n].rearrange("p (t c) -> p t c", c=128)), reads=[pb], writes=[bkt])
                                S.dma("sp", kmtm_d[a:b, h * 128:(h + 1) * 128].rearrange("(t p) c -> p t c", p=128), kt[:, :nt_, :], reads=[bkt], writes=[B_["kmtm"]])
                    S.barrier()

                with ExitStack() as phm:
                    ea_tm = sb(phm, "ea_tm", [128, NT, 8], F32)
                    fl_tm = sb(phm, "fl_tm", [128, NT, 8], F32)
                    decay_bc = sb(phm, "decay_bc", [128, 8, NT], F32)
                    bea, bfl, bdec = Buf(), Buf(), Buf()
                    with ExitStack() as ph:
                        psD = Ring([ps(ph, f"psD{i}", [128, 512], F32) for i in range(3)], excl=True)
                        t1 = sb(ph, "t1", [36, S_], F32)
                        t2 = sb(ph, "t2", [36, S_], F32)
                        t3 = sb(ph, "t3", [36, S_], F32)
                        bt1, bt2, bt3 = Buf(), Buf(), Buf()
                        S.op("dve", lambda e: e.memset(t1[:], 30.0), writes=[bt1])
                        S.op("pool", lambda e: e.memset(t3[:], 0.0), writes=[bt3])
                        S.dma("sp", t3[0:4, :], gTf_d[0:4, :], reads=[B_["gT"]], writes=[bt3])
                        S.dma("sp", t3[32:36, :], gTb_d[0:4, :], reads=[B_["gT"]], writes=[bt3])
                        S.dma("sp", t1[0:4, :], gTf_d[4:8, :], reads=[B_["gT"]], writes=[bt1])
                        S.dma("sp", t1[32:36, :], gTb_d[4:8, :], reads=[B_["gT"]], writes=[bt1])
                        S.op("act", lambda e: e.activation(out=t1[:], in_=t1[:], func=AF.Exp, scale=-1.0), reads=[bt1], writes=[bt1])
                        S.op("dve", lambda e: e.tensor_scalar(out=t1[:], in0=t1[:], scalar1=1.0, scalar2=None, op0=ALU.add), reads=[bt1], writes=[bt1])
                        S.op("act", lambda e: e.activation(out=t1[:], in_=t1[:], func=AF.Ln), reads=[bt1], writes=[bt1])
                        S.op("dve", lambda e: e.tensor_tensor_scan(out=t2[:], data0=t1[:], data1=zcol[0:36, 0:1].broadcast_to([36, S_]), initial=0.0, op0=ALU.add, op1=ALU.add),
                             reads=[bt1, b_const], writes=[bt2])
                        yield
                        S.op("dve", lambda e: e.tensor_tensor(out=t3[:], in0=t3[:], in1=t2[:], op=ALU.add), reads=[bt3, bt2], writes=[bt3])
                        S.op("dve", lambda e: e.tensor_tensor_scan(out=t1[:], data0=t3[:], data1=t3[:], initial=0.0, op0=ALU.max, op1=ALU.max), reads=[bt3, bt1], writes=[bt1])
                        yield
                        mcur = sb(ph, "mcur", [36, NT], F32)
                        mprev = sb(ph, "mprev", [36, NT], F32)
                        mprevc = sb(ph, "mprevc", [36, NT], F32)
                        dec = sb(ph, "dec", [36, NT], F32)
                        bm = Buf()
                        t1v = t1[:].rearrange("p (t c) -> p t c", c=128)
                        t2v = t2[:].rearrange("p (t c) -> p t c", c=128)
                        t3v = t3[:].rearrange("p (t c) -> p t c", c=128)
                        S.op("dve", lambda e: e.tensor_copy(out=mcur[:].rearrange("p (t o) -> p t o", o=1), in_=t1v[:, :, 127:128]), reads=[bt1], writes=[bm])
                        S.op("dve", lambda e: e.memset(mprev[:, 0:1], 0.0), writes=[bm])
                        S.op("dve", lambda e: e.tensor_copy(out=mprev[:, 1:NT], in_=mcur[:, 0:NT - 1]), reads=[bm], writes=[bm])
                        S.op("dve", lambda e: e.tensor_tensor(out=dec[:], in0=mprev[:], in1=mcur[:], op=ALU.subtract), reads=[bm], writes=[bm])
                        S.op("act", lambda e: e.activation(out=dec[:], in_=dec[:], func=AF.Exp), reads=[bm], writes=[bm])
                        S.op("dve", lambda e: e.tensor_scalar(out=mprevc[:], in0=mprev[:], scalar1=-0.5 * math.log(DH), scalar2=None, op0=ALU.add), reads=[bm], writes=[bm])
                        mpb = mprev[:].rearrange("p (t o) -> p t o", o=1).broadcast_to([36, NT, 128])
                        mpcb = mprevc[:].rearrange("p (t o) -> p t o", o=1).broadcast_to([36, NT, 128])
                        S.op("dve", lambda e: e.tensor_tensor(out=t3v, in0=t3v, in1=mpb, op=ALU.subtract), reads=[bt3, bm], writes=[bt3])
                        S.op("act", lambda e: e.activation(out=t3[:], in_=t3[:], func=AF.Exp), reads=[bt3], writes=[bt3])
                        yield
                        S.op("dve", lambda e: e.tensor_tensor(out=t2v, in0=t2v, in1=mpcb, op=ALU.subtract), reads=[bt2, bm], writes=[bt2])
                        S.op("act", lambda e: e.activation(out=t2[:], in_=t2[:], func=AF.Exp), reads=[bt2], writes=[bt2])
                        tm_r = Ring([sb(ph, f"tmA{i}", [128, 128], F32) for i in range(2)])
                        for ku in range(NT):
                            yield
                            pt, pb = psD.next()
                            S.op("pe", lambda e: e.transpose(out=pt[:, 0:36], in_=t3[0:36, ku * 128:(ku + 1) * 128], identity=ident[0:36, 0:36]), reads=[bt3, b_const], writes=[pb])
                            S.op("pe", lambda e: e.transpose(out=pt[:, 36:72], in_=t2[0:36, ku * 128:(ku + 1) * 128], identity=ident[0:36, 0:36]), reads=[bt2, b_const], writes=[pb])
                            yield
                            tmA, btm = tm_r.next()
                            S.op("act", lambda e: e.copy(out=tmA[:, 0:72], in_=pt[:, 0:72]), reads=[pb], writes=[btm])
                            yield
                            S.op("pool", lambda e: e.tensor_copy(out=ea_tm[:, ku, 0:4], in_=tmA[:, 0:4]), reads=[btm], writes=[bea])
                            S.op("pool", lambda e: e.tensor_copy(out=fl_tm[:, ku, 0:4], in_=tmA[:, 36:40]), reads=[btm], writes=[bfl])
                            pt2, pb2 = psD.next()
                            S.op("pe", lambda e: e.matmul(pt2[:, 0:4], lhsT=anti[:], rhs=tmA[:, 32:36], start=True, stop=True), reads=[btm, b_const], writes=[pb2])
                            S.op("pe", lambda e: e.matmul(pt2[:, 4:8], lhsT=anti[:], rhs=tmA[:, 68:72], start=True, stop=True), reads=[btm, b_const], writes=[pb2])
                            kb = k_of_ku_b(ku)
                            yield
                            S.op("dve", lambda e: e.tensor_copy(out=ea_tm[:, kb, 4:8], in_=pt2[:, 0:4]), reads=[pb2], writes=[bea])
                            S.op("dve", lambda e: e.tensor_copy(out=fl_tm[:, kb, 4:8], in_=pt2[:, 4:8]), reads=[pb2], writes=[bfl])
                        for j in range(8):
                            pt, pb = psD.next()
                            S.op("pe", lambda e: e.matmul(pt[:, 0:NT], lhsT=sel36[0:36, j, :], rhs=dec[0:36, 0:NT], start=True, stop=True), reads=[bm, b_const], writes=[pb])
                            S.op("dve", lambda e: e.tensor_copy(out=decay_bc[:, j, :], in_=pt[:, 0:NT]), reads=[pb], writes=[bdec])
                        S.barrier()
                    with ExitStack() as ph:
                        psE = Ring([ps(ph, f"psE{i}", [128, 512], F32) for i in range(4)], excl=True)
                        C32 = [[sb(ph, f"C32_{d}{h}", [128, 129], F32) for h in range(4)] for d in range(2)]
                        Cb = [[sb(ph, f"Cb_{d}{h}", [128, 129], BF16) for h in range(4)] for d in range(2)]
                        bC32 = [[Buf() for h in range(4)] for d in range(2)]
                        bCb = [[Buf() for h in range(4)] for d in range(2)]
                        for d in range(2):
                            for h in range(4):
                                S.op("pool", lambda e: e.memset(C32[d][h][:], 0.0), writes=[bC32[d][h]])
                                S.op("pool", lambda e: e.memset(Cb[d][h][:], 0.0), writes=[bCb[d][h]])
                        qt_r = Ring([sb(ph, f"eQT{i}", [128, 4, 128], BF16) for i in range(4)])
                        kt_r = Ring([sb(ph, f"eKT{i}", [128, 4, 128], BF16) for i in range(4)])
                        ktm_r = Ring([sb(ph, f"eKtm{i}", [128, 512], BF16) for i in range(4)])
                        v_r = Ring([sb(ph, f"eV{i}", [128, 512], F32) for i in range(4)])
                        vp_r = Ring([sb(ph, f"eVp{i}", [128, 129], BF16) for i in range(8)])
                        sm_r = Ring([sb(ph, f"eSm{i}", [128, 128], BF16) for i in range(8)])
                        dm_r = Ring([sb(ph, f"edm{i}", [128, 2], F32) for i in range(8)])
                        hst_r = Ring([sb(ph, f"ehst{i}", [128, 512], F32) for i in range(4)])
                        masks = [maskf, maskb]

                        def e_load(u, d):
                            k = u if d == 0 else k_of_ku_b(u)
                            QT, bQT = qt_r.next()
                            KT, bKT = kt_r.next()
                            Ktm, bKtm = ktm_r.next()
                            V, bV = v_r.next()
                            S.dma("sp", QT[:], qmT_d[:, :, k * 128:(k + 1) * 128].rearrange("h p s -> p h s"), reads=[B_["qmT"]], writes=[bQT])
                            S.dma("sp", KT[:], kmT_d[:, :, k * 128:(k + 1) * 128].rearrange("h p s -> p h s"), reads=[B_["kmT"]], writes=[bKT])
                            S.dma("sp", Ktm[:], kmtm_d[k * 128:(k + 1) * 128, :], reads=[B_["kmtm"]], writes=[bKtm])
                            S.dma("sp", V[:], pv_d[k * 128:(k + 1) * 128, :], reads=[B_["pv"]], writes=[bV])
                            return (k, QT, bQT, KT, bKT, Ktm, bKtm, V, bV)

                        steps = [(u, d) for u in range(NT) for d in range(2)]
                        (bS_, bbS), (bOa, bbOa), (bOb, bbOb), (bCa, bbCa) = [(psE.tiles[i], psE.bufs[i]) for i in range(4)]

                        def pO_of(h):
                            return (bOa, bbOa, h * 129) if h < 3 else (bOb, bbOb, 0)

                        def pC_of(h):
                            return (bCa, bbCa, h * 129) if h < 3 else (bOb, bbOb, 129)

                        nxt = e_load(*steps[0])
                        for si, (u, d) in enumerate(steps):
                            (k, QT, bQT, KT, bKT, Ktm, bKtm, V, bV) = nxt
                            if si + 1 < len(steps):
                                nxt = e_load(*steps[si + 1])
                            S.flush()
                            hst, bhst = hst_r.next()
                            Vps, Sms, dms = [], [], []
                            for h in range(4):
                                j = d * 4 + h
                                Vp, bVp = vp_r.next()
                                S.op("pool", lambda e: e.tensor_scalar(out=Vp[:, 0:128], in0=V[:, h * 128:(h + 1) * 128], scalar1=ea_tm[:, k, j:j + 1], scalar2=None, op0=ALU.mult), reads=[bV, bea], writes=[bVp])
                                S.op("pool", lambda e: e.tensor_copy(out=Vp[:, 128:129], in_=ea_tm[:, k, j:j + 1]), reads=[bea], writes=[bVp])
                                S.op("pe", lambda e: e.matmul(bS_[:, h * 128:(h + 1) * 128], lhsT=KT[:, h, :], rhs=QT[:, h, :], start=True, stop=True), reads=[bKT, bQT], writes=[bbS])
                                Vps.append((Vp, bVp))
                            yield
                            for h in range(4):
                                Sm, bSm = sm_r.next()
                                S.op("dve", lambda e: e.tensor_tensor(out=Sm[:], in0=bS_[:, h * 128:(h + 1) * 128], in1=masks[d][:], op=ALU.mult), reads=[bbS, b_const], writes=[bSm])
                                Sms.append((Sm, bSm))
                            yield
                            for h in range(4):
                                pO, bpO, o = pO_of(h)
                                S.op("pe", lambda e: e.matmul(pO[:, o:o + 129], lhsT=Sms[h][0][:], rhs=Vps[h][0][:, 0:129], start=True, stop=False), reads=[Sms[h][1], Vps[h][1]], writes=[bpO])
                                S.op("pe", lambda e: e.matmul(pO[:, o:o + 129], lhsT=QT[:, h, :], rhs=Cb[d][h][:, 0:129], start=False, stop=True), reads=[bQT, bCb[d][h]], writes=[bpO])
                            yield
                            for h in range(4):
                                pO, bpO, o = pO_of(h)
                                dm, bdm = dm_r.next()
                                S.op("dve", lambda e: e.tensor_copy(out=dm[:, 0:1], in_=pO[:, o + 128:o + 129]), reads=[bpO], writes=[bdm])
                                dms.append((dm, bdm))
                            yield
                            for h in range(4):
                                j = d * 4 + h
                                dm, bdm = dms[h]
                                S.op("dve", lambda e: e.scalar_tensor_tensor(out=dm[:, 1:2], in0=dm[:, 0:1], scalar=-1.0, in1=dm[:, 0:1], op0=ALU.mult, op1=ALU.max), reads=[bdm], writes=[bdm])
                                S.op("dve", lambda e: e.tensor_tensor(out=dm[:, 1:2], in0=dm[:, 1:2], in1=fl_tm[:, k, j:j + 1], op=ALU.max), reads=[bdm, bfl], writes=[bdm])
                                S.op("dve", lambda e: e.reciprocal(out=dm[:, 1:2], in_=dm[:, 1:2]), reads=[bdm], writes=[bdm])
                            yield
                            for h in range(4):
                                pO, bpO, o = pO_of(h)
                                S.op("dve", lambda e: e.tensor_scalar(out=hst[:, h * 128:(h + 1) * 128], in0=pO[:, o:o + 128], scalar1=dms[h][0][:, 1:2], scalar2=None, op0=ALU.mult), reads=[bpO, dms[h][1]], writes=[bhst])
                            for h in range(4):
                                pC, bpC, o = pC_of(h)
                                S.op("pe", lambda e: e.matmul(pC[:, o:o + 129], lhsT=Ktm[:, h * 128:(h + 1) * 128], rhs=Vps[h][0][:, 0:129], start=True, stop=True), reads=[bKtm, Vps[h][1]], writes=[bpC])
                            yield
                            for h in range(4):
                                pC, bpC, o = pC_of(h)
                                S.op("dve", lambda e: e.tensor_tensor(out=C32[d][h][:], in0=pC[:, o:o + 129], in1=C32[d][h][:], op=ALU.add), reads=[bpC, bC32[d][h]], writes=[bC32[d][h]])
                            yield
                            for h in range(4):
                                j = d * 4 + h
                                S.op("pool", lambda e: e.tensor_scalar(out=Cb[d][h][:], in0=C32[d][h][:], scalar1=decay_bc[:, j, u:u + 1], scalar2=None, op0=ALU.mult), reads=[bC32[d][h], bdec], writes=[bCb[d][h]])
                            yield
                            for h in range(4):
                                j = d * 4 + h
                                S.op("dve", lambda e: e.tensor_scalar(out=C32[d][h][:], in0=C32[d][h][:], scalar1=decay_bc[:, j, u:u + 1], scalar2=None, op0=ALU.mult),
                                     reads=[bC32[d][h], bdec], writes=[bC32[d][h]])
                            dst = hf_d if d == 0 else hb_d
                            bn = "hf" if d == 0 else "hb"
                            S.defer(lambda dst=dst, k=k, hst=hst, bhst=bhst, bn=bn: S.dma("sp", dst[k * 128:(k + 1) * 128, :], hst[:], reads=[bhst], writes=[B_[bn]]))
                            yield
                        S.barrier()
                with ExitStack() as ph:
                    psF = Ring([ps(ph, f"psF{i}", [128, 1024], BF16) for i in range(2)], excl=True)
                    nw_bc, bnw = load_row_bc(ph, "nw_bc", m_norm_w_d[l:l + 1, :], 512)
                    hf_r = Ring([sb(ph, f"fhf{i}", [128, 512], F32) for i in range(3)])
                    hb_r = Ring([sb(ph, f"fhb{i}", [128, 512], F32) for i in range(3)])
                    so_r = Ring([sb(ph, f"fso{i}", [128, 512], F32) for i in range(3)])
                    sq_r = Ring([sb(ph, f"fsq{i}", [128, 512], F32) for i in range(2)])
                    st_r = Ring([sb(ph, f"fst{i}", [128, 16], F32) for i in range(3)])
                    mo_r = Ring([sb(ph, f"fmo{i}", [128, 512], BF16) for i in range(2)])
                    mt_r = Ring([sb(ph, f"fmt{i}", [128, 4, 128], BF16) for i in range(2)])
                    ftiles = list(range(0 if ctx_out else 2, NT))

                    def f_load(k):
                        a, ba = hf_r.next()
                        b, bb = hb_r.next()
                        c, bc = so_r.next()
                        S.dma("sp", a[:], hf_d[k * 128:(k + 1) * 128, :], reads=[B_["hf"]], writes=[ba])
                        S.dma("sp", b[:], hb_d[k * 128:(k + 1) * 128, :], reads=[B_["hb"]], writes=[bb])
                        S.dma("sp", c[:], so_d[k * 128:(k + 1) * 128, :], reads=[B_["so"]], writes=[bc])
                        return (a, ba, b, bb, c, bc)

                    nxt = f_load(ftiles[0])
                    for fi, k in enumerate(ftiles):
                        (a, ba, b, bb, c, bc) = nxt
                        if fi + 1 < len(ftiles):
                            nxt = f_load(ftiles[fi + 1])
                        S.flush()
                        yield
                        st, bst = st_r.next()
                        sq, bsq = sq_r.next()
                        a3 = a[:].rearrange("p (h c) -> p h c", c=128)
                        sq3 = sq[:].rearrange("p (h c) -> p h c", c=128)
                        S.op("dve", lambda e: e.tensor_tensor(out=a[:], in0=a[:], in1=b[:], op=ALU.add), reads=[ba, bb], writes=[ba])
                        S.op("dve", lambda e: e.tensor_reduce(out=st[:, 0:4], in_=a3, axis=AX.X, op=ALU.add), reads=[ba], writes=[bst])
                        S.op("dve", lambda e: e.tensor_scalar(out=st[:, 0:4], in0=st[:, 0:4], scalar1=-1.0 / 128, scalar2=None, op0=ALU.mult), reads=[bst], writes=[bst])
                        S.op("dve", lambda e: e.tensor_tensor(out=a3, in0=a3, in1=st[:, 0:4].rearrange("p (h o) -> p h o", o=1).broadcast_to([128, 4, 128]), op=ALU.add), reads=[ba, bst], writes=[ba])
                        yield
                        S.op("act", lambda e: e.activation(out=sq[:], in_=a[:], func=AF.Square), reads=[ba], writes=[bsq])
                        yield
                        S.op("dve", lambda e: e.tensor_reduce(out=st[:, 4:8], in_=sq3, axis=AX.X, op=ALU.add), reads=[bsq], writes=[bst])
                        S.op("dve", lambda e: e.tensor_scalar(out=st[:, 4:8], in0=st[:, 4:8], scalar1=1.0 / 128, scalar2=1e-6, op0=ALU.mult, op1=ALU.add), reads=[bst], writes=[bst])
                        yield
                        S.op("act", lambda e: e.activation(out=st[:, 4:8], in_=st[:, 4:8], func=AF.Sqrt), reads=[bst], writes=[bst])
                        yield
                        S.op("dve", lambda e: e.reciprocal(out=st[:, 4:8], in_=st[:, 4:8]), reads=[bst], writes=[bst])
                        S.op("dve", lambda e: e.tensor_tensor(out=a3, in0=a3, in1=st[:, 4:8].rearrange("p (h o) -> p h o", o=1).broadcast_to([128, 4, 128]), op=ALU.mult), reads=[ba, bst], writes=[ba])
                        S.op("pool", lambda e: e.tensor_tensor(out=c[:], in0=c[:], in1=nw_bc[:], op=ALU.mult), reads=[bc, bnw], writes=[bc])
                        mo, bmo = mo_r.next()
                        S.op("dve", lambda e: e.tensor_tensor(out=mo[:], in0=a[:], in1=c[:], op=ALU.mult), reads=[ba, bc], writes=[bmo])
                        yield
                        pt, pb = psF.next()
                        for cch in range(4):
                            S.op("pe", lambda e: e.transpose(out=pt[:, cch * 128:(cch + 1) * 128], in_=mo[:, cch * 128:(cch + 1) * 128], identity=identb[:]), reads=[bmo, b_const], writes=[pb])
                        yield
                        mt, bmt = mt_r.next()
                        S.op("act", lambda e: e.copy(out=mt[:], in_=pt[:, 0:512].rearrange("p (c t) -> p c t", t=128)), reads=[pb], writes=[bmt])
                        S.defer(lambda k=k, mt=mt, bmt=bmt: S.dma("sp", moT_d.rearrange("(c p) s -> p c s", p=128)[:, :, k * 128:(k + 1) * 128], mt[:], reads=[bmt], writes=[B_["moT"]]))
                    S.barrier()
            with ExitStack() as ph:
                psS = Ring([ps(ph, f"psS{i}", [128, 512], F32) for i in range(2)], excl=True)
                psO = Ring([ps(ph, f"psO{i}", [128, 512], F32) for i in range(1)], excl=True)
                psL = Ring([ps(ph, f"psL{i}", [128, 512], F32) for i in range(1)], excl=True)
                for e_ in range(NE):
                    for (src, dst) in ((w_gate_d, wgb_d), (w_up_d, wub_d), (w_down_d, wdb_d)):
                        S.dma("pool", dst[l, e_], src[l, e_], writes=[B_["wcast"]])
                krT = sb(ph, "g_krT", [64, S_], BF16)
                bkr = Buf()
                S.dma("sp", krT[:], krT_d, reads=[B_["krT"]], writes=[bkr])
                kn_r = Ring([sb(ph, f"g_kn{i}", [128, S_], BF16) for i in range(1)])
                va_r = Ring([sb(ph, f"g_va{i}", [128, NT, 128], BF16) for i in range(1)])
                qn_r = Ring([sb(ph, f"g_qn{i}", [128, 512], BF16) for i in range(3)])
                qr_r = Ring([sb(ph, f"g_qr{i}", [64, 512], BF16) for i in range(3)])
                pT_r = Ring([sb(ph, f"g_pT{i}", [128, 512], BF16) for i in range(4)])
                rec_r = Ring([sb(ph, f"g_rec{i}", [128, 512], F32) for i in range(2)])
                ao_r = Ring([sb(ph, f"g_ao{i}", [128, 512], BF16) for i in range(2)])
                qblocks = [(CTX + 512 * i, 512, 0, NT) for i in range(T // 512)]
                if ctx_out:
                    qblocks = [(0, CTX, 0, 2)] + qblocks
                work = [(h, qb) for h in range(4) for qb in qblocks]

                def g_loadh(h):
                    kn, bkn = kn_r.next()
                    va, bva = va_r.next()
                    S.dma("sp", kn[:], knT_d[h], reads=[B_["knT"]], writes=[bkn])
                    S.dma("sp", va[:], va_d[:, h * 128:(h + 1) * 128].rearrange("(t p) c -> p t c", p=128), reads=[B_["va"]], writes=[bva])
                    return (kn, bkn, va, bva)

                def g_loadq(h, qb):
                    t0, nb, k0, k1 = qb
                    qn, bqn = qn_r.next()
                    qr, bqr = qr_r.next()
                    S.dma("sp", qn[:, :nb], qT_d[h, 0:128, t0:t0 + nb], reads=[B_["qT"]], writes=[bqn])
                    S.dma("sp", qr[:, :nb], qT_d[h, 128:192, t0:t0 + nb], reads=[B_["qT"]], writes=[bqr])
                    return (qn, bqn, qr, bqr)

                hl = {}
                nq = g_loadq(*work[0])
                gen = xstream()
                gstep = [0]
                for wi, (h, qb) in enumerate(work):
                    t0, nb, k0, k1 = qb
                    (qn, bqn, qr, bqr) = nq
                    if wi + 1 < len(work):
                        nq = g_loadq(*work[wi + 1])
                    if h not in hl:
                        hl[h] = g_loadh(h)
                    S.flush()
                    (kn, bkn, va, bva) = hl[h]
                    pO, bpO = psO.next()
                    pL, bpL = psL.next()
                    def g_qk(kt):
                        pS, bpS = psS.next()
                        S.op("pe", lambda e: e.matmul(pS[:, :nb], lhsT=kn[:, kt * 128:(kt + 1) * 128], rhs=qn[:, :nb], start=True, stop=False), reads=[bkn, bqn], writes=[bpS])
                        S.op("pe", lambda e: e.matmul(pS[:, :nb], lhsT=krT[:, kt * 128:(kt + 1) * 128], rhs=qr[:, :nb], start=False, stop=True), reads=[bkr, bqr], writes=[bpS])
                        return (pS, bpS)

                    cur = g_qk(k0)
                    for kt in range(k0, k1):
                        pS, bpS = cur
                        if kt + 1 < k1:
                            cur = g_qk(kt + 1)
                        gstep[0] += 1
                        if gstep[0] % 2 == 0:
                            next(gen, None)
                        pT, bpT = pT_r.next()
                        S.op("act", lambda e: e.activation(out=pT[:, :nb], in_=pS[:, :nb], func=AF.Exp), reads=[bpS], writes=[bpT])
                        S.op("pe", lambda e: e.matmul(pO[:, :nb], lhsT=va[:, kt, :], rhs=pT[:, :nb], start=(kt == k0), stop=(kt == k1 - 1)), reads=[bva, bpT], writes=[bpO])
                        S.op("pe", lambda e: e.matmul(pL[:, :nb], lhsT=onesb[:], rhs=pT[:, :nb], start=(kt == k0), stop=(kt == k1 - 1)), reads=[b_const, bpT], writes=[bpL])
                    rec, brec = rec_r.next()
                    ao, bao = ao_r.next()
                    S.op("dve", lambda e: e.reciprocal(out=rec[:, :nb], in_=pL[:, :nb]), reads=[bpL], writes=[brec])
                    S.op("dve", lambda e: e.tensor_tensor(out=ao[:, :nb], in0=pO[:, :nb], in1=rec[:, :nb], op=ALU.mult), reads=[bpO, brec], writes=[bao])
                    S.defer(lambda h=h, t0=t0, nb=nb, ao=ao, bao=bao: S.dma("sp", aoT_d[h * 128:(h + 1) * 128, t0:t0 + nb], ao[:, :nb], reads=[bao], writes=[B_["aoT"]]))
                for _ in gen:
                    pass
                S.barrier()
            if stop_after == "G":
                break

            hblocks = [b for b in blocks if ctx_out or b[0] != 0]
            with ExitStack() as phI:
                wgt_tm = sb(phI, "wgt_tm", [128, NT, NE], F32)
                bwgt = Buf()
                with ExitStack() as ph:
                    psH = Ring([ps(ph, f"psH{i}", [128, 512], F32) for i in range(8)], excl=True)
                    w_out_b = sb(ph, "w_out_b", [128, 8, 1024], BF16)
                    bwo = Buf()
                    S.dma("pool", w_out_b[:], w_out_d[l].rearrange("(c p) n -> p c n", p=128), writes=[bwo])
                    wr32 = sb(ph, "wr32", [128, 8, NE], F32)
                    bwr = Buf()
                    S.dma("sp", wr32[:], w_router_d[l].rearrange("(c p) n -> p c n", p=128), writes=[bwr])
                    g2x, bg2x = modrow_bc(ph, "g2x", 0, 2)
                    ln1g, bl1g = load_row_bc(ph, "ln1g", ln1_g_d[l:l + 1, :], D)
                    ln1b, bl1b = load_row_bc(ph, "ln1b", ln1_b_d[l:l + 1, :], D)
                    sh3x, bsh3x = modcol(ph, "sh3x", 0, 3)
                    sc4x, bsc4x = modcol(ph, "sc4x", 0, 4, True)
                    sh3r, bsh3r = modrow_bc(ph, "sh3r", 0, 3)
                    sc4r, bsc4r = modrow_bc(ph, "sc4r", 0, 4)
                    S.op("dve", lambda e: e.tensor_scalar(out=sc4r[:], in0=sc4r[:], scalar1=1.0, scalar2=None, op0=ALU.add), reads=[bsc4r], writes=[bsc4r])
                    h2tm_r = Ring([sb(ph, f"h_h2tm{i}", [128, D], BF16) for i in range(2)])
                    if ctx_out:
                        g2c, bg2c = modrow_bc(ph, "g2c", 1, 2)
                        sh3c, bsh3c = modcol(ph, "sh3c", 1, 3)
                        sc4c, bsc4c = modcol(ph, "sc4c", 1, 4, True)
                    mo_r = Ring([sb(ph, f"h_mo{i}", [128, 4, 512], BF16) for i in range(2)])
                    ao_r = Ring([sb(ph, f"h_ao{i}", [128, 4, 512], BF16) for i in range(2)])
                    x_r = Ring([sb(ph, f"h_x{i}", [128, D], F32) for i in range(8)])
                    tmp_r = Ring([sb(ph, f"h_tmp{i}", [128, D], F32) for i in range(2)])
                    xn2_r = Ring([sb(ph, f"h_xn2{i}", [128, D], F32) for i in range(4)])
                    stm_r = Ring([sb(ph, f"h_stm{i}", [128, 5, 4], F32) for i in range(3)])
                    junk = sb(ph, "h_junk", [128, D], BF16)
                    bjunk = Buf()
                    h2T32 = sb(ph, "h2T32", [128, 8, 512], F32)
                    h2Tb = sb(ph, "h2Tb", [128, 8, 512], BF16)
                    bh32, bhb = Buf(), Buf()
                    sm_r = Ring([sb(ph, f"h_sm{i}", [128, 24], F32) for i in range(3)])
                    affTs = sb(ph, "affTs", [NE, 512], F32)
                    baffTs = Buf()
                    print("phase H sbuf remaining", nc.sbuf_bytes_remaining)

                    def h_load(bi):
                        k0, ntile = hblocks[bi]
                        nb = ntile * 128
                        t0 = k0 * 128
                        mo, bmo = mo_r.next()
                        ao, bao = ao_r.next()
                        S.dma("sp", mo[:, :, :nb], moT_d.rearrange("(c p) s -> p c s", p=128)[:, :, t0:t0 + nb], reads=[B_["moT"]], writes=[bmo])
                        S.dma("sp", ao[:, :, :nb], aoT_d.rearrange("(c p) s -> p c s", p=128)[:, :, t0:t0 + nb], reads=[B_["aoT"]], writes=[bao])
                        xs = []
                        for ti in range(ntile):
                            xt, bx = x_r.next()
                            S.dma("sp", xt[:], tile_src(l, k0 + ti), reads=[B_["x2"]], writes=[bx])
                            xs.append((xt, bx))
                        return (mo, bmo, ao, bao, xs)

                    nxt = h_load(0)
                    for bi, (k0, ntile) in enumerate(hblocks):
                        (mo, bmo, ao, bao, xs) = nxt
                        S.flush()
                        if bi + 1 < len(hblocks):
                            nxt = h_load(bi + 1)
                        is_ctx = (k0 == 0)
                        nb = ntile * 128
                        t0 = k0 * 128
                        g2, bg2 = (g2c, bg2c) if is_ctx else (g2x, bg2x)
                        sh3, bsh3 = (sh3c, bsh3c) if is_ctx else (sh3x, bsh3x)
                        sc4, bsc4 = (sc4c, bsc4c) if is_ctx else (sc4x, bsc4x)
                        xn2s = []
                        for ti in range(ntile):
                            xt, bx = xs[ti]
                            tmp, btmp = tmp_r.next()
                            for half in range(2):
                                pt, pb = psH.next()
                                for c in range(8):
                                    src, bsrc = (mo, bmo) if c < 4 else (ao, bao)
                                    S.op("pe", lambda e: e.matmul(pt[:, :], lhsT=src[:, c % 4, ti * 128:(ti + 1) * 128], rhs=w_out_b[:, c, half * 512:(half + 1) * 512], start=(c == 0), stop=(c == 7)),
                                         reads=[bsrc, bwo], writes=[pb])
                                S.op("dve", lambda e: e.tensor_tensor(out=tmp[:, half * 512:(half + 1) * 512], in0=pt[:, :], in1=g2[:, half * 512:(half + 1) * 512], op=ALU.mult),
                                     reads=[pb, bg2], writes=[btmp])
                            S.op("dve", lambda e: e.scalar_tensor_tensor(out=xt[:], in0=xt[:], scalar=ALPHA, in1=tmp[:], op0=ALU.mult, op1=ALU.add), reads=[bx, btmp], writes=[bx])
                        stm, bstm = stm_r.next()
                        ln_stats_multi([(xs[ti][0][:], xs[ti][1]) for ti in range(ntile)], 1e-5, stm, bstm, junk[:], bjunk)
                        for ti in range(ntile):
                            k = k0 + ti
                            xt, bx = xs[ti]
                            S.op("act", lambda e: e.activation(out=xt[:], in_=xt[:], func=AF.Identity, bias=stm[:, 2, ti:ti + 1], scale=1.0), reads=[bx, bstm], writes=[bx])
                            S.op("dve", lambda e: e.scalar_tensor_tensor(out=xt[:], in0=xt[:], scalar=stm[:, 4, ti:ti + 1], in1=ln1g[:], op0=ALU.mult, op1=ALU.mult), reads=[bx, bstm, bl1g], writes=[bx])
                            S.op("dve", lambda e: e.tensor_tensor(out=xt[:], in0=xt[:], in1=ln1b[:], op=ALU.add), reads=[bx, bl1b], writes=[bx])
                            S.defer(lambda k=k, xt=xt, bx=bx: S.dma("sp", x1_d[k * 128:(k + 1) * 128, :], xt[:], reads=[bx], writes=[B_["x1"]]))
                        stm2, bstm2 = stm_r.next()
                        ln_stats_multi([(xs[ti][0][:], xs[ti][1]) for ti in range(ntile)], 1e-6, stm2, bstm2, junk[:], bjunk)
                        for ti in range(ntile):
                            k = k0 + ti
                            xt, bx = xs[ti]
                            xn2, bxn2 = xn2_r.next()
                            S.op("dve", lambda e: e.tensor_scalar(out=xn2[:], in0=xt[:], scalar1=stm2[:, 2, ti:ti + 1], scalar2=stm2[:, 4, ti:ti + 1], op0=ALU.add, op1=ALU.mult), reads=[bx, bstm2], writes=[bxn2])
                            xn2s.append((xn2, bxn2))
                            if not is_ctx:
                                tmp2, btmp2 = tmp_r.next()
                                h2tm, bh2tm = h2tm_r.next()
                                S.op("dve", lambda e: e.tensor_tensor(out=tmp2[:], in0=xn2[:], in1=sc4r[:], op=ALU.mult), reads=[bxn2, bsc4r], writes=[btmp2])
                                S.op("dve", lambda e: e.tensor_tensor(out=h2tm[:], in0=tmp2[:], in1=sh3r[:], op=ALU.add), reads=[btmp2, bsh3r], writes=[bh2tm])
                                S.dma("sp", h2tm_d[k * 128:(k + 1) * 128, :], h2tm[:], reads=[bh2tm], writes=[B_["h2tm"]])
                        for kc in range(8):
                            pt, pb = psH.next()
                            for ti in range(ntile):
                                xn2, bxn2 = xn2s[ti]
                                S.op("pe", lambda e: e.transpose(out=pt[:, ti * 128:(ti + 1) * 128], in_=xn2[:, kc * 128:(kc + 1) * 128], identity=ident[:]), reads=[bxn2, b_const], writes=[pb])
                            S.op("act", lambda e: e.activation(out=h2T32[:, kc, :nb], in_=pt[:, :nb], func=AF.Identity, bias=sh3[:, kc:kc + 1], scale=sc4[:, kc:kc + 1]),
                                 reads=[pb, bsh3, bsc4], writes=[bh32])
                        S.op("pool", lambda e: e.tensor_copy(out=h2Tb[:, :, :nb], in_=h2T32[:, :, :nb]), reads=[bh32], writes=[bhb])
                        S.dma("sp", h2T_d.rearrange("(c p) s -> p c s", p=128)[:, :, t0:t0 + nb], h2Tb[:, :, :nb], reads=[bhb], writes=[B_["h2T"]])
                        for ti in range(ntile):
                            pt, pb = psH.next()
                            for kc in range(8):
                                S.op("pe", lambda e: e.matmul(pt[:, 0:NE], lhsT=h2T32[:, kc, ti * 128:(ti + 1) * 128], rhs=wr32[:, kc, :], start=(kc == 0), stop=(kc == 7)), reads=[bh32, bwr], writes=[pb])
                            sm, bsm = sm_r.next()
                            S.op("dve", lambda e: e.tensor_reduce(out=sm[:, 16:17], in_=pt[:, 0:NE], axis=AX.X, op=ALU.max), reads=[pb], writes=[bsm])
                            S.op("dve", lambda e: e.tensor_scalar(out=sm[:, 16:17], in0=sm[:, 16:17], scalar1=-1.0, scalar2=None, op0=ALU.mult), reads=[bsm], writes=[bsm])
                            S.op("act", lambda e: e.activation(out=sm[:, 0:NE], in_=pt[:, 0:NE], func=AF.Exp, bias=sm[:, 16:17], accum_out=sm[:, 17:18]), reads=[pb, bsm], writes=[bsm])
                            S.op("dve", lambda e: e.reciprocal(out=sm[:, 17:18], in_=sm[:, 17:18]), reads=[bsm], writes=[bsm])
                            S.op("dve", lambda e: e.tensor_scalar(out=sm[:, 0:NE], in0=sm[:, 0:NE], scalar1=sm[:, 17:18], scalar2=None, op0=ALU.mult), reads=[bsm], writes=[bsm])
                            S.dma("sp", afftm_d[(k0 + ti) * 128:(k0 + ti + 1) * 128, :], sm[:, 0:NE], reads=[bsm], writes=[B_["afftm"]])
                            pt2, pb2 = psH.next()
                            S.op("pe", lambda e: e.transpose(out=pt2[0:NE, 0:128], in_=sm[:, 0:NE], identity=ident[:]), reads=[bsm, b_const], writes=[pb2])
                            S.op("act", lambda e: e.copy(out=affTs[:, ti * 128:(ti + 1) * 128], in_=pt2[0:NE, 0:128]), reads=[pb2], writes=[baffTs])
                        S.dma("sp", affT_d[:, t0:t0 + nb], affTs[:, :nb], reads=[baffTs], writes=[B_["affT"]])
                    S.barrier()
                if stop_after == "H":
                    break
                with ExitStack() as ph:
                    psI = Ring([ps(ph, f"psI{i}", [128, 512], F32) for i in range(2)], excl=True)
                    affT = sb(ph, "i_affT", [NE, S_], F32)
                    wT = sb(ph, "i_wT", [NE, S_], F32)
                    junkI = sb(ph, "i_junk", [NE, T], F16)
                    baff, bwT, bjk = Buf(), Buf(), Buf()
                    S.dma("sp", affT[:], affT_d, reads=[B_["affT"]], writes=[baff])
                    if not ctx_out:
                        S.op("dve", lambda e: e.memset(wT[:, 0:CTX], 0.0), writes=[bwT])
                    sets = [(CTX, S_, T // 8)]
                    if ctx_out:
                        sets = [(0, CTX, CTX // 8)] + sets
                    for (a, b, kcap) in sets:
                        n = b - a
                        cs = sb(ph, f"i_cs{a}", [NE, 8], F32)
                        bcs = Buf()
                        S.op("dve", lambda e: e.memset(cs[:, 0:1], 0.0), writes=[bcs])
                        S.op("dve", lambda e: e.memset(cs[:, 1:2], 1.0), writes=[bcs])
                        for it in range(32):
                            S.op("dve", lambda e: e.tensor_scalar(out=cs[:, 2:3], in0=cs[:, 0:1], scalar1=0.5, scalar2=None, op0=ALU.mult), reads=[bcs], writes=[bcs])
                            S.op("dve", lambda e: e.scalar_tensor_tensor(out=cs[:, 2:3], in0=cs[:, 1:2], scalar=0.5, in1=cs[:, 2:3], op0=ALU.mult, op1=ALU.add), reads=[bcs], writes=[bcs])
                            S.op("dve", lambda e: e.memset(cs[:, 3:4], 0.0), writes=[bcs])
                            S.op("dve", lambda e: e.tensor_scalar(out=junkI[:, :n], in0=affT[:, a:b], scalar1=cs[:, 2:3], scalar2=0.0, op0=ALU.is_ge, op1=ALU.add, accum_out=cs[:, 3:4]),
                                 reads=[baff, bcs], writes=[bjk, bcs])
                            S.op("dve", lambda e: e.tensor_scalar(out=cs[:, 4:5], in0=cs[:, 3:4], scalar1=float(kcap) - 0.5, scalar2=None, op0=ALU.is_ge), reads=[bcs], writes=[bcs])
                            S.op("dve", lambda e: e.tensor_tensor(out=cs[:, 5:6], in0=cs[:, 2:3], in1=cs[:, 0:1], op=ALU.subtract), reads=[bcs], writes=[bcs])
                            S.op("dve", lambda e: e.scalar_tensor_tensor(out=cs[:, 0:1], in0=cs[:, 5:6], scalar=cs[:, 4:5], in1=cs[:, 0:1], op0=ALU.mult, op1=ALU.add), reads=[bcs], writes=[bcs])
                            S.op("dve", lambda e: e.tensor_tensor(out=cs[:, 5:6], in0=cs[:, 1:2], in1=cs[:, 2:3], op=ALU.subtract), reads=[bcs], writes=[bcs])
                            S.op("dve", lambda e: e.scalar_tensor_tensor(out=cs[:, 1:2], in0=cs[:, 5:6], scalar=cs[:, 4:5], in1=cs[:, 2:3], op0=ALU.mult, op1=ALU.add), reads=[bcs], writes=[bcs])
                        if a == 0:
                            S.op("dve", lambda e: e.scalar_tensor_tensor(out=wT[:, a:b], in0=affT[:, a:b], scalar=cs[:, 0:1], in1=affT[:, a:b], op0=ALU.is_ge, op1=ALU.mult), reads=[baff, bcs], writes=[bwT])
                        else:
                            S.op("dve", lambda e: e.tensor_scalar(out=wT[:, a:b], in0=affT[:, a:b], scalar1=cs[:, 0:1], scalar2=None, op0=ALU.is_ge), reads=[baff, bcs], writes=[bwT])
                            S.op("dve", lambda e: e.tensor_tensor_scan(out=affT[:, a:b], data0=wT[:, a:b], data1=zcol[0:NE, 0:1].broadcast_to([NE, n]), initial=0.0, op0=ALU.add, op1=ALU.add),
                                 reads=[bwT, b_const, baff], writes=[baff])
                            S.op("dve", lambda e: e.tensor_copy(out=junkI[:, :n], in_=affT[:, a:b]), reads=[baff, bjk], writes=[bjk])
                            S.dma("sp", cnt_d, junkI[:, :n], reads=[bjk], writes=[B_["cnt"]])
                            tcs = sb(ph, "i_tcs", [NE, NX], F32)
                            btcs = Buf()
                            S.op("dve", lambda e: e.tensor_copy(out=tcs[:].rearrange("p (t o) -> p t o", o=1), in_=affT[:, a:b].rearrange("p (t c) -> p t c", c=128)[:, :, 127:128]),
                                 reads=[baff], writes=[btcs])
                            S.dma("sp", tc_d.rearrange("o (e k) -> (o e) k", e=NE), tcs[:], reads=[btcs], writes=[B_["cnt"]])
                    for k in range(2 if ctx_out else 0):
                        pt, pb = psI.next()
                        S.op("pe", lambda e: e.transpose(out=pt[:, 0:NE], in_=wT[0:NE, k * 128:(k + 1) * 128], identity=ident[0:NE, 0:NE]), reads=[bwT, b_const], writes=[pb])
                        S.op("act", lambda e: e.copy(out=wgt_tm[:, k, :], in_=pt[:, 0:NE]), reads=[pb], writes=[bwgt])
                    S.barrier()
                if stop_after == "I":
                    break
                with ExitStack() as ph:
                    psG = Ring([ps(ph, f"psG{i}", [128, 512], F32) for i in range(3)], excl=True)
                    psY = Ring([ps(ph, f"psY{i}", [128, 512], F32) for i in range(3)], excl=True)
                    psT = Ring([ps(ph, f"psT{i}", [128, 1024], BF16) for i in range(2)], excl=True)
                    w_r = Ring([sb(ph, f"m_w{i}", [128, 8, 1024], BF16) for i in range(4)])
                    act_r = Ring([sb(ph, f"m_act{i}", [128, 8, 512], BF16) for i in range(2)])
                    sg_r = Ring([sb(ph, f"m_sg{i}", [128, 512], F32) for i in range(3)])
                    zt = sb(ph, "m_zt", [128, D], F32)
                    bzt = Buf()
                    S.op("dve", lambda e: e.memset(zt[:], 0.0), writes=[bzt])
                    S.dma("sp", moe_d[CTX:S_, :].rearrange("(t p) d -> p t d", p=128), zt[:].rearrange("p (o d) -> p o d", o=1).broadcast_to([128, NX, D]), reads=[bzt], writes=[B_["moe"]])
                    wsrc = (wgb_d, wub_d, wdb_d)
                    nblk = 2 if ctx_out else 1
                    seq = [(bi, e_, m) for bi in range(nblk) for e_ in range(NE) for m in range(3)]
                    loaded = {}
                    nload = [0]

                    def w_load(i):
                        while nload[0] <= i and nload[0] < len(seq):
                            j = nload[0]
                            bi_, e2, m = seq[j]
                            wt, wb_ = w_r.next()
                            S.dma("sp", wt[:], wsrc[m][l, e2].rearrange("(c p) n -> p c n", p=128), reads=[B_["wcast"]], writes=[wb_])
                            loaded[j] = (wt, wb_)
                            nload[0] += 1

                    def ffn_gate_up(wg, bwg, wu, bwu, h2, bh2, hs, n):
                        act, bact = act_r.next()
                        for fc in range(8):
                            pg, bpg = psG.next()
                            for kc in range(8):
                                S.op("pe", lambda e: e.matmul(pg[:, :n], lhsT=wg[:, kc, fc * 128:(fc + 1) * 128], rhs=h2[:, kc, hs:hs + n], start=(kc == 0), stop=(kc == 7)), reads=[bwg, bh2], writes=[bpg])
                            pu, bpu = psG.next()
                            for kc in range(8):
                                S.op("pe", lambda e: e.matmul(pu[:, :n], lhsT=wu[:, kc, fc * 128:(fc + 1) * 128], rhs=h2[:, kc, hs:hs + n], start=(kc == 0), stop=(kc == 7)), reads=[bwu, bh2], writes=[bpu])
                            sg, bsg = sg_r.next()
                            S.op("act", lambda e: e.activation(out=sg[:, :n], in_=pg[:, :n], func=AF.Silu), reads=[bpg], writes=[bsg])
                            S.op("dve", lambda e: e.tensor_tensor(out=act[:, fc, :n], in0=pu[:, :n], in1=sg[:, :n], op=ALU.mult), reads=[bpu, bsg], writes=[bact])
                        return act, bact

                    w_load(2)
                    si = 0
                    if ctx_out:
                        with ExitStack() as phc:
                            h2c = sb(phc, "m_h2c", [128, 8, CTX], BF16)
                            accc = sb(phc, "m_accc", [128, 2, D], F32)
                            bh2c, baccc = Buf(), Buf()
                            S.dma("sp", h2c[:], h2T_d.rearrange("(c p) s -> p c s", p=128)[:, :, 0:CTX], reads=[B_["h2T"]], writes=[bh2c])
                            for e_ in range(NE):
                                w_load(si + 2)
                                wg, bwg = loaded.pop(si)
                                wu, bwu = loaded.pop(si + 1)
                                wd, bwd = loaded.pop(si + 2)
                                w_load(si + 3)
                                act, bact = ffn_gate_up(wg, bwg, wu, bwu, h2c, bh2c, 0, CTX)
                                w_load(si + 5)
                                for tl in range(2):
                                    for oh in range(2):
                                        py, bpy = psY.next()
                                        for fc in range(8):
                                            S.op("pe", lambda e: e.matmul(py[:, :], lhsT=act[:, fc, tl * 128:(tl + 1) * 128], rhs=wd[:, fc, oh * 512:(oh + 1) * 512], start=(fc == 0), stop=(fc == 7)), reads=[bact, bwd], writes=[bpy])
                                        if e_ == 0:
                                            S.op("dve", lambda e: e.tensor_scalar(out=accc[:, tl, oh * 512:(oh + 1) * 512], in0=py[:, :], scalar1=wgt_tm[:, tl, e_:e_ + 1], scalar2=None, op0=ALU.mult),
                                                 reads=[bpy, bwgt], writes=[baccc])
                                        else:
                                            S.op("dve", lambda e: e.scalar_tensor_tensor(out=accc[:, tl, oh * 512:(oh + 1) * 512], in0=py[:, :], scalar=wgt_tm[:, tl, e_:e_ + 1], in1=accc[:, tl, oh * 512:(oh + 1) * 512], op0=ALU.mult, op1=ALU.add),
                                                 reads=[bpy, bwgt, baccc], writes=[baccc])
                                si += 3
                            S.dma("sp", moe_d[0:CTX, :].rearrange("(t p) d -> p t d", p=128), accc[:], reads=[baccc], writes=[B_["moe"]])
                            S.barrier()
                    tcb = sb(ph, "m_tcb", [128, NE, NX], F32)
                    btcb = Buf()
                    S.dma("sp", tcb[:].rearrange("p e k -> p (e k)"), tc_d.broadcast_to([128, NE * NX]), reads=[B_["cnt"]], writes=[btcb])
                    junkM = sb(ph, "m_junk", [128, 128], F32)
                    bjm = Buf()
                    kf = sb(ph, "m_kf", [128, NE, 8], F32)
                    rowf = sb(ph, "m_rowf", [128, NE, 8], F32)
                    rowi = sb(ph, "m_rowi", [128, NE, 8], I32)
                    posf = sb(ph, "m_posf", [128, NE, 8], F32)
                    ct_r = Ring([sb(ph, f"m_ct{i}", [128, 128], F16) for i in range(8)])
                    cnt2d = cnt_d.rearrange("e (k c) -> (e k) c", c=128)
                    idxf = sb(ph, "m_idxf", [128, NE, 8], F32)
                    idxi = sb(ph, "m_idxi", [128, NE, 8], I32)
                    bidx = [Buf() for _ in range(NE)]
                    gts = sb(ph, "m_gts", [128, NE, 8, NE], F32)
                    bgts = [Buf() for _ in range(NE)]
                    xg_r = Ring([sb(ph, f"m_xg{i}", [128, 4, D], BF16) for i in range(3)])
                    XgT_r = Ring([sb(ph, f"m_XgT{i}", [128, 8, 512], BF16) for i in range(2)])
                    ys_r = Ring([sb(ph, f"m_ys{i}", [128, D], F32) for i in range(4)])
                    print("phase M sbuf remaining", nc.sbuf_bytes_remaining)
                    NSL = T // 8 // 128
                    HPE = max(1, NSL // 4)
                    SPH = NSL // HPE
                    units = [(e_, hh) for e_ in range(NE) for hh in range(HPE)]
                    def stageA(u):
                        e_, hh = units[u]
                        if hh == 0:
                            be = bidx[e_]
                            for j in range(NSL):
                                S.op("dve", lambda e: e.memset(kf[:, e_, j:j + 1], 0.0), writes=[be])
                                S.op("dve", lambda e: e.tensor_scalar(out=junkM[:, 0:NX], in0=tcb[:, e_, :], scalar1=slotf[:, j:j + 1], scalar2=0.0, op0=ALU.is_le, op1=ALU.add, accum_out=kf[:, e_, j:j + 1]),
                                     reads=[btcb, b_const, bjm], writes=[bjm, be])
                            S.op("dve", lambda e: e.tensor_scalar(out=rowf[:, e_, 0:NSL], in0=kf[:, e_, 0:NSL], scalar1=float(e_ * NX), scalar2=None, op0=ALU.add), reads=[be], writes=[be])
                            S.op("dve", lambda e: e.tensor_copy(out=rowi[:, e_, 0:NSL], in_=rowf[:, e_, 0:NSL]), reads=[be], writes=[be])
                            for j in range(NSL):
                                ct, bct = ct_r.next()
                                S.idma(out=ct[:], out_offset=None, in_=cnt2d, in_offset=bass.IndirectOffsetOnAxis(ap=rowi[:, e_, j:j + 1], axis=0), reads=[be, B_["cnt"]], writes=[bct])
                                S.op("dve", lambda e: e.memset(posf[:, e_, j:j + 1], 0.0), writes=[be])
                                S.op("dve", lambda e: e.tensor_scalar(out=junkM[:, 0:128], in0=ct[:], scalar1=slotf[:, j:j + 1], scalar2=0.0, op0=ALU.is_le, op1=ALU.add, accum_out=posf[:, e_, j:j + 1]),
                                     reads=[bct, b_const, bjm], writes=[bjm, be])
                            S.op("dve", lambda e: e.scalar_tensor_tensor(out=idxf[:, e_, 0:NSL], in0=kf[:, e_, 0:NSL], scalar=128.0, in1=posf[:, e_, 0:NSL], op0=ALU.mult, op1=ALU.add), reads=[be], writes=[be])
                            S.op("dve", lambda e: e.tensor_scalar(out=idxf[:, e_, 0:NSL], in0=idxf[:, e_, 0:NSL], scalar1=float(CTX), scalar2=None, op0=ALU.add), reads=[be], writes=[be])
                            S.op("dve", lambda e: e.tensor_copy(out=idxi[:, e_, 0:NSL], in_=idxf[:, e_, 0:NSL]), reads=[be], writes=[be])
                        xg, bxg = xg_r.next()
                        for jj in range(SPH):
                            j = hh * SPH + jj
                            S.idma(out=xg[:, jj, :], out_offset=None, in_=h2tm_d[:, :], in_offset=bass.IndirectOffsetOnAxis(ap=idxi[:, e_, j:j + 1], axis=0),
                                   reads=[bidx[e_], B_["h2tm"]], writes=[bxg])
                            S.idma(out=gts[:, e_, j, :], out_offset=None, in_=afftm_d[:, :], in_offset=bass.IndirectOffsetOnAxis(ap=idxi[:, e_, j:j + 1], axis=0),
                                   reads=[bidx[e_], B_["afftm"]], writes=[bgts[e_]])
                        return (xg, bxg)

                    def stageB(u, xgt):
                        e_, hh = units[u]
                        xg, bxg = xgt
                        XgT, bXgT = XgT_r.next()
                        for kc in range(8):
                            pt, pb = psT.next()
                            for jj in range(SPH):
                                S.op("pe", lambda e: e.transpose(out=pt[:, jj * 128:(jj + 1) * 128], in_=xg[:, jj, kc * 128:(kc + 1) * 128], identity=identb[:]), reads=[bxg, b_const], writes=[pb])
                            if kc % 2:
                                S.op("act", lambda e: e.copy(out=XgT[:, kc, 0:SPH * 128], in_=pt[:, 0:SPH * 128]), reads=[pb], writes=[bXgT])
                            else:
                                S.op("dve", lambda e: e.tensor_copy(out=XgT[:, kc, 0:SPH * 128], in_=pt[:, 0:SPH * 128]), reads=[pb], writes=[bXgT])
                        return (XgT, bXgT)

                    wcur = {}
                    sc_prev = [B_["moe"].last_w]
                    sc_cur = []

                    def stageC(u, XgTt):
                        nonlocal si
                        e_, hh = units[u]
                        XgT, bXgT = XgTt
                        if hh == 0:
                            w_load(si + 2)
                            wcur["g"] = loaded.pop(si)
                            wcur["u"] = loaded.pop(si + 1)
                            wcur["d"] = loaded.pop(si + 2)
                            w_load(si + 3)
                        wg, bwg = wcur["g"]
                        wu, bwu = wcur["u"]
                        wd, bwd = wcur["d"]
                        n = SPH * 128
                        act, bact = ffn_gate_up(wg, bwg, wu, bwu, XgT, bXgT, 0, n)
                        if hh == HPE - 1:
                            w_load(si + 5)
                        for jj in range(SPH):
                            j = hh * SPH + jj
                            ys, bys = ys_r.next()
                            for oh in range(2):
                                py, bpy = psY.next()
                                for fc in range(8):
                                    S.op("pe", lambda e: e.matmul(py[:, :], lhsT=act[:, fc, jj * 128:(jj + 1) * 128], rhs=wd[:, fc, oh * 512:(oh + 1) * 512], start=(fc == 0), stop=(fc == 7)), reads=[bact, bwd], writes=[bpy])
                                if oh == 0:
                                    S.op("act", lambda e: e.activation(out=ys[:, 0:512], in_=py[:, :], func=AF.Copy, scale=gts[:, e_, j, e_:e_ + 1]), reads=[bpy, bgts[e_]], writes=[bys])
                                else:
                                    S.op("dve", lambda e: e.tensor_scalar(out=ys[:, 512:1024], in0=py[:, :], scalar1=gts[:, e_, j, e_:e_ + 1], scalar2=None, op0=ALU.mult), reads=[bpy, bgts[e_]], writes=[bys])
                            for t in sc_prev:
                                S._wait("pool", t)
                            tk = S.idma(out=moe_d[:, :], out_offset=bass.IndirectOffsetOnAxis(ap=idxi[:, e_, j:j + 1], axis=0), in_=ys[:], in_offset=None,
                                        reads=[bys, bidx[e_]], writes=[], compute_op=ALU.add)
                            sc_cur.append(tk)
                        if hh == HPE - 1:
                            sc_prev[:] = sc_cur
                            sc_cur[:] = []
                        if hh == HPE - 1:
                            si += 3

                    xgq = {}
                    XgTq = {}
                    nU = len(units)
                    xgq[0] = stageA(0)
                    if nU > 1:
                        xgq[1] = stageA(1)
                    XgTq[0] = stageB(0, xgq.pop(0))
                    for u in range(nU):
                        if u + 2 < nU:
                            xgq[u + 2] = stageA(u + 2)
                        if u + 1 < nU:
                            XgTq[u + 1] = stageB(u + 1, xgq.pop(u + 1))
                        stageC(u, XgTq.pop(u))
                    S.barrier()
                with ExitStack() as ph:
                    g5x, bg5x = modrow_bc(ph, "g5x", 0, 5)
                    if ctx_out:
                        g5c, bg5c = modrow_bc(ph, "g5c", 1, 5)
                    ln2g, bl2g = load_row_bc(ph, "ln2g", ln2_g_d[l:l + 1, :], D)
                    ln2b, bl2b = load_row_bc(ph, "ln2b", ln2_b_d[l:l + 1, :], D)
                    x1_r = Ring([sb(ph, f"p_x1{i}", [128, D], F32) for i in range(4)])
                    mo_r = Ring([sb(ph, f"p_mo{i}", [128, D], F32) for i in range(4)])
                    st_r = Ring([sb(ph, f"p_st{i}", [128, 8], F32) for i in range(3)])
                    junk = sb(ph, "p_junk", [128, D], BF16)
                    bjunk = Buf()
                    ptiles = list(range(0 if ctx_out else 2, NT))

                    def p_load(k):
                        xt, bx = x1_r.next()
                        mt, bm = mo_r.next()
                        S.dma("sp", xt[:], x1_d[k * 128:(k + 1) * 128, :], reads=[B_["x1"]], writes=[bx])
                        S.dma("sp", mt[:], moe_d[k * 128:(k + 1) * 128, :], reads=[B_["moe"]], writes=[bm])
                        return (xt, bx, mt, bm)

                    nxt = p_load(ptiles[0])
                    for pi, k in enumerate(ptiles):
                        (xt, bx, mt, bm) = nxt
                        S.flush()
                        if pi + 1 < len(ptiles):
                            nxt = p_load(ptiles[pi + 1])
                        g5, bg5 = (g5c, bg5c) if k < 2 else (g5x, bg5x)
                        S.op("dve", lambda e: e.tensor_tensor(out=mt[:], in0=mt[:], in1=g5[:], op=ALU.mult), reads=[bm, bg5], writes=[bm])
                        S.op("dve", lambda e: e.scalar_tensor_tensor(out=xt[:], in0=xt[:], scalar=ALPHA, in1=mt[:], op0=ALU.mult, op1=ALU.add), reads=[bx, bm], writes=[bx])
                        st8 = st_r.next()
                        ln_stats(st8, xt[:], bx, 1e-5, junk[:], bjunk)
                        S.op("act", lambda e: e.activation(out=xt[:], in_=xt[:], func=AF.Identity, bias=st8[0][:, 2:3], scale=1.0), reads=[bx, st8[1]], writes=[bx])
                        S.op("dve", lambda e: e.scalar_tensor_tensor(out=xt[:], in0=xt[:], scalar=st8[0][:, 4:5], in1=ln2g[:], op0=ALU.mult, op1=ALU.mult), reads=[bx, st8[1], bl2g], writes=[bx])
                        S.op("dve", lambda e: e.tensor_tensor(out=xt[:], in0=xt[:], in1=ln2b[:], op=ALU.add), reads=[bx, bl2b], writes=[bx])
                        if last:
                            if k >= 2:
                                S.defer(lambda k=k, xt=xt, bx=bx: S.dma("sp", out_d[(k - 2) * 128:(k - 1) * 128, :], xt[:], reads=[bx], writes=[B_["out"]]))
                            if debug:
                                S.defer(lambda k=k, xt=xt, bx=bx: S.dma("sp", x2_d[k * 128:(k + 1) * 128, :], xt[:], reads=[bx], writes=[B_["x2"]]))
                        else:
                            S.defer(lambda k=k, xt=xt, bx=bx: S.dma("sp", x2_d[k * 128:(k + 1) * 128, :], xt[:], reads=[bx], writes=[B_["x2"]]))
                    S.barrier()
        S.finish()
        print("ninst", S.ninst, "nwait", S.nwait)
    return nc


def host_consts(T):
    ident = np.eye(128, dtype=np.float32)
    anti = np.ascontiguousarray(ident[::-1])
    jj = np.arange(128)[:, None]
    ii = np.arange(128)[None, :]
    maskf = (jj <= ii).astype(np.float32)
    maskb = (jj >= ii).astype(np.float32)
    half = 32
    inv = (10000.0 ** (-np.arange(0, half, 2, dtype=np.float32) / np.float32(half))).astype(np.float32)
    rows = T // GRID_W
    row = np.repeat(np.arange(rows, dtype=np.float32), GRID_W)
    col = np.tile(np.arange(GRID_W, dtype=np.float32), rows)
    ang = np.concatenate([row[:, None] * inv, col[:, None] * inv], axis=-1).astype(np.float32)
    cos = np.cos(ang).astype(np.float32).T
    sin = np.sin(ang).astype(np.float32).T
    cos2 = np.ascontiguousarray(np.concatenate([cos, cos], axis=0))
    sin2 = np.ascontiguousarray(np.concatenate([sin, sin], axis=0))
    sel = np.zeros((36, 8, 128), np.float32)
    for j in range(8):
        r = j if j < 4 else 32 + (j - 4)
        sel[r, j, :] = 1.0
    slot = (np.arange(8, dtype=np.float32)[None, :] * 128 + np.arange(128, dtype=np.float32)[:, None]).astype(np.float32)
    return {"c_slot": np.ascontiguousarray(slot), "c_ident": ident, "c_anti": anti, "c_maskf": maskf, "c_maskb": maskb, "c_cos": cos2, "c_sin": sin2, "c_sel": sel}


WNAMES = ["w_mod", "b_mod", "w_in", "b_gates", "conv_w", "conv_b", "m_norm_w", "q_norm_w", "kv_norm_w", "w_uq", "w_ukv", "w_out",
          "ln1_g", "ln1_b", "w_router", "w_gate", "w_up", "w_down", "ln2_g", "ln2_b"]


def make_in_map(b, inputs, T, consts):
    m = {}
    m["x"] = np.ascontiguousarray(inputs["x"][b, :T], dtype=np.float32)
    m["ctx"] = np.ascontiguousarray(inputs["ctx"][b], dtype=np.float32)
    cc = np.stack([np.asarray(inputs["c"][b], np.float32), np.asarray(inputs["c_ctx"], np.float32)], axis=-1)
    m["ccol"] = np.ascontiguousarray(cc.reshape(8, 128, 2).transpose(1, 0, 2))
    for n in WNAMES:
        m[n] = inputs[n]
    m.update(consts)
    return m


_NC_CACHE = {}


def kernel(**inputs):
    T = inputs["x"].shape[1]
    nb = inputs["x"].shape[0]
    inputs = {k: np.ascontiguousarray(np.asarray(v), dtype=np.float32) for k, v in inputs.items()}
    consts = host_consts(T)
    if T not in _NC_CACHE:
        _NC_CACHE[T] = build(T)
    nc = _NC_CACHE[T]
    in_maps = [make_in_map(b, inputs, T, consts) for b in range(nb)]
    res = run_bass_kernel_spmd(nc, in_maps, core_ids=list(range(nb)))
    return np.stack([np.asarray(r["out"], dtype=np.float32) for r in res.results], axis=0)
```

```python
import math
import numpy as np
from contextlib import ExitStack
import concourse.bass as bass
import concourse.mybir as mybir
from concourse.bass_utils import run_bass_kernel_spmd

F32 = mybir.dt.float32
BF16 = mybir.dt.bfloat16
F16 = mybir.dt.float16
I32 = mybir.dt.int32
AF = mybir.ActivationFunctionType
ALU = mybir.AluOpType
AX = mybir.AxisListType

D = 1024
KC = 8
DEPTH = 2
CTX = 256
GRID_W = 64
MW = 512
OFF_Q, OFF_K, OFF_V, OFF_O, OFF_G = 0, 512, 1024, 1536, 2048
OFF_CQ = OFF_G + 16
OFF_CKV = OFF_CQ + 384
OFF_KR = OFF_CKV + 256
IN_COLS = OFF_KR + 64
NE = 16
ALPHA = (2 * DEPTH) ** 0.25
A_SCALE = 192 ** -0.5
DH = 128

SAME_ENGINE_SYNC = True
EPOCH = 30000
NDSEM = 16


class Buf:
    __slots__ = ("name", "last_w", "readers", "excl")

    def __init__(self, name="", excl=False):
        self.name = name
        self.last_w = None
        self.readers = {}
        self.excl = excl


class Ring:
    def __init__(self, tiles, excl=False):
        self.tiles = tiles
        self.bufs = [Buf(excl=excl) for _ in tiles]
        self.i = 0

    def next(self):
        j = self.i % len(self.tiles)
        self.i += 1
        return self.tiles[j], self.bufs[j]


class Sched:
    def __init__(self, nc, es):
        self.nc = nc
        self.es = es
        self.engs = {"pe": nc.tensor, "dve": nc.vector, "act": nc.scalar, "pool": nc.gpsimd, "sp": nc.sync}
        self.cnt = {e: 0 for e in self.engs}
        self.epoch = {e: 0 for e in self.engs}
        self.sems = {}
        for e in self.engs:
            self.sems[(e, 0)] = es.enter_context(nc.semaphore(f"s_{e}_0"))
        self.dq = ["sp", "act", "pool"]
        self.dcount = {q: 0 for q in self.dq}
        for q in self.dq:
            for i in range(NDSEM):
                self.sems[("d", q, i)] = es.enter_context(nc.semaphore(f"d_{q}_{i}"))
        self.seen = {e: {} for e in self.engs}
        self.ninst = 0
        self.nwait = 0
        self.deferred = []

    def _wait(self, e, tok):
        if tok is None:
            return
        if tok[0] == "e":
            _, F, ep, v = tok
            if F == e and (not SAME_ENGINE_SYNC or e == "pe"):
                return
            s = self.seen[e].get(F)
            if s is not None and s >= (ep, v):
                return
            self.engs[e].wait_ge(self.sems[(F, ep)], v)
            self.seen[e][F] = (ep, v)
            self.nwait += 1
        else:
            _, q, i, v = tok
            key = ("d", q, i)
            if self.seen[e].get(key, 0) >= v:
                return
            self.engs[e].wait_ge(self.sems[key], v)
            self.seen[e][key] = v
            self.nwait += 1

    def _deps(self, e, reads, writes):
        toks = []
        for b in reads:
            if b.last_w is not None:
                toks.append(b.last_w)
            if b.excl:
                toks.extend(t for t in b.readers.values() if not (t[0] == "e" and t[1] == e))
        for b in writes:
            if b.last_w is not None:
                toks.append(b.last_w)
            toks.extend(b.readers.values())
        for t in toks:
            self._wait(e, t)

    def _commit(self, tok, reads, writes):
        key = tok[:2] if tok[0] == "e" else tok[:3]
        for b in reads:
            b.readers[key] = tok
        for b in writes:
            b.last_w = tok
            b.readers = {}

    def op(self, e, fn, reads=(), writes=()):
        self._deps(e, reads, writes)
        if self.cnt[e] >= EPOCH:
            self.epoch[e] += 1
            self.cnt[e] = 0
            self.sems[(e, self.epoch[e])] = self.es.enter_context(self.nc.semaphore(f"s_{e}_{self.epoch[e]}"))
        ins = fn(self.engs[e])
        self.cnt[e] += 1
        ins.then_inc(self.sems[(e, self.epoch[e])], 1)
        tok = ("e", e, self.epoch[e], self.cnt[e])
        self._commit(tok, reads, writes)
        self.ninst += 1
        return tok

    def dma(self, q, out, in_, reads=(), writes=(), **kw):
        self._deps(q, reads, writes)
        j = self.dcount[q]
        i, rnd = j % NDSEM, j // NDSEM
        if rnd > 0:
            self._wait(q, ("d", q, i, 16 * rnd))
        ins = self.engs[q].dma_start(out=out, in_=in_, **kw)
        ins.then_inc(self.sems[("d", q, i)], 16)
        self.dcount[q] = j + 1
        tok = ("d", q, i, 16 * (rnd + 1))
        self._commit(tok, reads, writes)
        self.ninst += 1
        return tok

    def idma(self, out, out_offset, in_, in_offset, reads=(), writes=(), **kw):
        q = "pool"
        self._deps(q, reads, writes)
        j = self.dcount[q]
        i, rnd = j % NDSEM, j // NDSEM
        if rnd > 0:
            self._wait(q, ("d", q, i, 16 * rnd))
        ins = self.engs[q].indirect_dma_start(out=out, out_offset=out_offset, in_=in_, in_offset=in_offset, **kw)
        ins.then_inc(self.sems[("d", q, i)], 16)
        self.dcount[q] = j + 1
        tok = ("d", q, i, 16 * (rnd + 1))
        self._commit(tok, reads, writes)
        self.ninst += 1
        return tok

    def defer(self, fn):
        self.deferred.append(fn)

    def flush(self):
        d, self.deferred = self.deferred, []
        for fn in d:
            fn()

    def _all_tokens(self):
        toks = []
        for e in self.engs:
            if self.cnt[e] > 0 or self.epoch[e] > 0:
                toks.append(("e", e, self.epoch[e], self.cnt[e]))
        for q in self.dq:
            j = self.dcount[q]
            for i in range(NDSEM):
                n = (j - i + NDSEM - 1) // NDSEM if j > i else 0
                if n > 0:
                    toks.append(("d", q, i, 16 * n))
        return toks

    def barrier(self):
        self.flush()
        toks = self._all_tokens()
        for e in self.engs:
            for t in toks:
                if t[0] == "e" and t[1] == e:
                    continue
                self._wait(e, t)

    def finish(self):
        self.flush()
        for t in self._all_tokens():
            if t[0] == "d":
                self._wait("sp", t)


def build(T, depth=DEPTH, debug=False, stop_after=None):
    S_ = CTX + T
    NT = S_ // 128
    NX = T // 128
    assert T % 512 == 0
    nc = bass.Bass("TRN2", target_bir_lowering=False)
    skind = "ExternalOutput" if debug else "Internal"

    def din(name, shape, dt=F32):
        return nc.dram_tensor(name, list(shape), dt, kind="ExternalInput").ap()

    def dscr(name, shape, dt=F32):
        return nc.dram_tensor(name, list(shape), dt, kind=skind).ap()

    L = depth
    x_d = din("x", [T, D])
    ctx_d = din("ctx", [CTX, D])
    ccol_d = din("ccol", [128, 8, 2])
    w_mod_d = din("w_mod", [L, D, 6 * D])
    b_mod_d = din("b_mod", [L, 6 * D])
    w_in_d = din("w_in", [L, D, IN_COLS])
    b_gates_d = din("b_gates", [L, 16])
    conv_w_d = din("conv_w", [L, 5, 1024])
    conv_b_d = din("conv_b", [L, 1024])
    m_norm_w_d = din("m_norm_w", [L, 512])
    q_norm_w_d = din("q_norm_w", [L, 384])
    kv_norm_w_d = din("kv_norm_w", [L, 256])
    w_uq_d = din("w_uq", [L, 384, 768])
    w_ukv_d = din("w_ukv", [L, 256, 1024])
    w_out_d = din("w_out", [L, 1024, 1024])
    ln1_g_d = din("ln1_g", [L, D])
    ln1_b_d = din("ln1_b", [L, D])
    w_router_d = din("w_router", [L, D, NE])
    w_gate_d = din("w_gate", [L, NE, D, D])
    w_up_d = din("w_up", [L, NE, D, D])
    w_down_d = din("w_down", [L, NE, D, D])
    ln2_g_d = din("ln2_g", [L, D])
    ln2_b_d = din("ln2_b", [L, D])
    ident_d = din("c_ident", [128, 128])
    anti_d = din("c_anti", [128, 128])
    maskf_d = din("c_maskf", [128, 128])
    maskb_d = din("c_maskb", [128, 128])
    cos_d = din("c_cos", [64, T])
    sin_d = din("c_sin", [64, T])
    sel_d = din("c_sel", [36, 8, 128])
    out_d = nc.dram_tensor("out", [T, D], F32, kind="ExternalOutput").ap()

    modvec_d = dscr("modvec", [L, 2, 6 * D])
    pqk_d = dscr("pqk", [1024, S_])
    gTf_d = dscr("gTf", [8, S_])
    gTb_d = dscr("gTb", [8, S_])
    pv_d = dscr("pv", [S_, 512])
    so_d = dscr("so", [S_, 512])
    krT_d = dscr("krT", [64, S_], BF16)
    qT_d = dscr("qT", [4, 192, S_], BF16)
    knT_d = dscr("knT", [4, 128, S_], BF16)
    va_d = dscr("va", [S_, 512], BF16)
    qmT_d = dscr("qmT", [4, 128, S_], BF16)
    kmT_d = dscr("kmT", [4, 128, S_], BF16)
    kmtm_d = dscr("kmtm", [S_, 512], BF16)
    hf_d = dscr("hf", [S_, 512])
    hb_d = dscr("hb", [S_, 512])
    moT_d = dscr("moT", [512, S_], BF16)
    aoT_d = dscr("aoT", [512, S_], BF16)
    x1_d = dscr("x1", [S_, D])
    h2T_d = dscr("h2T", [1024, S_], BF16)
    affT_d = dscr("affT", [NE, S_])
    x2_d = dscr("x2", [S_, D])
    h2tm_d = dscr("h2tm", [S_, D], BF16)
    afftm_d = dscr("afftm", [S_, NE])
    cnt_d = dscr("cnt", [NE, T], F16)
    tc_d = dscr("tcnt", [1, NE * (T // 128)])
    moe_d = dscr("moe", [S_, D])
    slot_d = din("c_slot", [128, 8])
    wgb_d = dscr("wgb", [L, NE, D, D], BF16)
    wub_d = dscr("wub", [L, NE, D, D], BF16)
    wdb_d = dscr("wdb", [L, NE, D, D], BF16)

    es = ExitStack()
    with es:
        S = Sched(nc, es)

        uid = [0]

        def sb(st, name, shape, dt):
            uid[0] += 1
            return st.enter_context(nc.sbuf_tensor(f"{name}_{uid[0]}", list(shape), dt))

        def ps(st, name, shape, dt):
            uid[0] += 1
            return st.enter_context(nc.psum_tensor(f"{name}_{uid[0]}", list(shape), dt))

        B_ = {n: Buf(n) for n in ["modvec", "pqk", "gT", "pv", "so", "krT", "qT", "knT", "va", "qmT", "kmT", "kmtm",
                                  "hf", "hb", "moT", "aoT", "x1", "h2T", "affT", "x2", "wcast", "out", "h2tm", "afftm", "cnt", "moe"]}

        ident = sb(es, "ident", [128, 128], F32)
        anti = sb(es, "anti", [128, 128], F32)
        identb = sb(es, "identb", [128, 128], BF16)
        maskf = sb(es, "maskf", [128, 128], F32)
        maskb = sb(es, "maskb", [128, 128], F32)
        ones32 = sb(es, "ones32", [128, 128], F32)
        onesb = sb(es, "onesb", [128, 128], BF16)
        zcol = sb(es, "zcol", [128, 1], F32)
        sel36 = sb(es, "sel36", [36, 8, 128], F32)
        b_const = Buf("const")
        S.dma("sp", ident[:], ident_d, writes=[b_const])
        S.dma("sp", anti[:], anti_d, writes=[b_const])
        S.dma("sp", maskf[:], maskf_d, writes=[b_const])
        S.dma("sp", maskb[:], maskb_d, writes=[b_const])
        S.dma("sp", sel36[:], sel_d, writes=[b_const])
        slotf = sb(es, "slotf", [128, 8], F32)
        S.dma("sp", slotf[:], slot_d, writes=[b_const])
        S.op("dve", lambda e: e.memset(ones32[:], 1.0), writes=[b_const])
        S.op("dve", lambda e: e.memset(onesb[:], 1.0), writes=[b_const])
        S.op("dve", lambda e: e.memset(zcol[:], 0.0), writes=[b_const])
        S.op("dve", lambda e: e.tensor_copy(out=identb[:], in_=ident[:]), reads=[b_const], writes=[b_const])


        def tile_src(l, k):
            if l == 0:
                return ctx_d[k * 128:(k + 1) * 128, :] if k < 2 else x_d[(k - 2) * 128:(k - 1) * 128, :]
            return x2_d[k * 128:(k + 1) * 128, :]

        def ku_b(k):
            return (1 - k) if k < 2 else 2 + (NT - 1 - k)

        def k_of_ku_b(ku):
            return (1 - ku) if ku < 2 else NT - 1 - (ku - 2)

        def ln_stats(st8, xt, bx, eps, junk, bjunk):
            t, bst = st8
            S.op("dve", lambda e: e.tensor_reduce(out=t[:, 0:1], in_=xt, axis=AX.X, op=ALU.add), reads=[bx], writes=[bst])
            S.op("act", lambda e: e.activation(out=junk, in_=xt, func=AF.Square, accum_out=t[:, 1:2]), reads=[bx], writes=[bjunk, bst])
            S.op("dve", lambda e: e.tensor_scalar(out=t[:, 2:3], in0=t[:, 0:1], scalar1=-1.0 / D, scalar2=None, op0=ALU.mult), reads=[bst], writes=[bst])
            S.op("dve", lambda e: e.tensor_tensor(out=t[:, 3:4], in0=t[:, 2:3], in1=t[:, 2:3], op=ALU.mult), reads=[bst], writes=[bst])
            S.op("dve", lambda e: e.scalar_tensor_tensor(out=t[:, 3:4], in0=t[:, 1:2], scalar=1.0 / D, in1=t[:, 3:4], op0=ALU.mult, op1=ALU.subtract), reads=[bst], writes=[bst])
            S.op("dve", lambda e: e.tensor_scalar(out=t[:, 3:4], in0=t[:, 3:4], scalar1=eps, scalar2=None, op0=ALU.add), reads=[bst], writes=[bst])
            S.op("act", lambda e: e.activation(out=t[:, 4:5], in_=t[:, 3:4], func=AF.Sqrt), reads=[bst], writes=[bst])
            S.op("dve", lambda e: e.reciprocal(out=t[:, 4:5], in_=t[:, 4:5]), reads=[bst], writes=[bst])

        def ln_stats_multi(tiles, eps, st, bst, junk, bjunk):
            n = len(tiles)
            for i, (xt, bx) in enumerate(tiles):
                S.op("dve", lambda e: e.tensor_reduce(out=st[:, 0, i:i + 1], in_=xt, axis=AX.X, op=ALU.add), reads=[bx], writes=[bst])
                S.op("act", lambda e: e.activation(out=junk, in_=xt, func=AF.Square, accum_out=st[:, 1, i:i + 1]), reads=[bx], writes=[bjunk, bst])
            S.op("dve", lambda e: e.tensor_scalar(out=st[:, 2, :n], in0=st[:, 0, :n], scalar1=-1.0 / D, scalar2=None, op0=ALU.mult), reads=[bst], writes=[bst])
            S.op("dve", lambda e: e.tensor_tensor(out=st[:, 3, :n], in0=st[:, 2, :n], in1=st[:, 2, :n], op=ALU.mult), reads=[bst], writes=[bst])
            S.op("dve", lambda e: e.scalar_tensor_tensor(out=st[:, 3, :n], in0=st[:, 1, :n], scalar=1.0 / D, in1=st[:, 3, :n], op0=ALU.mult, op1=ALU.subtract), reads=[bst], writes=[bst])
            S.op("dve", lambda e: e.tensor_scalar(out=st[:, 3, :n], in0=st[:, 3, :n], scalar1=eps, scalar2=None, op0=ALU.add), reads=[bst], writes=[bst])
            S.op("act", lambda e: e.activation(out=st[:, 4, :n], in_=st[:, 3, :n], func=AF.Sqrt), reads=[bst], writes=[bst])
            S.op("dve", lambda e: e.reciprocal(out=st[:, 4, :n], in_=st[:, 4, :n]), reads=[bst], writes=[bst])

        def load_col(st, name, src_1d, n):
            t = sb(st, name, [128, n], F32)
            b = Buf(name)
            S.dma("sp", t[:], src_1d.rearrange("(c p) -> p c", p=128), writes=[b], allow_slow_non_contiguous=True)
            return t, b

        def load_row_bc(st, name, src_row, n):
            t = sb(st, name, [128, n], F32)
            b = Buf(name)
            S.dma("sp", t[:], src_row.broadcast_to([128, n]), writes=[b])
            return t, b

        for l in range(L):
            last = (l == L - 1)
            ctx_out = not last

            with ExitStack() as ph:
                psA = Ring([ps(ph, f"psA{i}", [128, 512], F32) for i in range(2)], excl=True)
                cc = sb(ph, "cc", [128, 8, 2], F32)
                scs = sb(ph, "scs", [128, 8, 2], F32)
                bcc, bscs, bbm, bmr = Buf(), Buf(), Buf(), Buf()
                S.dma("sp", cc[:], ccol_d, writes=[bcc])
                S.op("act", lambda e: e.activation(out=scs[:], in_=cc[:], func=AF.Silu), reads=[bcc], writes=[bscs])
                modrow = sb(ph, "modrow", [2, 6 * D], F32)
                bmod = sb(ph, "bmod", [2, 6 * D], F32)
                S.dma("sp", bmod[:], b_mod_d[l:l + 1, :].broadcast_to([2, 6 * D]), writes=[bbm])
                wring = Ring([sb(ph, f"wm{i}", [128, 8, 512], F32) for i in range(2)])
                for g in range(12):
                    wt, wb_ = wring.next()
                    S.dma("sp", wt[:], w_mod_d[l, :, g * 512:(g + 1) * 512].rearrange("(kc p) n -> p kc n", p=128), writes=[wb_])
                    pt, pb = psA.next()
                    for kc in range(8):
                        S.op("pe", lambda e: e.matmul(pt[0:2, :], lhsT=scs[:, kc, :], rhs=wt[:, kc, :], start=(kc == 0), stop=(kc == 7)),
                             reads=[bscs, wb_], writes=[pb])
                    S.op("dve", lambda e: e.tensor_tensor(out=modrow[:, g * 512:(g + 1) * 512], in0=pt[0:2, :], in1=bmod[:, g * 512:(g + 1) * 512], op=ALU.add),
                         reads=[pb, bbm], writes=[bmr])
                S.dma("sp", modvec_d[l], modrow[:], reads=[bmr], writes=[B_["modvec"]])
                S.barrier()
            if stop_after == "A":
                break

            def modcol(st, name, r, i, plus1=False):
                t = sb(st, name, [128, 8], F32)
                b = Buf(name)
                S.dma("sp", t[:], modvec_d[l, r, i * D:(i + 1) * D].rearrange("(c p) -> p c", p=128), reads=[B_["modvec"]], writes=[b],
                      allow_slow_non_contiguous=True)
                if plus1:
                    S.op("dve", lambda e: e.tensor_scalar(out=t[:], in0=t[:], scalar1=1.0, scalar2=None, op0=ALU.add), reads=[b], writes=[b])
                return t, b

            def modrow_bc(st, name, r, i):
                t = sb(st, name, [128, D], F32)
                b = Buf(name)
                S.dma("sp", t[:], modvec_d[l, r:r + 1, i * D:(i + 1) * D].broadcast_to([128, D]), reads=[B_["modvec"]], writes=[b])
                return t, b

            blocks = [(0, 2)] + [(2 + 4 * i, 4) for i in range(NX // 4)]

            with ExitStack() as ph:
                psB = Ring([ps(ph, f"psB{i}", [128, 512], F32) for i in range(8)], excl=True)
                w_in_b = sb(ph, "w_in_b", [128, 8, IN_COLS], BF16)
                bwin = Buf("w_in_b")
                S.dma("pool", w_in_b[:], w_in_d[l].rearrange("(kc p) n -> p kc n", p=128), writes=[bwin])
                w_krJ = sb(ph, "w_krJ", [128, 8, 64], BF16)
                S.op("dve", lambda e: e.tensor_scalar(out=w_krJ[:, :, 0:32], in0=w_in_b[:, :, OFF_KR + 32:OFF_KR + 64], scalar1=-1.0, scalar2=None, op0=ALU.mult),
                     reads=[bwin], writes=[bwin])
                S.op("dve", lambda e: e.tensor_copy(out=w_krJ[:, :, 32:64], in_=w_in_b[:, :, OFF_KR:OFF_KR + 32]), reads=[bwin], writes=[bwin])
                w_uq_b = sb(ph, "w_uq_b", [128, 3, 768], BF16)
                w_uqJ = sb(ph, "w_uqJ", [128, 3, 4, 64], BF16)
                w_ukv_b = sb(ph, "w_ukv_b", [128, 2, 1024], BF16)
                w_ukv_v = sb(ph, "w_ukv_v", [128, 2, 512], BF16)
                bwuq = Buf("w_uq")
                bwukv = Buf("w_ukv")
                with ExitStack() as ph2:
                    w_uq32 = sb(ph2, "w_uq32", [128, 3, 768], F32)
                    S.dma("sp", w_uq32[:], w_uq_d[l].rearrange("(c p) n -> p c n", p=128), writes=[bwuq])
                    qnw, bqnw = load_col(ph2, "qnw", q_norm_w_d[l], 3)
                    for c in range(3):
                        S.op("dve", lambda e: e.tensor_scalar(out=w_uq_b[:, c, :], in0=w_uq32[:, c, :], scalar1=qnw[:, c:c + 1], scalar2=A_SCALE, op0=ALU.mult, op1=ALU.mult),
                             reads=[bwuq, bqnw], writes=[bwuq])
                    for h in range(4):
                        S.op("dve", lambda e: e.tensor_scalar(out=w_uqJ[:, :, h, 0:32], in0=w_uq_b[:, :, h * 192 + 160:h * 192 + 192], scalar1=-1.0, scalar2=None, op0=ALU.mult),
                             reads=[bwuq], writes=[bwuq])
                        S.op("dve", lambda e: e.tensor_copy(out=w_uqJ[:, :, h, 32:64], in_=w_uq_b[:, :, h * 192 + 128:h * 192 + 160]), reads=[bwuq], writes=[bwuq])
                    w_ukv32 = sb(ph2, "w_ukv32", [128, 2, 1024], F32)
                    S.dma("sp", w_ukv32[:], w_ukv_d[l].rearrange("(c p) n -> p c n", p=128), writes=[bwukv])
                    kvnw, bkvnw = load_col(ph2, "kvnw", kv_norm_w_d[l], 2)
                    for c in range(2):
                        S.op("dve", lambda e: e.tensor_scalar(out=w_ukv_b[:, c, :], in0=w_ukv32[:, c, :], scalar1=kvnw[:, c:c + 1], scalar2=None, op0=ALU.mult),
                             reads=[bwukv, bkvnw], writes=[bwukv])
                    for h in range(4):
                        S.op("dve", lambda e: e.tensor_copy(out=w_ukv_v[:, :, h * 128:(h + 1) * 128], in_=w_ukv_b[:, :, h * 256 + 128:h * 256 + 256]), reads=[bwukv], writes=[bwukv])
                    S.barrier()
                bg_bc, bbg = load_row_bc(ph, "bg_bc", b_gates_d[l:l + 1, :], 16)
                shx, bshx = modcol(ph, "shx", 0, 0)
                scx, bscx = modcol(ph, "scx", 0, 1, True)
                shc, bshc = modcol(ph, "shc", 1, 0)
                scc, bscc = modcol(ph, "scc", 1, 1, True)

                xring = Ring([sb(ph, f"xt{i}", [128, D], F32) for i in range(8)])
                stm_r = Ring([sb(ph, f"stm{i}", [128, 5, 4], F32) for i in range(2)])
                junk = sb(ph, "junkB", [128, D], BF16)
                bjunk = Buf()
                hT = sb(ph, "hT", [128, 8, 512], BF16)
                bhT = Buf("hT")
                qkst = sb(ph, "qkst", [128, 4, 512], F32)
                bqkst = Buf()
                vst = sb(ph, "vst", [128, 4, 512], F32)
                bvst = Buf()
                ost = sb(ph, "ost", [128, 4, 512], F32)
                bost = Buf()
                gtm = sb(ph, "gtm", [128, 16], F32)
                bgtm = Buf()
                gstf = sb(ph, "gstf", [8, 512], F32)
                gstb = sb(ph, "gstb", [8, 512], F32)
                bgstf, bgstb = Buf(), Buf()
                cqb = sb(ph, "cqb", [128, 3, 512], BF16)
                cqsq = sb(ph, "cqsq", [128, 3, 512], F32)
                ckvb = sb(ph, "ckvb", [128, 2, 512], BF16)
                ckvsq = sb(ph, "ckvsq", [128, 2, 512], F32)
                bcq, bckv = Buf(), Buf()
                rq = sb(ph, "rq", [128, 512], F32)
                rkv = sb(ph, "rkv", [128, 512], F32)
                brq, brkv = Buf(), Buf()
                rkvc = sb(ph, "rkvc", [128, 4], F32)
                brkvc = Buf()
                cos_t = sb(ph, "cos_t", [64, 512], F32)
                sin_t = sb(ph, "sin_t", [64, 512], F32)
                bcs = Buf()
                krst = sb(ph, "krst", [64, 512], BF16)
                bkrst = Buf()
                tmp64 = sb(ph, "tmp64", [64, 512], F32)
                tmp64b = sb(ph, "tmp64b", [64, 512], F32)
                btmp = Buf()
                qst = sb(ph, "qst", [128, 4, 512], BF16)
                qrst = sb(ph, "qrst", [64, 4, 512], BF16)
                bqst, bqrst = Buf(), Buf()
                knst = sb(ph, "knst", [128, 4, 512], BF16)
                bknst = Buf()
                vast = sb(ph, "vast", [128, 4, 512], BF16)
                bvast = Buf()
                print("phase B sbuf remaining", nc.sbuf_bytes_remaining)

                evac_i = [0]

                def evac(out, in_, reads, writes):
                    evac_i[0] += 1
                    if evac_i[0] % 2:
                        S.op("act", lambda e: e.copy(out=out, in_=in_), reads=reads, writes=writes)
                    else:
                        S.op("dve", lambda e: e.tensor_copy(out=out, in_=in_), reads=reads, writes=writes)

                def load_block(bi):
                    k0, ntile = blocks[bi]
                    tiles = []
                    for ti in range(ntile):
                        xt, bx = xring.next()
                        S.dma("sp", xt[:], tile_src(l, k0 + ti), reads=[B_["x2"]], writes=[bx])
                        tiles.append((xt, bx))
                    return tiles

                nxt = load_block(0)
                for bi, (k0, ntile) in enumerate(blocks):
                    cur = nxt
                    is_ctx = (k0 == 0)
                    nb = ntile * 128
                    t0 = k0 * 128
                    sh_, sc_ = (shc, scc) if is_ctx else (shx, scx)
                    bsh_, bsc_ = (bshc, bscc) if is_ctx else (bshx, bscx)
                    xns = []
                    stm, bstm = stm_r.next()
                    ln_stats_multi([(cur[ti][0][:], cur[ti][1]) for ti in range(ntile)], 1e-6, stm, bstm, junk[:], bjunk)
                    for ti in range(ntile):
                        xt, bx = cur[ti]
                        S.op("dve", lambda e: e.tensor_scalar(out=xt[:], in0=xt[:], scalar1=stm[:, 2, ti:ti + 1], scalar2=stm[:, 4, ti:ti + 1], op0=ALU.add, op1=ALU.mult),
                             reads=[bx, bstm], writes=[bx])
                        xns.append((xt, bx))
                    if bi + 1 < len(blocks):
                        nxt = load_block(bi + 1)
                    S.flush()
                    for kc in range(8):
                        pt, pb = psB.next()
                        for ti in range(ntile):
                            xn, bxn = xns[ti]
                            S.op("pe", lambda e: e.transpose(out=pt[:, ti * 128:(ti + 1) * 128], in_=xn[:, kc * 128:(kc + 1) * 128], identity=ident[:]),
                                 reads=[bxn, b_const], writes=[pb])
                        if kc % 2:
                            S.op("act", lambda e: e.activation(out=hT[:, kc, :nb], in_=pt[:, :nb], func=AF.Identity, bias=sh_[:, kc:kc + 1], scale=sc_[:, kc:kc + 1]),
                                 reads=[pb, bsh_, bsc_], writes=[bhT])
                        else:
                            S.op("dve", lambda e: e.tensor_scalar(out=hT[:, kc, :nb], in0=pt[:, :nb], scalar1=sc_[:, kc:kc + 1], scalar2=sh_[:, kc:kc + 1], op0=ALU.mult, op1=ALU.add),
                                 reads=[pb, bsh_, bsc_], writes=[bhT])
                    for oc in range(8):
                        pt, pb = psB.next()
                        for kc in range(8):
                            S.op("pe", lambda e: e.matmul(pt[:, :nb], lhsT=w_in_b[:, kc, oc * 128:(oc + 1) * 128], rhs=hT[:, kc, :nb], start=(kc == 0), stop=(kc == 7)),
                                 reads=[bwin, bhT], writes=[pb])
                        evac(qkst[:, oc % 4, :nb], pt[:, :nb], [pb], [bqkst])
                        if oc % 4 == 3:
                            o0 = oc - 3
                            S.dma("sp", pqk_d.rearrange("(oc p) s -> p oc s", p=128)[:, o0:o0 + 4, t0:t0 + nb], qkst[:, :, :nb], reads=[bqkst], writes=[B_["pqk"]])
                    for ti in range(ntile):
                        k = k0 + ti
                        pt, pb = psB.next()
                        for kc in range(8):
                            S.op("pe", lambda e: e.matmul(pt[:, :], lhsT=hT[:, kc, ti * 128:(ti + 1) * 128], rhs=w_in_b[:, kc, OFF_V:OFF_V + 512], start=(kc == 0), stop=(kc == 7)),
                                 reads=[bwin, bhT], writes=[pb])
                        evac(vst[:, ti, :], pt[:, :], [pb], [bvst])
                        pt, pb = psB.next()
                        for kc in range(8):
                            S.op("pe", lambda e: e.matmul(pt[:, :], lhsT=hT[:, kc, ti * 128:(ti + 1) * 128], rhs=w_in_b[:, kc, OFF_O:OFF_O + 512], start=(kc == 0), stop=(kc == 7)),
                                 reads=[bwin, bhT], writes=[pb])
                        S.op("act", lambda e: e.activation(out=ost[:, ti, :], in_=pt[:, :], func=AF.Sigmoid), reads=[pb], writes=[bost])
                        pt, pb = psB.next()
                        for kc in range(8):
                            S.op("pe", lambda e: e.matmul(pt[:, 0:16], lhsT=hT[:, kc, ti * 128:(ti + 1) * 128], rhs=w_in_b[:, kc, OFF_G:OFF_G + 16], start=(kc == 0), stop=(kc == 7)),
                                 reads=[bwin, bhT], writes=[pb])
                        S.op("dve", lambda e: e.tensor_tensor(out=gtm[:], in0=pt[:, 0:16], in1=bg_bc[:], op=ALU.add), reads=[pb, bbg], writes=[bgtm])
                        pt2, pb2 = psB.next()
                        S.op("pe", lambda e: e.transpose(out=pt2[0:8, 0:128], in_=gtm[:, 0:8], identity=ident[:]), reads=[bgtm, b_const], writes=[pb2])
                        S.op("pe", lambda e: e.matmul(pt2[0:8, 128:256], lhsT=gtm[:, 8:16], rhs=anti[:], start=True, stop=True), reads=[bgtm, b_const], writes=[pb2])
                        S.op("act", lambda e: e.copy(out=gstf[:, ti * 128:(ti + 1) * 128], in_=pt2[0:8, 0:128]), reads=[pb2], writes=[bgstf])
                        tj = ntile - 1 - ti
                        S.op("act", lambda e: e.copy(out=gstb[:, tj * 128:(tj + 1) * 128], in_=pt2[0:8, 128:256]), reads=[pb2], writes=[bgstb])
                    S.dma("sp", pv_d[t0:t0 + nb, :].rearrange("(t p) c -> p t c", p=128), vst[:, :ntile, :], reads=[bvst], writes=[B_["pv"]])
                    S.dma("sp", so_d[t0:t0 + nb, :].rearrange("(t p) c -> p t c", p=128), ost[:, :ntile, :], reads=[bost], writes=[B_["so"]])
                    S.dma("sp", gTf_d[:, t0:t0 + nb], gstf[:, :nb], reads=[bgstf], writes=[B_["gT"]])
                    u0 = ku_b(k0 + ntile - 1) * 128
                    S.dma("sp", gTb_d[:, u0:u0 + nb], gstb[:, :nb], reads=[bgstb], writes=[B_["gT"]])
                    for c in range(3):
                        pt, pb = psB.next()
                        for kc in range(8):
                            S.op("pe", lambda e: e.matmul(pt[:, :nb], lhsT=w_in_b[:, kc, OFF_CQ + c * 128:OFF_CQ + (c + 1) * 128], rhs=hT[:, kc, :nb], start=(kc == 0), stop=(kc == 7)),
                                 reads=[bwin, bhT], writes=[pb])
                        S.op("dve", lambda e: e.tensor_copy(out=cqb[:, c, :nb], in_=pt[:, :nb]), reads=[pb], writes=[bcq])
                        S.op("act", lambda e: e.activation(out=cqsq[:, c, :nb], in_=pt[:, :nb], func=AF.Square), reads=[pb], writes=[bcq])
                    pt, pb = psB.next()
                    for c in range(3):
                        S.op("pe", lambda e: e.matmul(pt[:, :nb], lhsT=ones32[:], rhs=cqsq[:, c, :nb], start=(c == 0), stop=(c == 2)), reads=[bcq, b_const], writes=[pb])
                    S.op("dve", lambda e: e.tensor_scalar(out=rq[:, :nb], in0=pt[:, :nb], scalar1=1.0 / 384, scalar2=1e-6, op0=ALU.mult, op1=ALU.add), reads=[pb], writes=[brq])
                    S.op("act", lambda e: e.activation(out=rq[:, :nb], in_=rq[:, :nb], func=AF.Sqrt), reads=[brq], writes=[brq])
                    S.op("dve", lambda e: e.reciprocal(out=rq[:, :nb], in_=rq[:, :nb]), reads=[brq], writes=[brq])
                    for c in range(2):
                        pt, pb = psB.next()
                        for kc in range(8):
                            S.op("pe", lambda e: e.matmul(pt[:, :nb], lhsT=w_in_b[:, kc, OFF_CKV + c * 128:OFF_CKV + (c + 1) * 128], rhs=hT[:, kc, :nb], start=(kc == 0), stop=(kc == 7)),
                                 reads=[bwin, bhT], writes=[pb])
                        S.op("dve", lambda e: e.tensor_copy(out=ckvb[:, c, :nb], in_=pt[:, :nb]), reads=[pb], writes=[bckv])
                        S.op("act", lambda e: e.activation(out=ckvsq[:, c, :nb], in_=pt[:, :nb], func=AF.Square), reads=[pb], writes=[bckv])
                    pt, pb = psB.next()
                    for c in range(2):
                        S.op("pe", lambda e: e.matmul(pt[:, :nb], lhsT=ones32[:], rhs=ckvsq[:, c, :nb], start=(c == 0), stop=(c == 1)), reads=[bckv, b_const], writes=[pb])
                    S.op("dve", lambda e: e.tensor_scalar(out=rkv[:, :nb], in0=pt[:, :nb], scalar1=1.0 / 256, scalar2=1e-6, op0=ALU.mult, op1=ALU.add), reads=[pb], writes=[brkv])
                    S.op("act", lambda e: e.activation(out=rkv[:, :nb], in_=rkv[:, :nb], func=AF.Sqrt), reads=[brkv], writes=[brkv])
                    S.op("dve", lambda e: e.reciprocal(out=rkv[:, :nb], in_=rkv[:, :nb]), reads=[brkv], writes=[brkv])
                    pt, pb = psB.next()
                    for ti in range(ntile):
                        for c in range(2):
                            S.op("pe", lambda e: e.matmul(pt[:, ti:ti + 1], lhsT=ckvsq[:, c, ti * 128:(ti + 1) * 128], rhs=ones32[:, 0:1], start=(c == 0), stop=(c == 1)),
                                 reads=[bckv, b_const], writes=[pb])
                    S.op("dve", lambda e: e.tensor_scalar(out=rkvc[:, :ntile], in0=pt[:, :ntile], scalar1=1.0 / 256, scalar2=1e-6, op0=ALU.mult, op1=ALU.add), reads=[pb], writes=[brkvc])
                    S.op("act", lambda e: e.activation(out=rkvc[:, :ntile], in_=rkvc[:, :ntile], func=AF.Sqrt), reads=[brkvc], writes=[brkvc])
                    S.op("dve", lambda e: e.reciprocal(out=rkvc[:, :ntile], in_=rkvc[:, :ntile]), reads=[brkvc], writes=[brkvc])
                    if not is_ctx:
                        xo = t0 - CTX
                        S.dma("sp", cos_t[:, :nb], cos_d[:, xo:xo + nb], writes=[bcs])
                        S.dma("sp", sin_t[:, :nb], sin_d[:, xo:xo + nb], writes=[bcs])
                    pt, pb = psB.next()
                    for kc in range(8):
                        S.op("pe", lambda e: e.matmul(pt[0:64, :nb], lhsT=w_in_b[:, kc, OFF_KR:OFF_KR + 64], rhs=hT[:, kc, :nb], start=(kc == 0), stop=(kc == 7)),
                             reads=[bwin, bhT], writes=[pb])
                    if is_ctx:
                        S.op("act", lambda e: e.copy(out=krst[:, :nb], in_=pt[0:64, :nb]), reads=[pb], writes=[bkrst])
                    else:
                        pt2, pb2 = psB.next()
                        for kc in range(8):
                            S.op("pe", lambda e: e.matmul(pt2[0:64, :nb], lhsT=w_krJ[:, kc, :], rhs=hT[:, kc, :nb], start=(kc == 0), stop=(kc == 7)),
                                 reads=[bwin, bhT], writes=[pb2])
                        S.op("dve", lambda e: e.tensor_tensor(out=tmp64[:, :nb], in0=pt[0:64, :nb], in1=cos_t[:, :nb], op=ALU.mult), reads=[pb, bcs], writes=[btmp])
                        S.op("dve", lambda e: e.tensor_tensor(out=tmp64b[:, :nb], in0=pt2[0:64, :nb], in1=sin_t[:, :nb], op=ALU.mult), reads=[pb2, bcs], writes=[btmp])
                        S.op("dve", lambda e: e.tensor_tensor(out=krst[:, :nb], in0=tmp64[:, :nb], in1=tmp64b[:, :nb], op=ALU.add), reads=[btmp], writes=[bkrst])
                    S.dma("sp", krT_d[:, t0:t0 + nb], krst[:, :nb], reads=[bkrst], writes=[B_["krT"]])
                    if (not is_ctx) or ctx_out:
                        for h in range(4):
                            pt, pb = psB.next()
                            for c in range(3):
                                S.op("pe", lambda e: e.matmul(pt[:, :nb], lhsT=w_uq_b[:, c, h * 192:h * 192 + 128], rhs=cqb[:, c, :nb], start=(c == 0), stop=(c == 2)),
                                     reads=[bwuq, bcq], writes=[pb])
                            S.op("dve", lambda e: e.tensor_tensor(out=qst[:, h, :nb], in0=pt[:, :nb], in1=rq[:, :nb], op=ALU.mult), reads=[pb, brq], writes=[bqst])
                            pt, pb = psB.next()
                            for c in range(3):
                                S.op("pe", lambda e: e.matmul(pt[0:64, :nb], lhsT=w_uq_b[:, c, h * 192 + 128:h * 192 + 192], rhs=cqb[:, c, :nb], start=(c == 0), stop=(c == 2)),
                                     reads=[bwuq, bcq], writes=[pb])
                            if is_ctx:
                                S.op("dve", lambda e: e.tensor_tensor(out=qrst[:, h, :nb], in0=pt[0:64, :nb], in1=rq[0:64, :nb], op=ALU.mult), reads=[pb, brq], writes=[bqrst])
                            else:
                                pt2, pb2 = psB.next()
                                for c in range(3):
                                    S.op("pe", lambda e: e.matmul(pt2[0:64, :nb], lhsT=w_uqJ[:, c, h, :], rhs=cqb[:, c, :nb], start=(c == 0), stop=(c == 2)),
                                         reads=[bwuq, bcq], writes=[pb2])
                                S.op("dve", lambda e: e.tensor_tensor(out=tmp64[:, :nb], in0=pt[0:64, :nb], in1=cos_t[:, :nb], op=ALU.mult), reads=[pb, bcs], writes=[btmp])
                                S.op("dve", lambda e: e.tensor_tensor(out=tmp64b[:, :nb], in0=pt2[0:64, :nb], in1=sin_t[:, :nb], op=ALU.mult), reads=[pb2, bcs], writes=[btmp])
                                S.op("dve", lambda e: e.tensor_tensor(out=tmp64[:, :nb], in0=tmp64[:, :nb], in1=tmp64b[:, :nb], op=ALU.add), reads=[btmp], writes=[btmp])
                                S.op("dve", lambda e: e.tensor_tensor(out=qrst[:, h, :nb], in0=tmp64[:, :nb], in1=rq[0:64, :nb], op=ALU.mult), reads=[btmp, brq], writes=[bqrst])
                        S.dma("sp", qT_d[:, 0:128, t0:t0 + nb].rearrange("h p s -> p h s"), qst[:, :, :nb], reads=[bqst], writes=[B_["qT"]])
                        S.dma("sp", qT_d[:, 128:192, t0:t0 + nb].rearrange("h p s -> p h s"), qrst[:, :, :nb], reads=[bqrst], writes=[B_["qT"]])
                    for h in range(4):
                        pt, pb = psB.next()
                        for c in range(2):
                            S.op("pe", lambda e: e.matmul(pt[:, :nb], lhsT=w_ukv_b[:, c, h * 256:h * 256 + 128], rhs=ckvb[:, c, :nb], start=(c == 0), stop=(c == 1)),
                                 reads=[bwukv, bckv], writes=[pb])
                        S.op("dve", lambda e: e.tensor_tensor(out=knst[:, h, :nb], in0=pt[:, :nb], in1=rkv[:, :nb], op=ALU.mult), reads=[pb, brkv], writes=[bknst])
                    S.dma("sp", knT_d[:, :, t0:t0 + nb].rearrange("h p s -> p h s"), knst[:, :, :nb], reads=[bknst], writes=[B_["knT"]])
                    for ti in range(ntile):
                        pt, pb = psB.next()
                        for c in range(2):
                            S.op("pe", lambda e: e.matmul(pt[:, :], lhsT=ckvb[:, c, ti * 128:(ti + 1) * 128], rhs=w_ukv_v[:, c, :], start=(c == 0), stop=(c == 1)),
                                 reads=[bwukv, bckv], writes=[pb])
                        S.op("act", lambda e: e.activation(out=vast[:, ti, :], in_=pt[:, :], func=AF.Copy, scale=rkvc[:, ti:ti + 1]), reads=[pb, brkvc], writes=[bvast])
                    S.dma("sp", va_d[t0:t0 + nb, :].rearrange("(t p) c -> p t c", p=128), vast[:, :ntile, :], reads=[bvast], writes=[B_["va"]])
                S.barrier()
            if stop_after == "B":
                break
            def xstream():
                with ExitStack() as ph:
                    psC = Ring([ps(ph, f"psC{i}", [128, 1024], BF16) for i in range(2)], excl=True)
                    cw = sb(ph, "cw", [128, 5, 8], F32)
                    bcw = Buf()
                    for k in range(5):
                        S.dma("sp", cw[:, k, :], conv_w_d[l, k, :].rearrange("(oc p) -> p oc", p=128), writes=[bcw], allow_slow_non_contiguous=True)
                    cb_, bcb = load_col(ph, "cb", conv_b_d[l], 8)
                    CB = 1024
                    xin_r = Ring([sb(ph, f"xin{i}", [128, CB + 4], F32) for i in range(3)])
                    acc_r = Ring([sb(ph, f"cacc{i}", [128, CB], F32) for i in range(2)])
                    ctmp = sb(ph, "ctmp", [128, CB], F32)
                    bctmp = Buf()
                    qo_r = Ring([sb(ph, f"qo{i}", [128, CB], BF16) for i in range(3)])
                    kt_r = Ring([sb(ph, f"ktst{i}", [128, 8, 128], BF16) for i in range(2)])
                    pieces = []
                    for (sa, sb_) in ((0, CTX), (CTX, S_)):
                        a = sa
                        while a < sb_:
                            b = min(a + CB, sb_)
                            pieces.append((sa, sb_, a, b))
                            a = b
                    for oc in range(8):
                        eng = "dve"
                        for (sa, sb_, a, b) in pieces:
                            n = b - a
                            yield
                            xin, bxin = xin_r.next()
                            lo = a - 2 if a > sa else a
                            hi = b + 2 if b < sb_ else b
                            if a == sa:
                                S.op("dve", lambda e: e.memset(xin[:, 0:2], 0.0), writes=[bxin])
                            if b == sb_:
                                S.op("dve", lambda e: e.memset(xin[:, 2 + n:4 + n], 0.0), writes=[bxin])
                            S.dma("sp", xin[:, 2 - (a - lo):2 + n + (hi - b)], pqk_d[oc * 128:(oc + 1) * 128, lo:hi], reads=[B_["pqk"]], writes=[bxin])
                            yield
                            acc, bacc = acc_r.next()
                            S.op(eng, lambda e: e.tensor_scalar(out=acc[:, :n], in0=xin[:, 0:n], scalar1=cw[:, 0, oc:oc + 1], scalar2=None, op0=ALU.mult), reads=[bxin, bcw], writes=[bacc])
                            for k in range(1, 5):
                                if eng == "dve":
                                    S.op(eng, lambda e: e.scalar_tensor_tensor(out=acc[:, :n], in0=xin[:, k:k + n], scalar=cw[:, k, oc:oc + 1], in1=acc[:, :n], op0=ALU.mult, op1=ALU.add),
                                         reads=[bxin, bcw, bacc], writes=[bacc])
                                else:
                                    S.op(eng, lambda e: e.tensor_scalar(out=ctmp[:, :n], in0=xin[:, k:k + n], scalar1=cw[:, k, oc:oc + 1], scalar2=None, op0=ALU.mult), reads=[bxin, bcw], writes=[bctmp])
                                    S.op(eng, lambda e: e.tensor_tensor(out=acc[:, :n], in0=acc[:, :n], in1=ctmp[:, :n], op=ALU.add), reads=[bacc, bctmp], writes=[bacc])
                            yield
                            qo, bqo = qo_r.next()
                            S.op("act", lambda e: e.activation(out=qo[:, :n], in_=acc[:, :n], func=AF.Silu, bias=cb_[:, oc:oc + 1]), reads=[bacc, bcb], writes=[bqo])
                            if oc < 4:
                                S.dma("sp", qmT_d[oc, :, a:b], qo[:, :n], reads=[bqo], writes=[B_["qmT"]])
                            else:
                                h = oc - 4
                                S.dma("sp", kmT_d[h, :, a:b], qo[:, :n], reads=[bqo], writes=[B_["kmT"]])
                                nt_ = n // 128
                                yield
                                pt, pb = psC.next()
                                for j in range(nt_):
                                    S.op("pe", lambda e: e.transpose(out=pt[:, j * 128:(j + 1) * 128], in_=qo[:, j * 128:(j + 1) * 128], identity=identb[:]), reads=[bqo, b_const], writes=[pb])
                                yield
                                kt, bkt = kt_r.next()
                                S.op("dve", lambda e: e.tensor_copy(out=kt[:, :nt_, :], in_=pt[:, :n].rearrange("p (t c) -> p t c", c=128)), reads=[pb], writes=[bkt])
                                S.dma("sp", kmtm_d[a:b, h * 128:(h + 1) * 128].rearrange("(t p) c -> p t c", p=128), kt[:, :nt_, :], reads=[bkt], writes=[B_["kmtm"]])
                    S.barrier()

                with ExitStack() as phm:
                    ea_tm = sb(phm, "ea_tm", [128, NT, 8], F32)
                    fl_tm = sb(phm, "fl_tm", [128, NT, 8], F32)
                    decay_bc = sb(phm, "decay_bc", [128, 8, NT], F32)
                    bea, bfl, bdec = Buf(), Buf(), Buf()
                    with ExitStack() as ph:
                        psD = Ring([ps(ph, f"psD{i}", [128, 512], F32) for i in range(3)], excl=True)
                        t1 = sb(ph, "t1", [36, S_], F32)
                        t2 = sb(ph, "t2", [36, S_], F32)
                        t3 = sb(ph, "t3", [36, S_], F32)
                        bt1, bt2, bt3 = Buf(), Buf(), Buf()
                        S.op("dve", lambda e: e.memset(t1[:], 30.0), writes=[bt1])
                        S.op("pool", lambda e: e.memset(t3[:], 0.0), writes=[bt3])
                        S.dma("sp", t3[0:4, :], gTf_d[0:4, :], reads=[B_["gT"]], writes=[bt3])
                        S.dma("sp", t3[32:36, :], gTb_d[0:4, :], reads=[B_["gT"]], writes=[bt3])
                        S.dma("sp", t1[0:4, :], gTf_d[4:8, :], reads=[B_["gT"]], writes=[bt1])
                        S.dma("sp", t1[32:36, :], gTb_d[4:8, :], reads=[B_["gT"]], writes=[bt1])
                        S.op("act", lambda e: e.activation(out=t1[:], in_=t1[:], func=AF.Exp, scale=-1.0), reads=[bt1], writes=[bt1])
                        S.op("dve", lambda e: e.tensor_scalar(out=t1[:], in0=t1[:], scalar1=1.0, scalar2=None, op0=ALU.add), reads=[bt1], writes=[bt1])
                        S.op("act", lambda e: e.activation(out=t1[:], in_=t1[:], func=AF.Ln), reads=[bt1], writes=[bt1])
                        S.op("dve", lambda e: e.tensor_tensor_scan(out=t2[:], data0=t1[:], data1=zcol[0:36, 0:1].broadcast_to([36, S_]), initial=0.0, op0=ALU.add, op1=ALU.add),
                             reads=[bt1, b_const], writes=[bt2])
                        yield
                        S.op("dve", lambda e: e.tensor_tensor(out=t3[:], in0=t3[:], in1=t2[:], op=ALU.add), reads=[bt3, bt2], writes=[bt3])
                        S.op("dve", lambda e: e.tensor_tensor_scan(out=t1[:], data0=t3[:], data1=t3[:], initial=0.0, op0=ALU.max, op1=ALU.max), reads=[bt3, bt1], writes=[bt1])
                        yield
                        mcur = sb(ph, "mcur", [36, NT], F32)
                        mprev = sb(ph, "mprev", [36, NT], F32)
                        mprevc = sb(ph, "mprevc", [36, NT], F32)
                        dec = sb(ph, "dec", [36, NT], F32)
                        bm = Buf()
                        t1v = t1[:].rearrange("p (t c) -> p t c", c=128)
                        t2v = t2[:].rearrange("p (t c) -> p t c", c=128)
                        t3v = t3[:].rearrange("p (t c) -> p t c", c=128)
                        S.op("dve", lambda e: e.tensor_copy(out=mcur[:].rearrange("p (t o) -> p t o", o=1), in_=t1v[:, :, 127:128]), reads=[bt1], writes=[bm])
                        S.op("dve", lambda e: e.memset(mprev[:, 0:1], 0.0), writes=[bm])
                        S.op("dve", lambda e: e.tensor_copy(out=mprev[:, 1:NT], in_=mcur[:, 0:NT - 1]), reads=[bm], writes=[bm])
                        S.op("dve", lambda e: e.tensor_tensor(out=dec[:], in0=mprev[:], in1=mcur[:], op=ALU.subtract), reads=[bm], writes=[bm])
                        S.op("act", lambda e: e.activation(out=dec[:], in_=dec[:], func=AF.Exp), reads=[bm], writes=[bm])
                        S.op("dve", lambda e: e.tensor_scalar(out=mprevc[:], in0=mprev[:], scalar1=-0.5 * math.log(DH), scalar2=None, op0=ALU.add), reads=[bm], writes=[bm])
                        mpb = mprev[:].rearrange("p (t o) -> p t o", o=1).broadcast_to([36, NT, 128])
                        mpcb = mprevc[:].rearrange("p (t o) -> p t o", o=1).broadcast_to([36, NT, 128])
                        S.op("dve", lambda e: e.tensor_tensor(out=t3v, in0=t3v, in1=mpb, op=ALU.subtract), reads=[bt3, bm], writes=[bt3])
                        S.op("act", lambda e: e.activation(out=t3[:], in_=t3[:], func=AF.Exp), reads=[bt3], writes=[bt3])
                        yield
                        S.op("dve", lambda e: e.tensor_tensor(out=t2v, in0=t2v, in1=mpcb, op=ALU.subtract), reads=[bt2, bm], writes=[bt2])
                        S.op("act", lambda e: e.activation(out=t2[:], in_=t2[:], func=AF.Exp), reads=[bt2], writes=[bt2])
                        tm_r = Ring([sb(ph, f"tmA{i}", [128, 128], F32) for i in range(2)])
                        for ku in range(NT):
                            yield
                            pt, pb = psD.next()
                            S.op("pe", lambda e: e.transpose(out=pt[:, 0:36], in_=t3[0:36, ku * 128:(ku + 1) * 128], identity=ident[0:36, 0:36]), reads=[bt3, b_const], writes=[pb])
                            S.op("pe", lambda e: e.transpose(out=pt[:, 36:72], in_=t2[0:36, ku * 128:(ku + 1) * 128], identity=ident[0:36, 0:36]), reads=[bt2, b_const], writes=[pb])
                            yield
                            tmA, btm = tm_r.next()
                            S.op("act", lambda e: e.copy(out=tmA[:, 0:72], in_=pt[:, 0:72]), reads=[pb], writes=[btm])
                            yield
                            S.op("pool", lambda e: e.tensor_copy(out=ea_tm[:, ku, 0:4], in_=tmA[:, 0:4]), reads=[btm], writes=[bea])
                            S.op("pool", lambda e: e.tensor_copy(out=fl_tm[:, ku, 0:4], in_=tmA[:, 36:40]), reads=[btm], writes=[bfl])
                            pt2, pb2 = psD.next()
                            S.op("pe", lambda e: e.matmul(pt2[:, 0:4], lhsT=anti[:], rhs=tmA[:, 32:36], start=True, stop=True), reads=[btm, b_const], writes=[pb2])
                            S.op("pe", lambda e: e.matmul(pt2[:, 4:8], lhsT=anti[:], rhs=tmA[:, 68:72], start=True, stop=True), reads=[btm, b_const], writes=[pb2])
                            kb = k_of_ku_b(ku)
                            yield
                            S.op("dve", lambda e: e.tensor_copy(out=ea_tm[:, kb, 4:8], in_=pt2[:, 0:4]), reads=[pb2], writes=[bea])
                            S.op("dve", lambda e: e.tensor_copy(out=fl_tm[:, kb, 4:8], in_=pt2[:, 4:8]), reads=[pb2], writes=[bfl])
                        for j in range(8):
                            pt, pb = psD.next()
                            S.op("pe", lambda e: e.matmul(pt[:, 0:NT], lhsT=sel36[0:36, j, :], rhs=dec[0:36, 0:NT], start=True, stop=True), reads=[bm, b_const], writes=[pb])
                            S.op("dve", lambda e: e.tensor_copy(out=decay_bc[:, j, :], in_=pt[:, 0:NT]), reads=[pb], writes=[bdec])
                        S.barrier()
                    with ExitStack() as ph:
                        psE = Ring([ps(ph, f"psE{i}", [128, 512], F32) for i in range(3)], excl=True)
                        C32 = [[sb(ph, f"C32_{d}{h}", [128, 129], F32) for h in range(4)] for d in range(2)]
                        Cb = [[sb(ph, f"Cb_{d}{h}", [128, 129], BF16) for h in range(4)] for d in range(2)]
                        bC32 = [[Buf() for h in range(4)] for d in range(2)]
                        bCb = [[Buf() for h in range(4)] for d in range(2)]
                        for d in range(2):
                            for h in range(4):
                                S.op("pool", lambda e: e.memset(C32[d][h][:], 0.0), writes=[bC32[d][h]])
                                S.op("pool", lambda e: e.memset(Cb[d][h][:], 0.0), writes=[bCb[d][h]])
                        qt_r = Ring([sb(ph, f"eQT{i}", [128, 4, 128], BF16) for i in range(4)])
                        kt_r = Ring([sb(ph, f"eKT{i}", [128, 4, 128], BF16) for i in range(4)])
                        ktm_r = Ring([sb(ph, f"eKtm{i}", [128, 512], BF16) for i in range(4)])
                        v_r = Ring([sb(ph, f"eV{i}", [128, 512], F32) for i in range(4)])
                        vp_r = Ring([sb(ph, f"eVp{i}", [128, 129], BF16) for i in range(8)])
                        sm_r = Ring([sb(ph, f"eSm{i}", [128, 128], BF16) for i in range(8)])
                        dm_r = Ring([sb(ph, f"edm{i}", [128, 2], F32) for i in range(8)])
                        hst_r = Ring([sb(ph, f"ehst{i}", [128, 512], F32) for i in range(4)])
                        masks = [maskf, maskb]

                        def e_load(u, d):
                            k = u if d == 0 else k_of_ku_b(u)
                            QT, bQT = qt_r.next()
                            KT, bKT = kt_r.next()
                            Ktm, bKtm = ktm_r.next()
                            V, bV = v_r.next()
                            S.dma("sp", QT[:], qmT_d[:, :, k * 128:(k + 1) * 128].rearrange("h p s -> p h s"), reads=[B_["qmT"]], writes=[bQT])
                            S.dma("sp", KT[:], kmT_d[:, :, k * 128:(k + 1) * 128].rearrange("h p s -> p h s"), reads=[B_["kmT"]], writes=[bKT])
                            S.dma("sp", Ktm[:], kmtm_d[k * 128:(k + 1) * 128, :], reads=[B_["kmtm"]], writes=[bKtm])
                            S.dma("sp", V[:], pv_d[k * 128:(k + 1) * 128, :], reads=[B_["pv"]], writes=[bV])
                            return (k, QT, bQT, KT, bKT, Ktm, bKtm, V, bV)

                        steps = [(u, d) for u in range(NT) for d in range(2)]
                        (bS_, bbS), (bOa, bbOa), (bOb, bbOb) = [(psE.tiles[i], psE.bufs[i]) for i in range(3)]
                        bCa, bbCa = bS_, bbS

                        def pO_of(h):
                            return (bOa, bbOa, h * 129) if h < 3 else (bOb, bbOb, 0)

                        def pC_of(h):
                            return (bCa, bbCa, h * 129) if h < 3 else (bOb, bbOb, 129)

                        nxt = e_load(*steps[0])
                        for si, (u, d) in enumerate(steps):
                            (k, QT, bQT, KT, bKT, Ktm, bKtm, V, bV) = nxt
                            if si + 1 < len(steps):
                                nxt = e_load(*steps[si + 1])
                            S.flush()
                            hst, bhst = hst_r.next()
                            Vps, Sms, dms = [], [], []
                            for h in range(4):
                                j = d * 4 + h
                                Vp, bVp = vp_r.next()
                                S.op("act", lambda e: e.activation(out=Vp[:, 0:128], in_=V[:, h * 128:(h + 1) * 128], func=AF.Copy, scale=ea_tm[:, k, j:j + 1]), reads=[bV, bea], writes=[bVp])
                                S.op("pool", lambda e: e.tensor_copy(out=Vp[:, 128:129], in_=ea_tm[:, k, j:j + 1]), reads=[bea], writes=[bVp])
                                S.op("pe", lambda e: e.matmul(bS_[:, h * 128:(h + 1) * 128], lhsT=KT[:, h, :], rhs=QT[:, h, :], start=True, stop=True), reads=[bKT, bQT], writes=[bbS])
                                Vps.append((Vp, bVp))
                            yield
                            for h in range(4):
                                Sm, bSm = sm_r.next()
                                S.op("dve", lambda e: e.tensor_tensor(out=Sm[:], in0=bS_[:, h * 128:(h + 1) * 128], in1=masks[d][:], op=ALU.mult), reads=[bbS, b_const], writes=[bSm])
                                Sms.append((Sm, bSm))
                            yield
                            for h in range(4):
                                pO, bpO, o = pO_of(h)
                                S.op("pe", lambda e: e.matmul(pO[:, o:o + 129], lhsT=Sms[h][0][:], rhs=Vps[h][0][:, 0:129], start=True, stop=False), reads=[Sms[h][1], Vps[h][1]], writes=[bpO])
                                S.op("pe", lambda e: e.matmul(pO[:, o:o + 129], lhsT=QT[:, h, :], rhs=Cb[d][h][:, 0:129], start=False, stop=True), reads=[bQT, bCb[d][h]], writes=[bpO])
                            yield
                            for h in range(4):
                                pO, bpO, o = pO_of(h)
                                dm, bdm = dm_r.next()
                                S.op("act", lambda e: e.copy(out=dm[:, 0:1], in_=pO[:, o + 128:o + 129]), reads=[bpO], writes=[bdm])
                                dms.append((dm, bdm))
                            yield
                            for h in range(4):
                                j = d * 4 + h
                                dm, bdm = dms[h]
                                S.op("dve", lambda e: e.scalar_tensor_tensor(out=dm[:, 1:2], in0=dm[:, 0:1], scalar=-1.0, in1=dm[:, 0:1], op0=ALU.mult, op1=ALU.max), reads=[bdm], writes=[bdm])
                                S.op("dve", lambda e: e.tensor_tensor(out=dm[:, 1:2], in0=dm[:, 1:2], in1=fl_tm[:, k, j:j + 1], op=ALU.max), reads=[bdm, bfl], writes=[bdm])
                                S.op("dve", lambda e: e.reciprocal(out=dm[:, 1:2], in_=dm[:, 1:2]), reads=[bdm], writes=[bdm])
                            yield
                            for h in range(4):
                                pO, bpO, o = pO_of(h)
                                S.op("act", lambda e: e.activation(out=hst[:, h * 128:(h + 1) * 128], in_=pO[:, o:o + 128], func=AF.Copy, scale=dms[h][0][:, 1:2]), reads=[bpO, dms[h][1]], writes=[bhst])
                            for h in range(4):
                                pC, bpC, o = pC_of(h)
                                S.op("pe", lambda e: e.matmul(pC[:, o:o + 129], lhsT=Ktm[:, h * 128:(h + 1) * 128], rhs=Vps[h][0][:, 0:129], start=True, stop=True), reads=[bKtm, Vps[h][1]], writes=[bpC])
                            yield
                            for h in range(4):
                                pC, bpC, o = pC_of(h)
                                S.op("dve", lambda e: e.tensor_tensor(out=C32[d][h][:], in0=pC[:, o:o + 129], in1=C32[d][h][:], op=ALU.add), reads=[bpC, bC32[d][h]], writes=[bC32[d][h]])
                            yield
                            for h in range(4):
                                j = d * 4 + h
                                S.op("act", lambda e: e.activation(out=Cb[d][h][:], in_=C32[d][h][:], func=AF.Copy, scale=decay_bc[:, j, u:u + 1]), reads=[bC32[d][h], bdec], writes=[bCb[d][h]])
                            yield
                            for h in range(4):
                                j = d * 4 + h
                                S.op("dve", lambda e: e.tensor_scalar(out=C32[d][h][:], in0=C32[d][h][:], scalar1=decay_bc[:, j, u:u + 1], scalar2=None, op0=ALU.mult),
                                     reads=[bC32[d][h], bdec], writes=[bC32[d][h]])
                            dst = hf_d if d == 0 else hb_d
                            bn = "hf" if d == 0 else "hb"
                            S.defer(lambda dst=dst, k=k, hst=hst, bhst=bhst, bn=bn: S.dma("sp", dst[k * 128:(k + 1) * 128, :], hst[:], reads=[bhst], writes=[B_[bn]]))
                            yield
                        S.barrier()
                with ExitStack() as ph:
                    psF = Ring([ps(ph, f"psF{i}", [128, 1024], BF16) for i in range(2)], excl=True)
                    nw_bc, bnw = load_row_bc(ph, "nw_bc", m_norm_w_d[l:l + 1, :], 512)
                    hf_r = Ring([sb(ph, f"fhf{i}", [128, 512], F32) for i in range(3)])
                    hb_r = Ring([sb(ph, f"fhb{i}", [128, 512], F32) for i in range(3)])
                    so_r = Ring([sb(ph, f"fso{i}", [128, 512], F32) for i in range(3)])
                    sq_r = Ring([sb(ph, f"fsq{i}", [128, 512], F32) for i in range(2)])
                    st_r = Ring([sb(ph, f"fst{i}", [128, 16], F32) for i in range(3)])
                    mo_r = Ring([sb(ph, f"fmo{i}", [128, 512], BF16) for i in range(2)])
                    mt_r = Ring([sb(ph, f"fmt{i}", [128, 4, 128], BF16) for i in range(2)])
                    ftiles = list(range(0 if ctx_out else 2, NT))

                    def f_load(k):
                        a, ba = hf_r.next()
                        b, bb = hb_r.next()
                        c, bc = so_r.next()
                        S.dma("sp", a[:], hf_d[k * 128:(k + 1) * 128, :], reads=[B_["hf"]], writes=[ba])
                        S.dma("sp", b[:], hb_d[k * 128:(k + 1) * 128, :], reads=[B_["hb"]], writes=[bb])
                        S.dma("sp", c[:], so_d[k * 128:(k + 1) * 128, :], reads=[B_["so"]], writes=[bc])
                        return (a, ba, b, bb, c, bc)

                    nxt = f_load(ftiles[0])
                    for fi, k in enumerate(ftiles):
                        (a, ba, b, bb, c, bc) = nxt
                        if fi + 1 < len(ftiles):
                            nxt = f_load(ftiles[fi + 1])
                        S.flush()
                        yield
                        st, bst = st_r.next()
                        sq, bsq = sq_r.next()
                        a3 = a[:].rearrange("p (h c) -> p h c", c=128)
                        sq3 = sq[:].rearrange("p (h c) -> p h c", c=128)
                        S.op("dve", lambda e: e.tensor_tensor(out=a[:], in0=a[:], in1=b[:], op=ALU.add), reads=[ba, bb], writes=[ba])
                        S.op("dve", lambda e: e.tensor_reduce(out=st[:, 0:4], in_=a3, axis=AX.X, op=ALU.add), reads=[ba], writes=[bst])
                        S.op("dve", lambda e: e.tensor_scalar(out=st[:, 0:4], in0=st[:, 0:4], scalar1=-1.0 / 128, scalar2=None, op0=ALU.mult), reads=[bst], writes=[bst])
                        S.op("dve", lambda e: e.tensor_tensor(out=a3, in0=a3, in1=st[:, 0:4].rearrange("p (h o) -> p h o", o=1).broadcast_to([128, 4, 128]), op=ALU.add), reads=[ba, bst], writes=[ba])
                        yield
                        S.op("act", lambda e: e.activation(out=sq[:], in_=a[:], func=AF.Square), reads=[ba], writes=[bsq])
                        yield
                        S.op("dve", lambda e: e.tensor_reduce(out=st[:, 4:8], in_=sq3, axis=AX.X, op=ALU.add), reads=[bsq], writes=[bst])
                        S.op("dve", lambda e: e.tensor_scalar(out=st[:, 4:8], in0=st[:, 4:8], scalar1=1.0 / 128, scalar2=1e-6, op0=ALU.mult, op1=ALU.add), reads=[bst], writes=[bst])
                        yield
                        S.op("act", lambda e: e.activation(out=st[:, 4:8], in_=st[:, 4:8], func=AF.Sqrt), reads=[bst], writes=[bst])
                        yield
                        S.op("dve", lambda e: e.reciprocal(out=st[:, 4:8], in_=st[:, 4:8]), reads=[bst], writes=[bst])
                        S.op("dve", lambda e: e.tensor_tensor(out=a3, in0=a3, in1=st[:, 4:8].rearrange("p (h o) -> p h o", o=1).broadcast_to([128, 4, 128]), op=ALU.mult), reads=[ba, bst], writes=[ba])
                        S.op("pool", lambda e: e.tensor_tensor(out=c[:], in0=c[:], in1=nw_bc[:], op=ALU.mult), reads=[bc, bnw], writes=[bc])
                        mo, bmo = mo_r.next()
                        S.op("dve", lambda e: e.tensor_tensor(out=mo[:], in0=a[:], in1=c[:], op=ALU.mult), reads=[ba, bc], writes=[bmo])
                        yield
                        pt, pb = psF.next()
                        for cch in range(4):
                            S.op("pe", lambda e: e.transpose(out=pt[:, cch * 128:(cch + 1) * 128], in_=mo[:, cch * 128:(cch + 1) * 128], identity=identb[:]), reads=[bmo, b_const], writes=[pb])
                        yield
                        mt, bmt = mt_r.next()
                        S.op("act", lambda e: e.copy(out=mt[:], in_=pt[:, 0:512].rearrange("p (c t) -> p c t", t=128)), reads=[pb], writes=[bmt])
                        S.defer(lambda k=k, mt=mt, bmt=bmt: S.dma("sp", moT_d.rearrange("(c p) s -> p c s", p=128)[:, :, k * 128:(k + 1) * 128], mt[:], reads=[bmt], writes=[B_["moT"]]))
                    S.barrier()
            with ExitStack() as ph:
                psS = Ring([ps(ph, f"psS{i}", [128, 512], F32) for i in range(3)], excl=True)
                psO = Ring([ps(ph, f"psO{i}", [128, 512], F32) for i in range(1)], excl=True)
                psL = Ring([ps(ph, f"psL{i}", [128, 512], F32) for i in range(1)], excl=True)
                for e_ in range(NE):
                    for (src, dst) in ((w_gate_d, wgb_d), (w_up_d, wub_d), (w_down_d, wdb_d)):
                        S.dma("pool", dst[l, e_], src[l, e_], writes=[B_["wcast"]])
                krT = sb(ph, "g_krT", [64, S_], BF16)
                bkr = Buf()
                S.dma("sp", krT[:], krT_d, reads=[B_["krT"]], writes=[bkr])
                kn_r = Ring([sb(ph, f"g_kn{i}", [128, S_], BF16) for i in range(1)])
                va_r = Ring([sb(ph, f"g_va{i}", [128, NT, 128], BF16) for i in range(1)])
                qn_r = Ring([sb(ph, f"g_qn{i}", [128, 512], BF16) for i in range(3)])
                qr_r = Ring([sb(ph, f"g_qr{i}", [64, 512], BF16) for i in range(3)])
                pT_r = Ring([sb(ph, f"g_pT{i}", [128, 512], BF16) for i in range(4)])
                rec_r = Ring([sb(ph, f"g_rec{i}", [128, 512], F32) for i in range(2)])
                ao_r = Ring([sb(ph, f"g_ao{i}", [128, 512], BF16) for i in range(2)])
                qblocks = [(CTX + 512 * i, 512, 0, NT) for i in range(T // 512)]
                if ctx_out:
                    qblocks = [(0, CTX, 0, 2)] + qblocks
                work = [(h, qb) for h in range(4) for qb in qblocks]

                def g_loadh(h):
                    kn, bkn = kn_r.next()
                    va, bva = va_r.next()
                    S.dma("sp", kn[:], knT_d[h], reads=[B_["knT"]], writes=[bkn])
                    S.dma("sp", va[:], va_d[:, h * 128:(h + 1) * 128].rearrange("(t p) c -> p t c", p=128), reads=[B_["va"]], writes=[bva])
                    return (kn, bkn, va, bva)

                def g_loadq(h, qb):
                    t0, nb, k0, k1 = qb
                    qn, bqn = qn_r.next()
                    qr, bqr = qr_r.next()
                    S.dma("sp", qn[:, :nb], qT_d[h, 0:128, t0:t0 + nb], reads=[B_["qT"]], writes=[bqn])
                    S.dma("sp", qr[:, :nb], qT_d[h, 128:192, t0:t0 + nb], reads=[B_["qT"]], writes=[bqr])
                    return (qn, bqn, qr, bqr)

                hl = {}
                nq = g_loadq(*work[0])
                gen = xstream()
                gstep = [0]
                for wi, (h, qb) in enumerate(work):
                    t0, nb, k0, k1 = qb
                    (qn, bqn, qr, bqr) = nq
                    if wi + 1 < len(work):
                        nq = g_loadq(*work[wi + 1])
                    if h not in hl:
                        hl[h] = g_loadh(h)
                    S.flush()
                    (kn, bkn, va, bva) = hl[h]
                    pO, bpO = psO.next()
                    pL, bpL = psL.next()
                    def g_qk(kt):
                        pS, bpS = psS.next()
                        S.op("pe", lambda e: e.matmul(pS[:, :nb], lhsT=kn[:, kt * 128:(kt + 1) * 128], rhs=qn[:, :nb], start=True, stop=False), reads=[bkn, bqn], writes=[bpS])
                        S.op("pe", lambda e: e.matmul(pS[:, :nb], lhsT=krT[:, kt * 128:(kt + 1) * 128], rhs=qr[:, :nb], start=False, stop=True), reads=[bkr, bqr], writes=[bpS])
                        return (pS, bpS)

                    qkq = [g_qk(k0)]
                    if k0 + 1 < k1:
                        qkq.append(g_qk(k0 + 1))
                    for kt in range(k0, k1):
                        pS, bpS = qkq.pop(0)
                        if kt + 2 < k1:
                            qkq.append(g_qk(kt + 2))
                        gstep[0] += 1
                        if gstep[0] % 2 == 0:
                            next(gen, None)
                        pT, bpT = pT_r.next()
                        S.op("act", lambda e: e.activation(out=pT[:, :nb], in_=pS[:, :nb], func=AF.Exp), reads=[bpS], writes=[bpT])
                        S.op("pe", lambda e: e.matmul(pO[:, :nb], lhsT=va[:, kt, :], rhs=pT[:, :nb], start=(kt == k0), stop=(kt == k1 - 1)), reads=[bva, bpT], writes=[bpO])
                        S.op("pe", lambda e: e.matmul(pL[:, :nb], lhsT=onesb[:], rhs=pT[:, :nb], start=(kt == k0), stop=(kt == k1 - 1)), reads=[b_const, bpT], writes=[bpL])
                    rec, brec = rec_r.next()
                    ao, bao = ao_r.next()
                    S.op("dve", lambda e: e.reciprocal(out=rec[:, :nb], in_=pL[:, :nb]), reads=[bpL], writes=[brec])
                    S.op("dve", lambda e: e.tensor_tensor(out=ao[:, :nb], in0=pO[:, :nb], in1=rec[:, :nb], op=ALU.mult), reads=[bpO, brec], writes=[bao])
                    S.defer(lambda h=h, t0=t0, nb=nb, ao=ao, bao=bao: S.dma("sp", aoT_d[h * 128:(h + 1) * 128, t0:t0 + nb], ao[:, :nb], reads=[bao], writes=[B_["aoT"]]))
                for _ in gen:
                    pass
                S.barrier()
            if stop_after == "G":
                break

            hblocks = [b for b in blocks if ctx_out or b[0] != 0]
            with ExitStack() as phI:
                wgt_tm = sb(phI, "wgt_tm", [128, NT, NE], F32)
                bwgt = Buf()
                with ExitStack() as ph:
                    psH = Ring([ps(ph, f"psH{i}", [128, 512], F32) for i in range(8)], excl=True)
                    w_out_b = sb(ph, "w_out_b", [128, 8, 1024], BF16)
                    bwo = Buf()
                    S.dma("pool", w_out_b[:], w_out_d[l].rearrange("(c p) n -> p c n", p=128), writes=[bwo])
                    wr32 = sb(ph, "wr32", [128, 8, NE], F32)
                    bwr = Buf()
                    S.dma("sp", wr32[:], w_router_d[l].rearrange("(c p) n -> p c n", p=128), writes=[bwr])
                    g2x, bg2x = modrow_bc(ph, "g2x", 0, 2)
                    ln1g, bl1g = load_row_bc(ph, "ln1g", ln1_g_d[l:l + 1, :], D)
                    ln1b, bl1b = load_row_bc(ph, "ln1b", ln1_b_d[l:l + 1, :], D)
                    sh3x, bsh3x = modcol(ph, "sh3x", 0, 3)
                    sc4x, bsc4x = modcol(ph, "sc4x", 0, 4, True)
                    sh3r, bsh3r = modrow_bc(ph, "sh3r", 0, 3)
                    sc4r, bsc4r = modrow_bc(ph, "sc4r", 0, 4)
                    S.op("dve", lambda e: e.tensor_scalar(out=sc4r[:], in0=sc4r[:], scalar1=1.0, scalar2=None, op0=ALU.add), reads=[bsc4r], writes=[bsc4r])
                    h2tm_r = Ring([sb(ph, f"h_h2tm{i}", [128, D], BF16) for i in range(2)])
                    if ctx_out:
                        g2c, bg2c = modrow_bc(ph, "g2c", 1, 2)
                        sh3c, bsh3c = modcol(ph, "sh3c", 1, 3)
                        sc4c, bsc4c = modcol(ph, "sc4c", 1, 4, True)
                    mo_r = Ring([sb(ph, f"h_mo{i}", [128, 4, 512], BF16) for i in range(2)])
                    ao_r = Ring([sb(ph, f"h_ao{i}", [128, 4, 512], BF16) for i in range(2)])
                    x_r = Ring([sb(ph, f"h_x{i}", [128, D], F32) for i in range(8)])
                    tmp_r = Ring([sb(ph, f"h_tmp{i}", [128, D], F32) for i in range(2)])
                    xn2_r = Ring([sb(ph, f"h_xn2{i}", [128, D], F32) for i in range(4)])
                    stm_r = Ring([sb(ph, f"h_stm{i}", [128, 5, 4], F32) for i in range(3)])
                    junk = sb(ph, "h_junk", [128, D], BF16)
                    bjunk = Buf()
                    h2T32 = sb(ph, "h2T32", [128, 8, 512], F32)
                    h2Tb = sb(ph, "h2Tb", [128, 8, 512], BF16)
                    bh32, bhb = Buf(), Buf()
                    sm_r = Ring([sb(ph, f"h_sm{i}", [128, 24], F32) for i in range(3)])
                    affTs = sb(ph, "affTs", [NE, 512], F32)
                    baffTs = Buf()
                    print("phase H sbuf remaining", nc.sbuf_bytes_remaining)

                    def h_load(bi):
                        k0, ntile = hblocks[bi]
                        nb = ntile * 128
                        t0 = k0 * 128
                        mo, bmo = mo_r.next()
                        ao, bao = ao_r.next()
                        S.dma("sp", mo[:, :, :nb], moT_d.rearrange("(c p) s -> p c s", p=128)[:, :, t0:t0 + nb], reads=[B_["moT"]], writes=[bmo])
                        S.dma("sp", ao[:, :, :nb], aoT_d.rearrange("(c p) s -> p c s", p=128)[:, :, t0:t0 + nb], reads=[B_["aoT"]], writes=[bao])
                        xs = []
                        for ti in range(ntile):
                            xt, bx = x_r.next()
                            S.dma("sp", xt[:], tile_src(l, k0 + ti), reads=[B_["x2"]], writes=[bx])
                            xs.append((xt, bx))
                        return (mo, bmo, ao, bao, xs)

                    nxt = h_load(0)
                    for bi, (k0, ntile) in enumerate(hblocks):
                        (mo, bmo, ao, bao, xs) = nxt
                        S.flush()
                        if bi + 1 < len(hblocks):
                            nxt = h_load(bi + 1)
                        is_ctx = (k0 == 0)
                        nb = ntile * 128
                        t0 = k0 * 128
                        g2, bg2 = (g2c, bg2c) if is_ctx else (g2x, bg2x)
                        sh3, bsh3 = (sh3c, bsh3c) if is_ctx else (sh3x, bsh3x)
                        sc4, bsc4 = (sc4c, bsc4c) if is_ctx else (sc4x, bsc4x)
                        xn2s = []
                        for ti in range(ntile):
                            xt, bx = xs[ti]
                            tmp, btmp = tmp_r.next()
                            for half in range(2):
                                pt, pb = psH.next()
                                for c in range(8):
                                    src, bsrc = (mo, bmo) if c < 4 else (ao, bao)
                                    S.op("pe", lambda e: e.matmul(pt[:, :], lhsT=src[:, c % 4, ti * 128:(ti + 1) * 128], rhs=w_out_b[:, c, half * 512:(half + 1) * 512], start=(c == 0), stop=(c == 7)),
                                         reads=[bsrc, bwo], writes=[pb])
                                S.op("dve", lambda e: e.tensor_tensor(out=tmp[:, half * 512:(half + 1) * 512], in0=pt[:, :], in1=g2[:, half * 512:(half + 1) * 512], op=ALU.mult),
                                     reads=[pb, bg2], writes=[btmp])
                            S.op("dve", lambda e: e.scalar_tensor_tensor(out=xt[:], in0=xt[:], scalar=ALPHA, in1=tmp[:], op0=ALU.mult, op1=ALU.add), reads=[bx, btmp], writes=[bx])
                        stm, bstm = stm_r.next()
                        ln_stats_multi([(xs[ti][0][:], xs[ti][1]) for ti in range(ntile)], 1e-5, stm, bstm, junk[:], bjunk)
                        for ti in range(ntile):
                            k = k0 + ti
                            xt, bx = xs[ti]
                            S.op("act", lambda e: e.activation(out=xt[:], in_=xt[:], func=AF.Identity, bias=stm[:, 2, ti:ti + 1], scale=1.0), reads=[bx, bstm], writes=[bx])
                            S.op("dve", lambda e: e.scalar_tensor_tensor(out=xt[:], in0=xt[:], scalar=stm[:, 4, ti:ti + 1], in1=ln1g[:], op0=ALU.mult, op1=ALU.mult), reads=[bx, bstm, bl1g], writes=[bx])
                            S.op("dve", lambda e: e.tensor_tensor(out=xt[:], in0=xt[:], in1=ln1b[:], op=ALU.add), reads=[bx, bl1b], writes=[bx])
                            S.defer(lambda k=k, xt=xt, bx=bx: S.dma("sp", x1_d[k * 128:(k + 1) * 128, :], xt[:], reads=[bx], writes=[B_["x1"]]))
                        stm2, bstm2 = stm_r.next()
                        ln_stats_multi([(xs[ti][0][:], xs[ti][1]) for ti in range(ntile)], 1e-6, stm2, bstm2, junk[:], bjunk)
                        for ti in range(ntile):
                            k = k0 + ti
                            xt, bx = xs[ti]
                            xn2, bxn2 = xn2_r.next()
                            S.op("dve", lambda e: e.tensor_scalar(out=xn2[:], in0=xt[:], scalar1=stm2[:, 2, ti:ti + 1], scalar2=stm2[:, 4, ti:ti + 1], op0=ALU.add, op1=ALU.mult), reads=[bx, bstm2], writes=[bxn2])
                            xn2s.append((xn2, bxn2))
                            if not is_ctx:
                                tmp2, btmp2 = tmp_r.next()
                                h2tm, bh2tm = h2tm_r.next()
                                S.op("dve", lambda e: e.tensor_tensor(out=tmp2[:], in0=xn2[:], in1=sc4r[:], op=ALU.mult), reads=[bxn2, bsc4r], writes=[btmp2])
                                S.op("dve", lambda e: e.tensor_tensor(out=h2tm[:], in0=tmp2[:], in1=sh3r[:], op=ALU.add), reads=[btmp2, bsh3r], writes=[bh2tm])
                                S.dma("sp", h2tm_d[k * 128:(k + 1) * 128, :], h2tm[:], reads=[bh2tm], writes=[B_["h2tm"]])
                        for kc in range(8):
                            pt, pb = psH.next()
                            for ti in range(ntile):
                                xn2, bxn2 = xn2s[ti]
                                S.op("pe", lambda e: e.transpose(out=pt[:, ti * 128:(ti + 1) * 128], in_=xn2[:, kc * 128:(kc + 1) * 128], identity=ident[:]), reads=[bxn2, b_const], writes=[pb])
                            S.op("act", lambda e: e.activation(out=h2T32[:, kc, :nb], in_=pt[:, :nb], func=AF.Identity, bias=sh3[:, kc:kc + 1], scale=sc4[:, kc:kc + 1]),
                                 reads=[pb, bsh3, bsc4], writes=[bh32])
                        S.op("pool", lambda e: e.tensor_copy(out=h2Tb[:, :, :nb], in_=h2T32[:, :, :nb]), reads=[bh32], writes=[bhb])
                        S.dma("sp", h2T_d.rearrange("(c p) s -> p c s", p=128)[:, :, t0:t0 + nb], h2Tb[:, :, :nb], reads=[bhb], writes=[B_["h2T"]])
                        for ti in range(ntile):
                            pt, pb = psH.next()
                            for kc in range(8):
                                S.op("pe", lambda e: e.matmul(pt[:, 0:NE], lhsT=h2T32[:, kc, ti * 128:(ti + 1) * 128], rhs=wr32[:, kc, :], start=(kc == 0), stop=(kc == 7)), reads=[bh32, bwr], writes=[pb])
                            sm, bsm = sm_r.next()
                            S.op("dve", lambda e: e.tensor_reduce(out=sm[:, 16:17], in_=pt[:, 0:NE], axis=AX.X, op=ALU.max), reads=[pb], writes=[bsm])
                            S.op("dve", lambda e: e.tensor_scalar(out=sm[:, 16:17], in0=sm[:, 16:17], scalar1=-1.0, scalar2=None, op0=ALU.mult), reads=[bsm], writes=[bsm])
                            S.op("act", lambda e: e.activation(out=sm[:, 0:NE], in_=pt[:, 0:NE], func=AF.Exp, bias=sm[:, 16:17], accum_out=sm[:, 17:18]), reads=[pb, bsm], writes=[bsm])
                            S.op("dve", lambda e: e.reciprocal(out=sm[:, 17:18], in_=sm[:, 17:18]), reads=[bsm], writes=[bsm])
                            S.op("dve", lambda e: e.tensor_scalar(out=sm[:, 0:NE], in0=sm[:, 0:NE], scalar1=sm[:, 17:18], scalar2=None, op0=ALU.mult), reads=[bsm], writes=[bsm])
                            S.dma("sp", afftm_d[(k0 + ti) * 128:(k0 + ti + 1) * 128, :], sm[:, 0:NE], reads=[bsm], writes=[B_["afftm"]])
                            pt2, pb2 = psH.next()
                            S.op("pe", lambda e: e.transpose(out=pt2[0:NE, 0:128], in_=sm[:, 0:NE], identity=ident[:]), reads=[bsm, b_const], writes=[pb2])
                            S.op("act", lambda e: e.copy(out=affTs[:, ti * 128:(ti + 1) * 128], in_=pt2[0:NE, 0:128]), reads=[pb2], writes=[baffTs])
                        S.dma("sp", affT_d[:, t0:t0 + nb], affTs[:, :nb], reads=[baffTs], writes=[B_["affT"]])
                    S.barrier()
                if stop_after == "H":
                    break
                with ExitStack() as ph:
                    psI = Ring([ps(ph, f"psI{i}", [128, 512], F32) for i in range(2)], excl=True)
                    affT = sb(ph, "i_affT", [NE, S_], F32)
                    wT = sb(ph, "i_wT", [NE, S_], F32)
                    junkI = sb(ph, "i_junk", [NE, T], F16)
                    baff, bwT, bjk = Buf(), Buf(), Buf()
                    S.dma("sp", affT[:], affT_d, reads=[B_["affT"]], writes=[baff])
                    if not ctx_out:
                        S.op("dve", lambda e: e.memset(wT[:, 0:CTX], 0.0), writes=[bwT])
                    sets = [(CTX, S_, T // 8)]
                    if ctx_out:
                        sets = [(0, CTX, CTX // 8)] + sets
                    for (a, b, kcap) in sets:
                        n = b - a
                        cs = sb(ph, f"i_cs{a}", [NE, 8], F32)
                        bcs = Buf()
                        S.op("dve", lambda e: e.memset(cs[:, 0:1], 0.0), writes=[bcs])
                        S.op("dve", lambda e: e.memset(cs[:, 1:2], 1.0), writes=[bcs])
                        for it in range(32):
                            S.op("dve", lambda e: e.tensor_scalar(out=cs[:, 2:3], in0=cs[:, 0:1], scalar1=0.5, scalar2=None, op0=ALU.mult), reads=[bcs], writes=[bcs])
                            S.op("dve", lambda e: e.scalar_tensor_tensor(out=cs[:, 2:3], in0=cs[:, 1:2], scalar=0.5, in1=cs[:, 2:3], op0=ALU.mult, op1=ALU.add), reads=[bcs], writes=[bcs])
                            S.op("dve", lambda e: e.memset(cs[:, 3:4], 0.0), writes=[bcs])
                            S.op("dve", lambda e: e.tensor_scalar(out=junkI[:, :n], in0=affT[:, a:b], scalar1=cs[:, 2:3], scalar2=0.0, op0=ALU.is_ge, op1=ALU.add, accum_out=cs[:, 3:4]),
                                 reads=[baff, bcs], writes=[bjk, bcs])
                            S.op("dve", lambda e: e.tensor_scalar(out=cs[:, 4:5], in0=cs[:, 3:4], scalar1=float(kcap) - 0.5, scalar2=None, op0=ALU.is_ge), reads=[bcs], writes=[bcs])
                            S.op("dve", lambda e: e.tensor_tensor(out=cs[:, 5:6], in0=cs[:, 2:3], in1=cs[:, 0:1], op=ALU.subtract), reads=[bcs], writes=[bcs])
                            S.op("dve", lambda e: e.scalar_tensor_tensor(out=cs[:, 0:1], in0=cs[:, 5:6], scalar=cs[:, 4:5], in1=cs[:, 0:1], op0=ALU.mult, op1=ALU.add), reads=[bcs], writes=[bcs])
                            S.op("dve", lambda e: e.tensor_tensor(out=cs[:, 5:6], in0=cs[:, 1:2], in1=cs[:, 2:3], op=ALU.subtract), reads=[bcs], writes=[bcs])
                            S.op("dve", lambda e: e.scalar_tensor_tensor(out=cs[:, 1:2], in0=cs[:, 5:6], scalar=cs[:, 4:5], in1=cs[:, 2:3], op0=ALU.mult, op1=ALU.add), reads=[bcs], writes=[bcs])
                        if a == 0:
                            S.op("dve", lambda e: e.scalar_tensor_tensor(out=wT[:, a:b], in0=affT[:, a:b], scalar=cs[:, 0:1], in1=affT[:, a:b], op0=ALU.is_ge, op1=ALU.mult), reads=[baff, bcs], writes=[bwT])
                        else:
                            S.op("dve", lambda e: e.tensor_scalar(out=wT[:, a:b], in0=affT[:, a:b], scalar1=cs[:, 0:1], scalar2=None, op0=ALU.is_ge), reads=[baff, bcs], writes=[bwT])
                            S.op("dve", lambda e: e.tensor_tensor_scan(out=affT[:, a:b], data0=wT[:, a:b], data1=zcol[0:NE, 0:1].broadcast_to([NE, n]), initial=0.0, op0=ALU.add, op1=ALU.add),
                                 reads=[bwT, b_const, baff], writes=[baff])
                            S.op("dve", lambda e: e.tensor_copy(out=junkI[:, :n], in_=affT[:, a:b]), reads=[baff, bjk], writes=[bjk])
                            S.dma("sp", cnt_d, junkI[:, :n], reads=[bjk], writes=[B_["cnt"]])
                            tcs = sb(ph, "i_tcs", [NE, NX], F32)
                            btcs = Buf()
                            S.op("dve", lambda e: e.tensor_copy(out=tcs[:].rearrange("p (t o) -> p t o", o=1), in_=affT[:, a:b].rearrange("p (t c) -> p t c", c=128)[:, :, 127:128]),
                                 reads=[baff], writes=[btcs])
                            S.dma("sp", tc_d.rearrange("o (e k) -> (o e) k", e=NE), tcs[:], reads=[btcs], writes=[B_["cnt"]])
                    for k in range(2 if ctx_out else 0):
                        pt, pb = psI.next()
                        S.op("pe", lambda e: e.transpose(out=pt[:, 0:NE], in_=wT[0:NE, k * 128:(k + 1) * 128], identity=ident[0:NE, 0:NE]), reads=[bwT, b_const], writes=[pb])
                        S.op("act", lambda e: e.copy(out=wgt_tm[:, k, :], in_=pt[:, 0:NE]), reads=[pb], writes=[bwgt])
                    S.barrier()
                if stop_after == "I":
                    break
                with ExitStack() as ph:
                    psG = Ring([ps(ph, f"psG{i}", [128, 512], F32) for i in range(3)], excl=True)
                    psY = Ring([ps(ph, f"psY{i}", [128, 512], F32) for i in range(3)], excl=True)
                    psT = Ring([ps(ph, f"psT{i}", [128, 1024], BF16) for i in range(2)], excl=True)
                    w_r = Ring([sb(ph, f"m_w{i}", [128, 8, 1024], BF16) for i in range(4)])
                    act_r = Ring([sb(ph, f"m_act{i}", [128, 8, 512], BF16) for i in range(2)])
                    sg_r = Ring([sb(ph, f"m_sg{i}", [128, 512], F32) for i in range(3)])
                    zt = sb(ph, "m_zt", [128, D], F32)
                    bzt = Buf()
                    S.op("dve", lambda e: e.memset(zt[:], 0.0), writes=[bzt])
                    S.dma("sp", moe_d[CTX:S_, :].rearrange("(t p) d -> p t d", p=128), zt[:].rearrange("p (o d) -> p o d", o=1).broadcast_to([128, NX, D]), reads=[bzt], writes=[B_["moe"]])
                    wsrc = (wgb_d, wub_d, wdb_d)
                    nblk = 2 if ctx_out else 1
                    seq = [(bi, e_, m) for bi in range(nblk) for e_ in range(NE) for m in range(3)]
                    loaded = {}
                    nload = [0]

                    def w_load(i):
                        while nload[0] <= i and nload[0] < len(seq):
                            j = nload[0]
                            bi_, e2, m = seq[j]
                            wt, wb_ = w_r.next()
                            S.dma("sp", wt[:], wsrc[m][l, e2].rearrange("(c p) n -> p c n", p=128), reads=[B_["wcast"]], writes=[wb_])
                            loaded[j] = (wt, wb_)
                            nload[0] += 1

                    def ffn_gate_up(wg, bwg, wu, bwu, h2, bh2, hs, n):
                        act, bact = act_r.next()
                        for fc in range(8):
                            pg, bpg = psG.next()
                            for kc in range(8):
                                S.op("pe", lambda e: e.matmul(pg[:, :n], lhsT=wg[:, kc, fc * 128:(fc + 1) * 128], rhs=h2[:, kc, hs:hs + n], start=(kc == 0), stop=(kc == 7)), reads=[bwg, bh2], writes=[bpg])
                            pu, bpu = psG.next()
                            for kc in range(8):
                                S.op("pe", lambda e: e.matmul(pu[:, :n], lhsT=wu[:, kc, fc * 128:(fc + 1) * 128], rhs=h2[:, kc, hs:hs + n], start=(kc == 0), stop=(kc == 7)), reads=[bwu, bh2], writes=[bpu])
                            sg, bsg = sg_r.next()
                            S.op("act", lambda e: e.activation(out=sg[:, :n], in_=pg[:, :n], func=AF.Silu), reads=[bpg], writes=[bsg])
                            S.op("dve", lambda e: e.tensor_tensor(out=act[:, fc, :n], in0=pu[:, :n], in1=sg[:, :n], op=ALU.mult), reads=[bpu, bsg], writes=[bact])
                        return act, bact

                    w_load(2)
                    si = 0
                    if ctx_out:
                        with ExitStack() as phc:
                            h2c = sb(phc, "m_h2c", [128, 8, CTX], BF16)
                            accc = sb(phc, "m_accc", [128, 2, D], F32)
                            bh2c, baccc = Buf(), Buf()
                            S.dma("sp", h2c[:], h2T_d.rearrange("(c p) s -> p c s", p=128)[:, :, 0:CTX], reads=[B_["h2T"]], writes=[bh2c])
                            for e_ in range(NE):
                                w_load(si + 2)
                                wg, bwg = loaded.pop(si)
                                wu, bwu = loaded.pop(si + 1)
                                wd, bwd = loaded.pop(si + 2)
                                w_load(si + 3)
                                act, bact = ffn_gate_up(wg, bwg, wu, bwu, h2c, bh2c, 0, CTX)
                                w_load(si + 5)
                                for tl in range(2):
                                    for oh in range(2):
                                        py, bpy = psY.next()
                                        for fc in range(8):
                                            S.op("pe", lambda e: e.matmul(py[:, :], lhsT=act[:, fc, tl * 128:(tl + 1) * 128], rhs=wd[:, fc, oh * 512:(oh + 1) * 512], start=(fc == 0), stop=(fc == 7)), reads=[bact, bwd], writes=[bpy])
                                        if e_ == 0:
                                            S.op("dve", lambda e: e.tensor_scalar(out=accc[:, tl, oh * 512:(oh + 1) * 512], in0=py[:, :], scalar1=wgt_tm[:, tl, e_:e_ + 1], scalar2=None, op0=ALU.mult),
                                                 reads=[bpy, bwgt], writes=[baccc])
                                        else:
                                            S.op("dve", lambda e: e.scalar_tensor_tensor(out=accc[:, tl, oh * 512:(oh + 1) * 512], in0=py[:, :], scalar=wgt_tm[:, tl, e_:e_ + 1], in1=accc[:, tl, oh * 512:(oh + 1) * 512], op0=ALU.mult, op1=ALU.add),
                                                 reads=[bpy, bwgt, baccc], writes=[baccc])
                                si += 3
                            S.dma("sp", moe_d[0:CTX, :].rearrange("(t p) d -> p t d", p=128), accc[:], reads=[baccc], writes=[B_["moe"]])
                            S.barrier()
                    tcb = sb(ph, "m_tcb", [128, NE, NX], F32)
                    btcb = Buf()
                    S.dma("sp", tcb[:].rearrange("p e k -> p (e k)"), tc_d.broadcast_to([128, NE * NX]), reads=[B_["cnt"]], writes=[btcb])
                    junkM = sb(ph, "m_junk", [128, 128], F32)
                    bjm = Buf()
                    kf = sb(ph, "m_kf", [128, NE, 8], F32)
                    rowf = sb(ph, "m_rowf", [128, NE, 8], F32)
                    rowi = sb(ph, "m_rowi", [128, NE, 8], I32)
                    posf = sb(ph, "m_posf", [128, NE, 8], F32)
                    ct_r = Ring([sb(ph, f"m_ct{i}", [128, 128], F16) for i in range(8)])
                    cnt2d = cnt_d.rearrange("e (k c) -> (e k) c", c=128)
                    idxf = sb(ph, "m_idxf", [128, NE, 8], F32)
                    idxi = sb(ph, "m_idxi", [128, NE, 8], I32)
                    bidx = [Buf() for _ in range(NE)]
                    gts = sb(ph, "m_gts", [128, NE, 8, NE], F32)
                    bgts = [Buf() for _ in range(NE)]
                    xg_r = Ring([sb(ph, f"m_xg{i}", [128, 4, D], BF16) for i in range(3)])
                    XgT_r = Ring([sb(ph, f"m_XgT{i}", [128, 8, 512], BF16) for i in range(2)])
                    ys_r = Ring([sb(ph, f"m_ys{i}", [128, D], F32) for i in range(4)])
                    print("phase M sbuf remaining", nc.sbuf_bytes_remaining)
                    NSL = T // 8 // 128
                    HPE = max(1, NSL // 4)
                    SPH = NSL // HPE
                    units = [(e_, hh) for e_ in range(NE) for hh in range(HPE)]
                    def stageA(u):
                        e_, hh = units[u]
                        if hh == 0:
                            be = bidx[e_]
                            for j in range(NSL):
                                S.op("dve", lambda e: e.memset(kf[:, e_, j:j + 1], 0.0), writes=[be])
                                S.op("dve", lambda e: e.tensor_scalar(out=junkM[:, 0:NX], in0=tcb[:, e_, :], scalar1=slotf[:, j:j + 1], scalar2=0.0, op0=ALU.is_le, op1=ALU.add, accum_out=kf[:, e_, j:j + 1]),
                                     reads=[btcb, b_const, bjm], writes=[bjm, be])
                            S.op("dve", lambda e: e.tensor_scalar(out=rowf[:, e_, 0:NSL], in0=kf[:, e_, 0:NSL], scalar1=float(e_ * NX), scalar2=None, op0=ALU.add), reads=[be], writes=[be])
                            S.op("dve", lambda e: e.tensor_copy(out=rowi[:, e_, 0:NSL], in_=rowf[:, e_, 0:NSL]), reads=[be], writes=[be])
                            for j in range(NSL):
                                ct, bct = ct_r.next()
                                S.idma(out=ct[:], out_offset=None, in_=cnt2d, in_offset=bass.IndirectOffsetOnAxis(ap=rowi[:, e_, j:j + 1], axis=0), reads=[be, B_["cnt"]], writes=[bct])
                                S.op("dve", lambda e: e.memset(posf[:, e_, j:j + 1], 0.0), writes=[be])
                                S.op("dve", lambda e: e.tensor_scalar(out=junkM[:, 0:128], in0=ct[:], scalar1=slotf[:, j:j + 1], scalar2=0.0, op0=ALU.is_le, op1=ALU.add, accum_out=posf[:, e_, j:j + 1]),
                                     reads=[bct, b_const, bjm], writes=[bjm, be])
                            S.op("dve", lambda e: e.scalar_tensor_tensor(out=idxf[:, e_, 0:NSL], in0=kf[:, e_, 0:NSL], scalar=128.0, in1=posf[:, e_, 0:NSL], op0=ALU.mult, op1=ALU.add), reads=[be], writes=[be])
                            S.op("dve", lambda e: e.tensor_scalar(out=idxf[:, e_, 0:NSL], in0=idxf[:, e_, 0:NSL], scalar1=float(CTX), scalar2=None, op0=ALU.add), reads=[be], writes=[be])
                            S.op("dve", lambda e: e.tensor_copy(out=idxi[:, e_, 0:NSL], in_=idxf[:, e_, 0:NSL]), reads=[be], writes=[be])
                        xg, bxg = xg_r.next()
                        for jj in range(SPH):
                            j = hh * SPH + jj
                            S.idma(out=xg[:, jj, :], out_offset=None, in_=h2tm_d[:, :], in_offset=bass.IndirectOffsetOnAxis(ap=idxi[:, e_, j:j + 1], axis=0),
                                   reads=[bidx[e_], B_["h2tm"]], writes=[bxg])
                            S.idma(out=gts[:, e_, j, :], out_offset=None, in_=afftm_d[:, :], in_offset=bass.IndirectOffsetOnAxis(ap=idxi[:, e_, j:j + 1], axis=0),
                                   reads=[bidx[e_], B_["afftm"]], writes=[bgts[e_]])
                        return (xg, bxg)

                    def stageB(u, xgt):
                        e_, hh = units[u]
                        xg, bxg = xgt
                        XgT, bXgT = XgT_r.next()
                        for kc in range(8):
                            pt, pb = psT.next()
                            for jj in range(SPH):
                                S.op("pe", lambda e: e.transpose(out=pt[:, jj * 128:(jj + 1) * 128], in_=xg[:, jj, kc * 128:(kc + 1) * 128], identity=identb[:]), reads=[bxg, b_const], writes=[pb])
                            if kc % 2:
                                S.op("act", lambda e: e.copy(out=XgT[:, kc, 0:SPH * 128], in_=pt[:, 0:SPH * 128]), reads=[pb], writes=[bXgT])
                            else:
                                S.op("dve", lambda e: e.tensor_copy(out=XgT[:, kc, 0:SPH * 128], in_=pt[:, 0:SPH * 128]), reads=[pb], writes=[bXgT])
                        return (XgT, bXgT)

                    wcur = {}
                    sc_prev = [B_["moe"].last_w]
                    sc_cur = []

                    def stageC(u, XgTt):
                        nonlocal si
                        e_, hh = units[u]
                        XgT, bXgT = XgTt
                        if hh == 0:
                            w_load(si + 2)
                            wcur["g"] = loaded.pop(si)
                            wcur["u"] = loaded.pop(si + 1)
                            wcur["d"] = loaded.pop(si + 2)
                            w_load(si + 3)
                        wg, bwg = wcur["g"]
                        wu, bwu = wcur["u"]
                        wd, bwd = wcur["d"]
                        n = SPH * 128
                        act, bact = ffn_gate_up(wg, bwg, wu, bwu, XgT, bXgT, 0, n)
                        if hh == HPE - 1:
                            w_load(si + 5)
                        for jj in range(SPH):
                            j = hh * SPH + jj
                            ys, bys = ys_r.next()
                            for oh in range(2):
                                py, bpy = psY.next()
                                for fc in range(8):
                                    S.op("pe", lambda e: e.matmul(py[:, :], lhsT=act[:, fc, jj * 128:(jj + 1) * 128], rhs=wd[:, fc, oh * 512:(oh + 1) * 512], start=(fc == 0), stop=(fc == 7)), reads=[bact, bwd], writes=[bpy])
                                if oh == 0:
                                    S.op("act", lambda e: e.activation(out=ys[:, 0:512], in_=py[:, :], func=AF.Copy, scale=gts[:, e_, j, e_:e_ + 1]), reads=[bpy, bgts[e_]], writes=[bys])
                                else:
                                    S.op("dve", lambda e: e.tensor_scalar(out=ys[:, 512:1024], in0=py[:, :], scalar1=gts[:, e_, j, e_:e_ + 1], scalar2=None, op0=ALU.mult), reads=[bpy, bgts[e_]], writes=[bys])
                            for t in sc_prev:
                                S._wait("pool", t)
                            tk = S.idma(out=moe_d[:, :], out_offset=bass.IndirectOffsetOnAxis(ap=idxi[:, e_, j:j + 1], axis=0), in_=ys[:], in_offset=None,
                                        reads=[bys, bidx[e_]], writes=[], compute_op=ALU.add)
                            sc_cur.append(tk)
                        if hh == HPE - 1:
                            sc_prev[:] = sc_cur
                            sc_cur[:] = []
                        if hh == HPE - 1:
                            si += 3

                    xgq = {}
                    XgTq = {}
                    nU = len(units)
                    xgq[0] = stageA(0)
                    if nU > 1:
                        xgq[1] = stageA(1)
                    XgTq[0] = stageB(0, xgq.pop(0))
                    for u in range(nU):
                        if u + 2 < nU:
                            xgq[u + 2] = stageA(u + 2)
                        if u + 1 < nU:
                            XgTq[u + 1] = stageB(u + 1, xgq.pop(u + 1))
                        stageC(u, XgTq.pop(u))
                    S.barrier()
                with ExitStack() as ph:
                    g5x, bg5x = modrow_bc(ph, "g5x", 0, 5)
                    if ctx_out:
                        g5c, bg5c = modrow_bc(ph, "g5c", 1, 5)
                    ln2g, bl2g = load_row_bc(ph, "ln2g", ln2_g_d[l:l + 1, :], D)
                    ln2b, bl2b = load_row_bc(ph, "ln2b", ln2_b_d[l:l + 1, :], D)
                    x1_r = Ring([sb(ph, f"p_x1{i}", [128, D], F32) for i in range(4)])
                    mo_r = Ring([sb(ph, f"p_mo{i}", [128, D], F32) for i in range(4)])
                    st_r = Ring([sb(ph, f"p_st{i}", [128, 8], F32) for i in range(3)])
                    junk = sb(ph, "p_junk", [128, D], BF16)
                    bjunk = Buf()
                    ptiles = list(range(0 if ctx_out else 2, NT))

                    def p_load(k):
                        xt, bx = x1_r.next()
                        mt, bm = mo_r.next()
                        S.dma("sp", xt[:], x1_d[k * 128:(k + 1) * 128, :], reads=[B_["x1"]], writes=[bx])
                        S.dma("sp", mt[:], moe_d[k * 128:(k + 1) * 128, :], reads=[B_["moe"]], writes=[bm])
                        return (xt, bx, mt, bm)

                    nxt = p_load(ptiles[0])
                    for pi, k in enumerate(ptiles):
                        (xt, bx, mt, bm) = nxt
                        S.flush()
                        if pi + 1 < len(ptiles):
                            nxt = p_load(ptiles[pi + 1])
                        g5, bg5 = (g5c, bg5c) if k < 2 else (g5x, bg5x)
                        S.op("dve", lambda e: e.tensor_tensor(out=mt[:], in0=mt[:], in1=g5[:], op=ALU.mult), reads=[bm, bg5], writes=[bm])
                        S.op("dve", lambda e: e.scalar_tensor_tensor(out=xt[:], in0=xt[:], scalar=ALPHA, in1=mt[:], op0=ALU.mult, op1=ALU.add), reads=[bx, bm], writes=[bx])
                        st8 = st_r.next()
                        ln_stats(st8, xt[:], bx, 1e-5, junk[:], bjunk)
                        S.op("act", lambda e: e.activation(out=xt[:], in_=xt[:], func=AF.Identity, bias=st8[0][:, 2:3], scale=1.0), reads=[bx, st8[1]], writes=[bx])
                        S.op("dve", lambda e: e.scalar_tensor_tensor(out=xt[:], in0=xt[:], scalar=st8[0][:, 4:5], in1=ln2g[:], op0=ALU.mult, op1=ALU.mult), reads=[bx, st8[1], bl2g], writes=[bx])
                        S.op("dve", lambda e: e.tensor_tensor(out=xt[:], in0=xt[:], in1=ln2b[:], op=ALU.add), reads=[bx, bl2b], writes=[bx])
                        if last:
                            if k >= 2:
                                S.defer(lambda k=k, xt=xt, bx=bx: S.dma("sp", out_d[(k - 2) * 128:(k - 1) * 128, :], xt[:], reads=[bx], writes=[B_["out"]]))
                            if debug:
                                S.defer(lambda k=k, xt=xt, bx=bx: S.dma("sp", x2_d[k * 128:(k + 1) * 128, :], xt[:], reads=[bx], writes=[B_["x2"]]))
                        else:
                            S.defer(lambda k=k, xt=xt, bx=bx: S.dma("sp", x2_d[k * 128:(k + 1) * 128, :], xt[:], reads=[bx], writes=[B_["x2"]]))
                    S.barrier()
        S.finish()
        print("ninst", S.ninst, "nwait", S.nwait)
    return nc


def host_consts(T):
    ident = np.eye(128, dtype=np.float32)
    anti = np.ascontiguousarray(ident[::-1])
    jj = np.arange(128)[:, None]
    ii = np.arange(128)[None, :]
    maskf = (jj <= ii).astype(np.float32)
    maskb = (jj >= ii).astype(np.float32)
    half = 32
    inv = (10000.0 ** (-np.arange(0, half, 2, dtype=np.float32) / np.float32(half))).astype(np.float32)
    rows = T // GRID_W
    row = np.repeat(np.arange(rows, dtype=np.float32), GRID_W)
    col = np.tile(np.arange(GRID_W, dtype=np.float32), rows)
    ang = np.concatenate([row[:, None] * inv, col[:, None] * inv], axis=-1).astype(np.float32)
    cos = np.cos(ang).astype(np.float32).T
    sin = np.sin(ang).astype(np.float32).T
    cos2 = np.ascontiguousarray(np.concatenate([cos, cos], axis=0))
    sin2 = np.ascontiguousarray(np.concatenate([sin, sin], axis=0))
    sel = np.zeros((36, 8, 128), np.float32)
    for j in range(8):
        r = j if j < 4 else 32 + (j - 4)
        sel[r, j, :] = 1.0
    slot = (np.arange(8, dtype=np.float32)[None, :] * 128 + np.arange(128, dtype=np.float32)[:, None]).astype(np.float32)
    return {"c_slot": np.ascontiguousarray(slot), "c_ident": ident, "c_anti": anti, "c_maskf": maskf, "c_maskb": maskb, "c_cos": cos2, "c_sin": sin2, "c_sel": sel}


WNAMES = ["w_mod", "b_mod", "w_in", "b_gates", "conv_w", "conv_b", "m_norm_w", "q_norm_w", "kv_norm_w", "w_uq", "w_ukv", "w_out",
          "ln1_g", "ln1_b", "w_router", "w_gate", "w_up", "w_down", "ln2_g", "ln2_b"]


def make_in_map(b, inputs, T, consts):
    m = {}
    m["x"] = np.ascontiguousarray(inputs["x"][b, :T], dtype=np.float32)
    m["ctx"] = np.ascontiguousarray(inputs["ctx"][b], dtype=np.float32)
    cc = np.stack([np.asarray(inputs["c"][b], np.float32), np.asarray(inputs["c_ctx"], np.float32)], axis=-1)
    m["ccol"] = np.ascontiguousarray(cc.reshape(8, 128, 2).transpose(1, 0, 2))
    for n in WNAMES:
        m[n] = inputs[n]
    m.update(consts)
    return m


_NC_CACHE = {}


def kernel(**inputs):
    T = inputs["x"].shape[1]
    nb = inputs["x"].shape[0]
    inputs = {k: np.ascontiguousarray(np.asarray(v), dtype=np.float32) for k, v in inputs.items()}
    consts = host_consts(T)
    if T not in _NC_CACHE:
        _NC_CACHE[T] = build(T)
    nc = _NC_CACHE[T]
    in_maps = [make_in_map(b, inputs, T, consts) for b in range(nb)]
    res = run_bass_kernel_spmd(nc, in_maps, core_ids=list(range(nb)))
    return np.stack([np.asarray(r["out"], dtype=np.float32) for r in res.results], axis=0)
```
